# Optimizing a Trainium2 kernel written in Bass

```python
import math, functools
import jax, jax.numpy as jnp
from jax import lax
import numpy as np

D_MODEL = 1024
BATCH = 16
SEQ = 2048
DEPTH = 2

CTX_LEN = 256
GRID_W = 64
D_FF = 2816
N_MOD = 9
EPS = 1e-6
ROPE_BASE = 10000.0
Q_BLOCK = 128

MLA_HEADS = 8
MLA_Q_RANK = 256
MLA_KV_RANK = 128
MLA_NOPE = 64
MLA_ROPE = 32
MLA_V = 64

SWA_HEADS = 8
SWA_KV_HEADS = 2
SWA_HEAD_DIM = 64
SWA_WINDOW = 128
SWA_BLOCK = 128

DIFF_HEADS = 8
DIFF_HEAD_DIM = 64

L0_IN_SIZES = (MLA_Q_RANK, MLA_KV_RANK, MLA_ROPE, SWA_HEADS * SWA_HEAD_DIM, SWA_KV_HEADS * SWA_HEAD_DIM, SWA_KV_HEADS * SWA_HEAD_DIM)
L0_IN_WIDTH = sum(L0_IN_SIZES)
L0_OUT_WIDTH = MLA_HEADS * MLA_V + SWA_HEADS * SWA_HEAD_DIM
L1_IN_WIDTH = 3 * DIFF_HEADS * 2 * DIFF_HEAD_DIM
L1_OUT_WIDTH = DIFF_HEADS * 2 * DIFF_HEAD_DIM

kernel_name = "hybrid_diffusion_mla_swa_diffattn_block"


def rms_norm(x, g):
    xf = x.astype(jnp.float32)
    y = xf * lax.rsqrt(jnp.mean(xf * xf, axis=-1, keepdims=True) + EPS)
    return (y * g.astype(jnp.float32)).astype(x.dtype)


def adaln(x, g, mod, k):
    return rms_norm(x, g) * (1 + mod[..., 3 * k + 1, :]) + mod[..., 3 * k, :]


def swiglu(h, wg, wu, wd):
    return (jax.nn.silu(h @ wg) * (h @ wu)) @ wd


def lambda_init_fn(layer_idx):
    return 0.8 - 0.6 * math.exp(-0.3 * layer_idx)


def axial_rope(n_rows, rot_dim):
    t_row = jnp.repeat(jnp.arange(n_rows, dtype=jnp.float32), GRID_W)
    t_col = jnp.broadcast_to(jnp.arange(GRID_W, dtype=jnp.float32)[None, :], (n_rows, GRID_W)).reshape(-1)
    n_f = rot_dim // 4
    inv = ROPE_BASE ** (-jnp.arange(n_f, dtype=jnp.float32) / n_f)
    ang = jnp.concatenate([t_row[:, None] * inv, t_col[:, None] * inv], axis=-1)
    return jnp.cos(ang), jnp.sin(ang)


def apply_rope(x, cos, sin):
    half = x.shape[-1] // 2
    shape = (cos.shape[0],) + (1,) * (x.ndim - 3) + (half,)
    cos, sin = cos.reshape(shape), sin.reshape(shape)
    x1, x2 = x[..., :half], x[..., half:]
    return jnp.concatenate([x1 * cos - x2 * sin, x1 * sin + x2 * cos], axis=-1).astype(x.dtype)


def sweep_query_blocks(fn, q):
    B, S = q.shape[:2]
    nb = S // Q_BLOCK
    qb = jnp.moveaxis(q.reshape((B, nb, Q_BLOCK) + q.shape[2:]), 1, 0)
    out = lax.map(fn, qb)
    return jnp.moveaxis(out, 0, 1).reshape((B, S) + out.shape[3:])


def softmax_attention_block(qb, k, v, scale):
    s = jnp.einsum('bqhd,bkhd->bhqk', qb, k).astype(jnp.float32) * scale
    p = jax.nn.softmax(s, axis=-1).astype(v.dtype)
    return jnp.einsum('bhqk,bkhd->bqhd', p, v)


def mla_queries(p_qa, qa_g, wqb, q_g, rope):
    B, N = p_qa.shape[:2]
    q = (rms_norm(p_qa, qa_g) @ wqb).reshape(B, N, MLA_HEADS, MLA_NOPE + MLA_ROPE)
    q = rms_norm(q, q_g)
    if rope is not None:
        q = jnp.concatenate([q[..., :MLA_NOPE], apply_rope(q[..., MLA_NOPE:], *rope)], axis=-1)
    return q


def mla_keys_values(p_kva, p_kr, kva_g, wkvb, k_g, rope):
    B, N = p_kva.shape[:2]
    kv = (rms_norm(p_kva, kva_g) @ wkvb).reshape(B, N, MLA_HEADS, MLA_NOPE + MLA_V)
    k_rope = jnp.broadcast_to(p_kr[:, :, None, :], (B, N, MLA_HEADS, MLA_ROPE))
    k = rms_norm(jnp.concatenate([kv[..., :MLA_NOPE], k_rope], axis=-1), k_g)
    if rope is not None:
        k = jnp.concatenate([k[..., :MLA_NOPE], apply_rope(k[..., MLA_NOPE:], *rope)], axis=-1)
    return k, kv[..., MLA_NOPE:]


def window_attention_latent(q, k, v, kc, vc, sink, scale):
    B, S, Hq, d = q.shape
    Hkv = k.shape[2]
    G = Hq // Hkv
    W = SWA_BLOCK
    nb = S // W
    L = kc.shape[1]

    def band(t):
        tb = t.reshape(B, nb, W, Hkv, d)
        tp = jnp.pad(tb, ((0, 0), (1, 1), (0, 0), (0, 0), (0, 0)))
        return jnp.moveaxis(jnp.concatenate([tp[:, :-2], tp[:, 1:-1], tp[:, 2:]], axis=2), 1, 0)

    qb = jnp.moveaxis(q.reshape(B, nb, W, Hkv, G, d), 1, 0)
    q_pos = jnp.arange(nb)[:, None] * W + jnp.arange(W)[None, :]
    k_pos = (jnp.arange(nb)[:, None] - 1) * W + jnp.arange(3 * W)[None, :]
    valid = ((jnp.abs(k_pos[:, None, :] - q_pos[:, :, None]) <= SWA_WINDOW)
             & (k_pos[:, None, :] >= 0) & (k_pos[:, None, :] < S))
    sink_logit = sink.reshape(Hkv, G)[None, :, :, None, None].astype(jnp.float32)

    def one_block(args):
        qblk, kblk, vblk, mask = args
        s_loc = jnp.einsum('bqhgd,bkhd->bhgqk', qblk, kblk).astype(jnp.float32) * scale
        s_loc = jnp.where(mask, s_loc, -jnp.inf)
        s_ctx = jnp.einsum('bqhgd,blhd->bhgql', qblk, kc).astype(jnp.float32) * scale
        s_sink = jnp.broadcast_to(sink_logit, s_ctx.shape[:-1] + (1,))
        p = jax.nn.softmax(jnp.concatenate([s_ctx, s_loc, s_sink], axis=-1), axis=-1).astype(v.dtype)
        return (jnp.einsum('bhgql,blhd->bqhgd', p[..., :L], vc)
                + jnp.einsum('bhgqk,bkhd->bqhgd', p[..., L:L + 3 * W], vblk))

    out = lax.map(one_block, (qb, band(k), band(v), valid))
    return jnp.moveaxis(out, 0, 1).reshape(B, S, Hq, d)


def window_attention_context(qc, kc, vc, sink, scale):
    B, L, Hq, d = qc.shape
    Hkv = kc.shape[2]
    G = Hq // Hkv
    qg = qc.reshape(B, L, Hkv, G, d)
    s = jnp.einsum('blhgd,bmhd->bhglm', qg, kc).astype(jnp.float32) * scale
    s_sink = jnp.broadcast_to(sink.reshape(Hkv, G)[None, :, :, None, None].astype(jnp.float32), s.shape[:-1] + (1,))
    p = jax.nn.softmax(jnp.concatenate([s, s_sink], axis=-1), axis=-1)[..., :-1].astype(vc.dtype)
    return jnp.einsum('bhglm,bmhd->blhgd', p, vc).reshape(B, L, Hq, d)


def mixer_mla_swa(h, hc, need_ctx, rope_mla, rope_swa, w_in, mla_qa_g, mla_wqb, mla_kva_g, mla_wkvb,
                  mla_q_g, mla_k_g, swa_q_g, swa_k_g, swa_sink, w_out):
    B, S = h.shape[:2]
    split_at = [int(i) for i in np.cumsum(L0_IN_SIZES)[:-1]]
    qa, kva, kr, sq, sk, sv = jnp.split(h @ w_in, split_at, axis=-1)
    qa_c, kva_c, kr_c, sq_c, sk_c, sv_c = jnp.split(hc @ w_in, split_at, axis=-1)
    mla_scale = (MLA_NOPE + MLA_ROPE) ** -0.5
    swa_scale = SWA_HEAD_DIM ** -0.5

    q = mla_queries(qa, mla_qa_g, mla_wqb, mla_q_g, rope_mla)
    k, v = mla_keys_values(kva, kr, mla_kva_g, mla_wkvb, mla_k_g, rope_mla)
    k_c, v_c = mla_keys_values(kva_c, kr_c, mla_kva_g, mla_wkvb, mla_k_g, None)
    k_all = jnp.concatenate([k_c, k], axis=1)
    v_all = jnp.concatenate([v_c, v], axis=1)
    a = sweep_query_blocks(lambda qb: softmax_attention_block(qb, k_all, v_all, mla_scale), q)

    def swa_heads(p, n_heads, g, rope):
        t = p.reshape(p.shape[0], p.shape[1], n_heads, SWA_HEAD_DIM)
        t = rms_norm(t, g) if g is not None else t
        return apply_rope(t, *rope) if rope is not None else t

    sq_l = swa_heads(sq, SWA_HEADS, swa_q_g, rope_swa)
    sk_l = swa_heads(sk, SWA_KV_HEADS, swa_k_g, rope_swa)
    sv_l = swa_heads(sv, SWA_KV_HEADS, None, None)
    sk_cc = swa_heads(sk_c, SWA_KV_HEADS, swa_k_g, None)
    sv_cc = swa_heads(sv_c, SWA_KV_HEADS, None, None)
    b = window_attention_latent(sq_l, sk_l, sv_l, sk_cc, sv_cc, swa_sink, swa_scale)

    y = jnp.concatenate([a.reshape(B, S, -1), b.reshape(B, S, -1)], axis=-1) @ w_out
    if not need_ctx:
        return y, None
    L = hc.shape[1]
    a_c = softmax_attention_block(mla_queries(qa_c, mla_qa_g, mla_wqb, mla_q_g, None), k_c, v_c, mla_scale)
    b_c = window_attention_context(swa_heads(sq_c, SWA_HEADS, swa_q_g, None), sk_cc, sv_cc, swa_sink, swa_scale)
    y_c = jnp.concatenate([a_c.reshape(B, L, -1), b_c.reshape(B, L, -1)], axis=-1) @ w_out
    return y, y_c


def mixer_diff(h, hc, need_ctx, rope, w_in, q_g, k_g, lambda_q1, lambda_k1, lambda_q2, lambda_k2,
               subln_g, w_out, lambda_init):
    scale = DIFF_HEAD_DIM ** -0.5

    def split_heads(t):
        B, N = t.shape[:2]
        q, k, v = jnp.split(t @ w_in, 3, axis=-1)
        return (q.reshape(B, N, DIFF_HEADS, 2, DIFF_HEAD_DIM),
                k.reshape(B, N, DIFF_HEADS, 2, DIFF_HEAD_DIM),
                v.reshape(B, N, DIFF_HEADS, 2 * DIFF_HEAD_DIM))

    def qk_prep(t, g, rope_t):
        t = rms_norm(t, g)
        return apply_rope(t, *rope_t) if rope_t is not None else t

    lam = (jnp.exp(jnp.sum(lambda_q1.astype(jnp.float32) * lambda_k1.astype(jnp.float32)))
           - jnp.exp(jnp.sum(lambda_q2.astype(jnp.float32) * lambda_k2.astype(jnp.float32)))
           + lambda_init)

    def diff_block(qb, k, v):
        s = jnp.einsum('bqhmd,bkhmd->bhmqk', qb, k).astype(jnp.float32) * scale
        p = jax.nn.softmax(s, axis=-1)
        a = (p[:, :, 0] - lam * p[:, :, 1]).astype(v.dtype)
        return jnp.einsum('bhqk,bkhd->bqhd', a, v)

    def heads_out(o):
        o = rms_norm(o, subln_g) * (1 - lambda_init)
        return o.reshape(o.shape[0], o.shape[1], -1) @ w_out

    q, k, v = split_heads(h)
    q, k = qk_prep(q, q_g, rope), qk_prep(k, k_g, rope)
    q_c, k_c, v_c = split_heads(hc)
    k_c = qk_prep(k_c, k_g, None)
    k_all = jnp.concatenate([k_c, k], axis=1)
    v_all = jnp.concatenate([v_c, v], axis=1)
    y = heads_out(sweep_query_blocks(lambda qb: diff_block(qb, k_all, v_all), q))
    if not need_ctx:
        return y, None
    y_c = heads_out(diff_block(qk_prep(q_c, q_g, None), k_c, v_c))
    return y, y_c


def setup_inputs(seed: int = 0) -> dict:
    key = jax.random.key(seed)
    keys = iter(jax.random.split(key, 48))
    f32 = jnp.float32

    def nrm(shape, s):
        return jax.random.normal(next(keys), shape, f32) * s

    def lin(shape, s=1.0):
        return nrm(shape, s * shape[-2] ** -0.5)

    def gain(shape):
        return 1.0 + nrm(shape, 0.05)

    def common(p):
        return {
            p + "ada_w": lin((D_MODEL, N_MOD * D_MODEL), 0.5),
            p + "ada_b": nrm((N_MOD * D_MODEL,), 0.02),
            p + "norm_g": gain((3, D_MODEL)),
            p + "ffn_wg": lin((2, D_MODEL, D_FF)),
            p + "ffn_wu": lin((2, D_MODEL, D_FF)),
            p + "ffn_wd": lin((2, D_FF, D_MODEL)),
        }

    inputs = {
        "x": nrm((BATCH, SEQ, D_MODEL), 1.0),
        "c": nrm((BATCH, D_MODEL), 1.0),
        "ctx": nrm((BATCH, CTX_LEN, D_MODEL), 1.0),
        "c_ctx": nrm((D_MODEL,), 1.0),
    }
    inputs.update(common("l0_"))
    inputs.update({
        "l0_w_in": lin((D_MODEL, L0_IN_WIDTH)),
        "l0_mla_qa_g": gain((MLA_Q_RANK,)),
        "l0_mla_wqb": lin((MLA_Q_RANK, MLA_HEADS * (MLA_NOPE + MLA_ROPE))),
        "l0_mla_kva_g": gain((MLA_KV_RANK,)),
        "l0_mla_wkvb": lin((MLA_KV_RANK, MLA_HEADS * (MLA_NOPE + MLA_V))),
        "l0_mla_q_g": gain((MLA_NOPE + MLA_ROPE,)),
        "l0_mla_k_g": gain((MLA_NOPE + MLA_ROPE,)),
        "l0_swa_q_g": gain((SWA_HEAD_DIM,)),
        "l0_swa_k_g": gain((SWA_HEAD_DIM,)),
        "l0_swa_sink": nrm((SWA_HEADS,), 1.0),
        "l0_w_out": lin((L0_OUT_WIDTH, D_MODEL)),
    })
    inputs.update(common("l1_"))
    inputs.update({
        "l1_w_in": lin((D_MODEL, L1_IN_WIDTH)),
        "l1_q_g": gain((DIFF_HEAD_DIM,)),
        "l1_k_g": gain((DIFF_HEAD_DIM,)),
        "l1_lambda_q1": nrm((DIFF_HEAD_DIM,), 0.1),
        "l1_lambda_k1": nrm((DIFF_HEAD_DIM,), 0.1),
        "l1_lambda_q2": nrm((DIFF_HEAD_DIM,), 0.1),
        "l1_lambda_k2": nrm((DIFF_HEAD_DIM,), 0.1),
        "l1_subln_g": gain((2 * DIFF_HEAD_DIM,)),
        "l1_w_out": lin((L1_OUT_WIDTH, D_MODEL)),
    })
    return inputs


def reference(x, c, ctx, c_ctx,
              l0_ada_w, l0_ada_b, l0_norm_g, l0_ffn_wg, l0_ffn_wu, l0_ffn_wd,
              l0_w_in, l0_mla_qa_g, l0_mla_wqb, l0_mla_kva_g, l0_mla_wkvb, l0_mla_q_g, l0_mla_k_g,
              l0_swa_q_g, l0_swa_k_g, l0_swa_sink, l0_w_out,
              l1_ada_w, l1_ada_b, l1_norm_g, l1_ffn_wg, l1_ffn_wu, l1_ffn_wd,
              l1_w_in, l1_q_g, l1_k_g, l1_lambda_q1, l1_lambda_k1, l1_lambda_q2, l1_lambda_k2,
              l1_subln_g, l1_w_out):
    n_rows = x.shape[1] // GRID_W
    rope_mla = axial_rope(n_rows, MLA_ROPE)
    rope_swa = axial_rope(n_rows, SWA_HEAD_DIM)
    rope_diff = axial_rope(n_rows, DIFF_HEAD_DIM)

    layers = [
        ((l0_ada_w, l0_ada_b, l0_norm_g, l0_ffn_wg, l0_ffn_wu, l0_ffn_wd),
         functools.partial(mixer_mla_swa, rope_mla=rope_mla, rope_swa=rope_swa, w_in=l0_w_in,
                           mla_qa_g=l0_mla_qa_g, mla_wqb=l0_mla_wqb, mla_kva_g=l0_mla_kva_g,
                           mla_wkvb=l0_mla_wkvb, mla_q_g=l0_mla_q_g, mla_k_g=l0_mla_k_g,
                           swa_q_g=l0_swa_q_g, swa_k_g=l0_swa_k_g, swa_sink=l0_swa_sink, w_out=l0_w_out)),
        ((l1_ada_w, l1_ada_b, l1_norm_g, l1_ffn_wg, l1_ffn_wu, l1_ffn_wd),
         functools.partial(mixer_diff, rope=rope_diff, w_in=l1_w_in, q_g=l1_q_g, k_g=l1_k_g,
                           lambda_q1=l1_lambda_q1, lambda_k1=l1_lambda_k1,
                           lambda_q2=l1_lambda_q2, lambda_k2=l1_lambda_k2,
                           subln_g=l1_subln_g, w_out=l1_w_out, lambda_init=lambda_init_fn(1))),
    ]

    h, hc = x, ctx
    for layer in range(DEPTH):
        (ada_w, ada_b, norm_g, wg, wu, wd), mixer = layers[layer]
        need_ctx = layer < DEPTH - 1
        mod = (jax.nn.silu(c) @ ada_w + ada_b).reshape(c.shape[0], 1, N_MOD, D_MODEL)
        mod_c = (jax.nn.silu(c_ctx) @ ada_w + ada_b).reshape(N_MOD, D_MODEL)
        h = h + 0.5 * mod[..., 2, :] * swiglu(adaln(h, norm_g[0], mod, 0), wg[0], wu[0], wd[0])
        hc = hc + 0.5 * mod_c[..., 2, :] * swiglu(adaln(hc, norm_g[0], mod_c, 0), wg[0], wu[0], wd[0])
        y, y_c = mixer(adaln(h, norm_g[1], mod, 1), adaln(hc, norm_g[1], mod_c, 1), need_ctx)
        h = h + mod[..., 5, :] * y
        h = h + 0.5 * mod[..., 8, :] * swiglu(adaln(h, norm_g[2], mod, 2), wg[1], wu[1], wd[1])
        if need_ctx:
            hc = hc + mod_c[..., 5, :] * y_c
            hc = hc + 0.5 * mod_c[..., 8, :] * swiglu(adaln(hc, norm_g[2], mod_c, 2), wg[1], wu[1], wd[1])
    return h
```

```python
import math
import numpy as np
import concourse.bass as bass
import concourse.mybir as mybir
from concourse.bass_utils import run_bass_kernel_spmd

F32 = mybir.dt.float32
BF16 = mybir.dt.bfloat16
AF = mybir.ActivationFunctionType
ALU = mybir.AluOpType
AX = mybir.AxisListType

D = 1024
S = 2048
LC = 256
T = S + LC
DFF = 2816
NJ = DFF // 128
EPS = 1e-6
NCORES = 8
LAMBDA_INIT1 = 0.8 - 0.6 * math.exp(-0.3 * 1)

ENGINES = ("pe", "act", "dve", "pool", "sp")
EPOCH = 30000


class Op:
    __slots__ = ("eng", "fn", "deps", "dma_key", "dma_cnt", "inc", "cnt", "waits", "idx", "n_dma")

    def __init__(self, eng, fn):
        self.eng = eng
        self.fn = fn
        self.deps = set()
        self.dma_key = None
        self.dma_cnt = 0
        self.inc = False
        self.cnt = 0
        self.waits = []
        self.n_dma = 1


class Prog:
    def __init__(self, nc):
        self.nc = nc
        self.ops = []
        self.per_eng = {e: [] for e in ENGINES}
        self.last_writer = {}
        self.readers = {}
        self.pending_bar = {e: set() for e in ENGINES}
        self.dma_counts = {}
        self.dma_since_bar = []

    def capture_begin(self):
        self.cap = []

    def capture_end(self):
        c = self.cap
        self.cap = None
        return c

    def replay_zip(self, lists):
        idx = [0] * len(lists)
        while True:
            alive = False
            for li, L in enumerate(lists):
                if idx[li] < len(L):
                    alive = True
                    self._add(*L[idx[li]])
                    idx[li] += 1
            if not alive:
                break

    def _add(self, eng, fn, reads, writes, dma_key=None, n_dma=1):
        if getattr(self, "cap", None) is not None:
            self.cap.append((eng, fn, list(reads), list(writes), dma_key, n_dma))
            return None
        op = Op(eng, fn)
        op.idx = len(self.ops)
        deps = op.deps
        for r in reads:
            w = self.last_writer.get(r)
            if w is not None:
                deps.add(w)
            self.readers.setdefault(r, []).append(op.idx)
        for w_ in writes:
            w = self.last_writer.get(w_)
            if w is not None:
                deps.add(w)
            rl = self.readers.get(w_)
            if rl:
                deps.update(rl)
            self.last_writer[w_] = op.idx
            self.readers[w_] = []
        if self.pending_bar[eng]:
            deps.update(self.pending_bar[eng])
            self.pending_bar[eng] = set()
        deps.discard(op.idx)
        if dma_key is not None:
            op.dma_key = dma_key
            op.n_dma = n_dma
            c = self.dma_counts.get(dma_key, 0) + n_dma
            self.dma_counts[dma_key] = c
            op.dma_cnt = c
            self.dma_since_bar.append(op.idx)
        self.ops.append(op)
        self.per_eng[eng].append(op)
        return op

    def pe(self, fn, reads, writes):
        return self._add("pe", fn, reads, writes)

    def act(self, fn, reads, writes):
        return self._add("act", fn, reads, writes)

    def dve(self, fn, reads, writes):
        return self._add("dve", fn, reads, writes)

    def pool(self, fn, reads, writes):
        return self._add("pool", fn, reads, writes)

    def dma(self, queue, fn, reads, writes, key, n_dma=1):
        return self._add(queue, fn, reads, writes, dma_key=key, n_dma=n_dma)

    def barrier(self):
        last = set()
        for e in ENGINES:
            for op in reversed(self.per_eng[e]):
                if op.dma_key is None:
                    last.add(op.idx)
                    break
        last.update(self.dma_since_bar)
        self.dma_since_bar = []
        for e in ENGINES:
            self.pending_bar[e] = set(last)

    def finalize_and_emit(self, block):
        nc = self.nc
        ops = self.ops
        for op in ops:
            for d in op.deps:
                dop = ops[d]
                if dop.dma_key is not None:
                    continue
                if dop.eng == "pe" and op.eng == "pe" and op.dma_key is None:
                    continue
                dop.inc = True
        counts = {e: 0 for e in ENGINES}
        for e in ENGINES:
            for op in self.per_eng[e]:
                if op.dma_key is None and op.inc:
                    counts[e] += 1
                    op.cnt = counts[e]
        import contextlib
        stack = contextlib.ExitStack()
        eng_sems = {}
        for e in ENGINES:
            n_ep = counts[e] // EPOCH + 1
            eng_sems[e] = [stack.enter_context(nc.semaphore(f"s_{e}{i}")) for i in range(n_ep)]
        dma_sems = {k: stack.enter_context(nc.semaphore(f"d_{k}")) for k in self.dma_counts}
        self.n_sems = sum(len(v) for v in eng_sems.values()) + len(dma_sems)
        for e in ENGINES:
            seen = {}
            for op in self.per_eng[e]:
                need = {}
                for d in op.deps:
                    dop = ops[d]
                    if dop.dma_key is not None:
                        key = ("d", dop.dma_key)
                        val = 16 * dop.dma_cnt
                    else:
                        if dop.eng == "pe" and e == "pe" and op.dma_key is None:
                            continue
                        c = dop.cnt - 1
                        key = ("e", dop.eng, c // EPOCH)
                        val = c % EPOCH + 1
                    if seen.get(key, 0) >= val:
                        continue
                    if need.get(key, 0) < val:
                        need[key] = val
                for key, val in need.items():
                    seen[key] = val
                    sem = dma_sems[key[1]] if key[0] == "d" else eng_sems[key[1]][key[2]]
                    op.waits.append((sem, val))

        def emit(engname, eng):
            for op in self.per_eng[engname]:
                for sem, val in op.waits:
                    eng.wait_ge(sem, val)
                if op.fn is None:
                    continue
                ins = op.fn(eng)
                if op.dma_key is not None:
                    assert len(ins) == op.n_dma, (len(ins), op.n_dma)
                    for i_ in ins:
                        i_.then_inc(dma_sems[op.dma_key], 16)
                elif op.inc:
                    c = op.cnt - 1
                    ins.then_inc(eng_sems[engname][c // EPOCH], 1)

        @block.tensor
        def _(eng):
            emit("pe", eng)

        @block.scalar
        def _(eng):
            emit("act", eng)

        @block.vector
        def _(eng):
            emit("dve", eng)

        @block.gpsimd
        def _(eng):
            emit("pool", eng)

        @block.sync
        def _(eng):
            emit("sp", eng)

        return stack


class Arena:
    def __init__(self, ap_f32, nbytes):
        self.ap = ap_f32
        self.nbytes = nbytes
        self.off = 0
        self.peak = 0

    def mark(self):
        return self.off

    def reset(self, m):
        self.off = m

    def alloc(self, shape_free, dtype):
        esz = 4 if dtype == F32 else 2
        n = int(np.prod(shape_free))
        nb = (n * esz + 31) // 32 * 32
        assert self.off + nb <= self.nbytes, f"arena overflow {self.off}+{nb}>{self.nbytes}"
        a = self.ap[:, self.off // 4:(self.off + nb) // 4]
        self.off += nb
        self.peak = max(self.peak, self.off)
        if dtype != F32:
            a = a.bitcast(dtype)
        a = a[:, 0:n]
        if len(shape_free) == 2:
            a = a.rearrange("p (a b) -> p a b", b=shape_free[1])
        elif len(shape_free) == 3:
            a = a.rearrange("p (a b c) -> p a b c", b=shape_free[1], c=shape_free[2])
        return a


def mm_group(args):
    def fn(e):
        ins = None
        for (o, l, r, st, sp) in args:
            ins = e.matmul(o, lhsT=l, rhs=r, start=st, stop=sp)
        return ins
    return fn


def tr_group(args):
    def fn(e):
        ins = None
        for (o, i, idn) in args:
            ins = e.transpose(o, i, idn)
        return ins
    return fn


def act_fn(out, in_, func, bias=None, scale=None, accum_out=None):
    def fn(e):
        kw = {}
        if bias is not None:
            kw["bias"] = bias
        if scale is not None:
            kw["scale"] = scale
        if accum_out is not None:
            kw["accum_out"] = accum_out
        return e.activation(out=out, in_=in_, func=func, **kw)
    return fn


def tt_fn(out, in0, in1, op):
    return lambda e: e.tensor_tensor(out=out, in0=in0, in1=in1, op=op)


def ts_fn(out, in0, s1, op0, s2=None, op1=None):
    if op1 is None:
        return lambda e: e.tensor_scalar(out=out, in0=in0, scalar1=s1, scalar2=None, op0=op0)
    return lambda e: e.tensor_scalar(out=out, in0=in0, scalar1=s1, scalar2=s2, op0=op0, op1=op1)


def stt_fn(out, in0, scalar, in1, op0, op1):
    return lambda e: e.scalar_tensor_tensor(out=out, in0=in0, scalar=scalar, in1=in1, op0=op0, op1=op1)


def copy_fn(out, in_):
    return lambda e: e.tensor_copy(out=out, in_=in_)


def red_fn(out, in_, op=None):
    return lambda e: e.tensor_reduce(out=out, in_=in_, axis=AX.X, op=(op or ALU.add))


def recip_fn(out, in_):
    return lambda e: e.reciprocal(out=out, in_=in_)


def run_pipe(items, s1, s2, lag=2, tick=None):
    toks = {}
    n = len(items)
    for i in range(n + lag):
        if i < n:
            toks[i] = s1(items[i])
        if i >= lag:
            s2(items[i - lag], toks.pop(i - lag))
        if tick is not None:
            tick(i)


def dma_fn(pairs):
    def fn(e):
        return [e.dma_start(out=o, in_=i) for (o, i) in pairs]
    return fn


FT = [(0, 256), (256, 512), (768, 512), (1280, 512), (1792, 512)]


def ft_of_tt(tt):
    return 0 if tt < 2 else 1 + (tt - 2) // 4


def v3(ap, b):
    return ap.rearrange("p (a b) -> p a b", b=b)


G0_QA, G0_KVA, G0_Q4, G0_KN4, G0_KR, G0_SQ8, G0_SK2, G0_SINK, G0N = 0, 256, 384, 768, 1024, 1056, 1568, 1696, 1704
G1N = 129
AR_BYTES = 210944


def build_program(stop_after=None, nseq=2):
    nc = bass.Bass("TRN2", target_bir_lowering=False)

    def din(name, shape):
        return nc.dram_tensor(name, list(shape), F32, kind="ExternalInput").ap()

    xT_in = din("xT", [2, D, S])
    ctxT_in = din("ctxT", [2, D, LC])
    c3_in = din("c3", [D, 3])
    ada_w = [din(f"ada_w{l}", [D, 9 * D]) for l in range(2)]
    ada_b = [din(f"ada_bT{l}", [128, 72]) for l in range(2)]
    norm_g = [din(f"norm_gT{l}", [128, 24]) for l in range(2)]
    wg = [din(f"wg{l}", [2, D, DFF]) for l in range(2)]
    wu = [din(f"wu{l}", [2, D, DFF]) for l in range(2)]
    wd = [din(f"wd{l}", [2, DFF, D]) for l in range(2)]
    w_in0 = din("w_in0", [D, 1184])
    wqb_in = din("wqb", [256, 768])
    wkvb_in = din("wkvb", [128, 1024])
    w_out = [din("w_out0", [D, D]), din("w_out1", [D, D])]
    gains0 = din("gains0", [128, G0N])
    w_in1 = din("w_in1", [D, 3072])
    gains1 = din("gains1", [128, G1N])
    lam_in = din("lam", [64, 4])
    consts_f = din("consts_f", [128, 1536])
    consts_b = din("consts_b", [128, 1280])
    outT = nc.dram_tensor("outT", [2, D, S], F32, kind="ExternalOutput").ap()
    xspill = nc.dram_tensor("xspill", [128, 8 * T], F32, kind="Internal").ap()

    import contextlib
    ctx = contextlib.ExitStack()
    arena_t = ctx.enter_context(nc.sbuf_tensor("arena", [128, AR_BYTES // 4], F32))
    ps = [ctx.enter_context(nc.psum_tensor(f"ps{i}", [128, 512], F32)) for i in range(8)]
    psb = [p_[:, :].bitcast(BF16) for p_ in ps]

    A = Arena(arena_t[:, :], AR_BYTES)
    ident = A.alloc((128,), BF16)
    ones = A.alloc((128,), BF16)
    maskp = A.alloc((512,), BF16)
    maskn = A.alloc((512,), BF16)
    onesf = A.alloc((128,), F32)
    ropeA = A.alloc((512,), F32)
    ropeB = A.alloc((1024,), F32)
    ropeA_v = v3(ropeA, 32)
    ropeB_v = v3(ropeB, 64)
    mod = [A.alloc((216,), F32) for _ in range(2)]
    mod_v = [v3(m_, 3) for m_ in mod]
    gs = [[v3(A.alloc((24,), F32), 3) for _ in range(3)] for _ in range(2)]
    hgt = [[v3(A.alloc((24,), F32), 3) for _ in range(3)] for _ in range(2)]
    adab = [A.alloc((72,), F32) for _ in range(2)]
    normg = [A.alloc((24,), F32) for _ in range(2)]
    c3s = A.alloc((24,), F32)
    sc = A.alloc((24,), F32)
    zero_c = A.alloc((1,), F32)
    eps_c = A.alloc((1,), F32)
    lamv = A.alloc((4,), F32)
    lamp = A.alloc((2,), F32)
    e12 = A.alloc((2,), F32)
    neglam = A.alloc((1,), F32)
    sgl = A.alloc((1,), F32)
    ws = [A.alloc((12288,), BF16) for _ in range(2)]
    x_mark = A.mark()
    xT = A.alloc((8 * T,), F32)
    xT_v = v3(xT, T)
    work_mark = A.mark()

    P = Prog(nc)
    WL = []
    wl_use = [0]
    wl_rec = [0]

    def next_slot():
        i = wl_use[0]
        wl_use[0] += 1
        while wl_rec[0] < len(WL) and wl_rec[0] <= i + 1:
            j = wl_rec[0]
            WL[j](j % 2)
            wl_rec[0] += 1
        return i % 2

    def X(k, ft):
        return f"x{k}.{ft}"

    def H(k, ft):
        return f"h{k}.{ft}"

    P.dma("sp", dma_fn([(ropeA, consts_f[:, 0:512]), (ropeB, consts_f[:, 512:1536])]), [], ["ropeA", "ropeB"], key="c0", n_dma=2)
    P.dma("pool", dma_fn([(ident, consts_b[:, 0:128]), (ones, consts_b[:, 128:256]),
                          (maskp, consts_b[:, 256:768]), (maskn, consts_b[:, 768:1280])]),
          [], ["ident", "ones", "maskp", "maskn"], key="c1", n_dma=4)
    P.dve(lambda e: e.memset(zero_c, 0.0), [], ["zero_c"])
    P.dve(lambda e: e.memset(eps_c, EPS), [], ["eps_c"])
    P.dve(lambda e: e.memset(onesf, 1.0), [], ["onesf"])
    P.dma("sp", dma_fn([(v3(c3s, 3), c3_in.rearrange("(k p) n -> p k n", p=128)),
                        (adab[0], ada_b[0][:, :]), (adab[1], ada_b[1][:, :]),
                        (normg[0], norm_g[0][:, :]), (normg[1], norm_g[1][:, :]),
                        (lamv[0:64, :], lam_in[:, :])]),
          [], ["c3s", "adab", "normg", "lamv"], key="c2", n_dma=6)
    scb = sc.bitcast(BF16)[:, 0:24]
    P.act(act_fn(scb, c3s, AF.Silu, bias=zero_c), ["c3s", "zero_c"], ["sc"])
    sc_v = v3(scb, 3)
    P.dve(tt_fn(lamp[0:64, 0:1], lamv[0:64, 0:1], lamv[0:64, 1:2], ALU.mult), ["lamv"], ["lamp0"])
    P.dve(tt_fn(lamp[0:64, 1:2], lamv[0:64, 2:3], lamv[0:64, 3:4], ALU.mult), ["lamv"], ["lamp1"])
    P.pe(mm_group([(ps[2][:, 0:2], onesf[0:64, :], lamp[0:64, :], True, True)]), ["lamp0", "lamp1", "onesf"], ["ps2"])
    P.act(act_fn(e12, ps[2][:, 0:2], AF.Exp, bias=zero_c), ["ps2", "zero_c"], ["e12"])
    P.dve(stt_fn(neglam, e12[:, 1:2], -LAMBDA_INIT1, e12[:, 0:1], ALU.add, ALU.subtract), ["e12"], ["neglam"])
    def load_x(s):
        allx = [X(k, ft) for k in range(8) for ft in range(5)]
        P.dma("sp", dma_fn([(xT_v[:, :, LC:T], xT_in[s].rearrange("(k p) t -> p k t", p=128)),
                            (xT_v[:, :, 0:LC], ctxT_in[s].rearrange("(k p) t -> p k t", p=128))]),
              [], allx, key="xload", n_dma=2)

    load_x(0)
    _wa = Arena(arena_t[:, :], AR_BYTES)
    _wa.reset(work_mark)
    adas = [v3(_wa.alloc((9216,), BF16), 1152) for _ in range(2)]
    for l in range(2):
        awr = ada_w[l].rearrange("(k p) n -> p k n", p=128)
        for blk in range(8):
            s_ = blk % 2
            P.dma("pool", dma_fn([(adas[s_], awr[:, :, blk * 1152:(blk + 1) * 1152])]), [], [f"adas{s_}"], key=f"adas{s_}")
            args = []
            for j in range(9):
                jj = blk * 9 + j
                for k in range(8):
                    args.append((ps[l][:, jj * 3:(jj + 1) * 3], adas[s_][:, k, j * 128:(j + 1) * 128], sc_v[:, k, :], k == 0, k == 7))
            P.pe(mm_group(args), [f"adas{s_}", "sc"], [f"ps{l}"])
        P.dve(tt_fn(mod_v[l], v3(ps[l][:, 0:216], 3), adab[l].unsqueeze(2).broadcast_to([128, 72, 3]), ALU.add),
              [f"ps{l}", "adab"], [f"mod{l}"])
        for i in range(3):
            for col in range(3):
                P.dve(stt_fn(gs[l][i][:, :, col], mod_v[l][:, (3 * i + 1) * 8:(3 * i + 2) * 8, col], 1.0,
                             normg[l][:, i * 8:(i + 1) * 8], ALU.add, ALU.mult), [f"mod{l}", "normg"], [f"gs{l}"])
            P.dve(ts_fn(hgt[l][i], mod_v[l][:, (3 * i + 2) * 8:(3 * i + 3) * 8, :], (1.0 if i == 1 else 0.5), ALU.mult),
                  [f"mod{l}"], [f"hg{l}"])
    P.barrier()

    def norm_to_h(l, i, s, tiles, hT_v, W):
        sq = [[W.alloc((512,), BF16) for _ in range(2)] for _ in range(2)]
        lnv = [W.alloc((512,), F32) for _ in range(2)]
        rstd = [W.alloc((512,), F32) for _ in range(2)]
        tn = [[W.alloc((512,), F32) for _ in range(2)] for _ in range(2)]
        caps = []
        for ti, ft in enumerate(tiles):
            P.capture_begin()
            par = ti % 2
            pbank = ps[6 + par]
            t0, n = FT[ft]
            col = 2 if ft == 0 else s
            for k in range(8):
                b = k % 2
                if k % 2 == 0:
                    P.act(act_fn(sq[par][b][:, :n], xT_v[:, k, t0:t0 + n], AF.Square, bias=zero_c), [X(k, ft)], [f"nsq{par}{b}"])
                else:
                    P.dve(tt_fn(sq[par][b][:, :n], xT_v[:, k, t0:t0 + n], xT_v[:, k, t0:t0 + n], ALU.mult), [X(k, ft)], [f"nsq{par}{b}"])
                P.pe(mm_group([(pbank[:, :n], ones, sq[par][b][:, :n], k == 0, k == 7)]), [f"nsq{par}{b}", "ones"], [f"ps{6 + par}"])
            P.act(act_fn(lnv[par][:, :n], pbank[:, :n], AF.Ln, bias=eps_c, scale=1.0 / D), [f"ps{6 + par}", "eps_c"], [f"nlnv{par}"])
            P.act(act_fn(rstd[par][:, :n], lnv[par][:, :n], AF.Exp, bias=zero_c, scale=-0.5), [f"nlnv{par}"], [f"nrstd{par}"])
            for k in range(8):
                b = k % 2
                P.dve(tt_fn(tn[par][b][:, :n], xT_v[:, k, t0:t0 + n], rstd[par][:, :n], ALU.mult), [X(k, ft), f"nrstd{par}"], [f"ntn{par}{b}"])
                P.act(act_fn(hT_v[:, k, t0:t0 + n], tn[par][b][:, :n], AF.Identity,
                             bias=mod_v[l][:, 3 * i * 8 + k, col:col + 1], scale=gs[l][i][:, k, col:col + 1]),
                      [f"ntn{par}{b}", f"mod{l}", f"gs{l}"], [H(k, ft)])
            caps.append(P.capture_end())
        for i_ in range(0, len(caps), 2):
            P.replay_zip(caps[i_:i_ + 2])

    FFN_GROUPS = [(0, 4), (4, 4), (8, 4), (12, 4), (16, 4), (20, 2)]

    def wl_ffn(l, f, g):
        def rec(s_):
            j0, ncn = FFN_GROUPS[g]
            wgr = wg[l][f].rearrange("(k p) n -> p k n", p=128)
            wur = wu[l][f].rearrange("(k p) n -> p k n", p=128)
            wdr = wd[l][f].rearrange("(j p) n -> p j n", p=128)
            wgs = v3(ws[s_][:, 0:4096], 512)
            wus = v3(ws[s_][:, 4096:8192], 512)
            wds = v3(ws[s_][:, 8192:12288], 1024)
            P.dma("pool", dma_fn([(wgs[:, :, 0:ncn * 128], wgr[:, :, j0 * 128:(j0 + ncn) * 128]),
                                  (wus[:, :, 0:ncn * 128], wur[:, :, j0 * 128:(j0 + ncn) * 128]),
                                  (wds[:, 0:ncn, :], wdr[:, j0:j0 + ncn, :])]),
                  [], [f"ws{s_}"], key=f"ws{s_}", n_dma=3)
        return rec

    def wl_diff(hg):
        def rec(s_):
            w1r = w_in1.rearrange("(k p) n -> p k n", p=128)
            wv = v3(ws[s_][:, 0:12288], 1536)
            P.dma("pool", dma_fn([(wv[:, :, 0:512], w1r[:, :, hg * 512:(hg + 1) * 512]),
                                  (wv[:, :, 512:1024], w1r[:, :, 1024 + hg * 512:1024 + (hg + 1) * 512]),
                                  (wv[:, :, 1024:1536], w1r[:, :, 2048 + hg * 512:2048 + (hg + 1) * 512])]),
                  [], [f"ws{s_}"], key=f"ws{s_}", n_dma=3)
        return rec

    def wl_wout1():
        def rec(s_):
            P.dma("pool", dma_fn([(v3(ws[s_][:, 0:8192], 1024), w_out[1].rearrange("(h p) n -> p h n", p=128))]), [], [f"ws{s_}"], key=f"ws{s_}")
        return rec

    def wl_mla(hg):
        def rec(s_):
            w0r = w_in0.rearrange("(k p) n -> p k n", p=128)
            win_v = v3(ws[s_][:, 0:3328], 416)
            wqb_v = v3(ws[s_][:, 3328:4096], 384)
            wkvb_v = ws[s_][:, 4096:4608]
            P.dma("pool", dma_fn([(win_v, w0r[:, :, 0:416]),
                                  (wqb_v, wqb_in.rearrange("(j p) n -> p j n", p=128)[:, :, hg * 384:(hg + 1) * 384]),
                                  (wkvb_v, wkvb_in[:, hg * 512:(hg + 1) * 512])]),
                  [], [f"ws{s_}"], key=f"ws{s_}", n_dma=3)
        return rec

    def wl_swa():
        def rec(s_):
            w0r = w_in0.rearrange("(k p) n -> p k n", p=128)
            wsv = v3(ws[s_][:, 0:6144], 768)
            P.dma("pool", dma_fn([(wsv, w0r[:, :, 416:1184])]), [], [f"ws{s_}"], key=f"ws{s_}")
        return rec

    def wl_wout0():
        def rec(s_):
            wo = v3(ws[s_][:, 0:8192], 1024)
            P.dma("pool", dma_fn([(wo[:, 0:4, :], w_out[0][0:512, :].rearrange("(h p) n -> p h n", p=128)),
                                  (wo[0:64, 4:8, :], w_out[0][512:768, :].rearrange("(h p) n -> p h n", p=64)),
                                  (wo[64:128, 4:8, :], w_out[0][768:1024, :].rearrange("(h p) n -> p h n", p=64))]),
                  [], [f"ws{s_}"], key=f"ws{s_}", n_dma=3)
        return rec

    def ffn(l, f, s, tiles):
        i = 0 if f == 0 else 2
        W = Arena(arena_t[:, :], AR_BYTES)
        W.reset(work_mark)
        hT = W.alloc((8 * T,), BF16)
        hT_v = v3(hT, T)
        sg = [W.alloc((512,), F32) for _ in range(2)]
        Ab = [[W.alloc((512,), BF16) for _ in range(4)] for _ in range(2)]
        norm_to_h(l, i, s, tiles, hT_v, W)
        groups = FFN_GROUPS
        ycnt = [0]
        for g in range(len(groups)):
            j0, ncn = groups[g]
            s_ = next_slot()
            wgs = v3(ws[s_][:, 0:4096], 512)
            wus = v3(ws[s_][:, 4096:8192], 512)
            wds = v3(ws[s_][:, 8192:12288], 1024)

            def GU(ft, ab):
                t0, n = FT[ft]
                for jj in range(ncn):
                    gb = jj % 2
                    hreads = [H(k, ft) for k in range(8)]
                    P.pe(mm_group([(ps[gb][:, :n], wgs[:, k, jj * 128:(jj + 1) * 128], hT_v[:, k, t0:t0 + n], k == 0, k == 7) for k in range(8)]),
                         [f"ws{s_}"] + hreads, [f"ps{gb}"])
                    P.pe(mm_group([(ps[2 + gb][:, :n], wus[:, k, jj * 128:(jj + 1) * 128], hT_v[:, k, t0:t0 + n], k == 0, k == 7) for k in range(8)]),
                         [f"ws{s_}"] + hreads, [f"ps{2 + gb}"])
                    P.act(act_fn(sg[gb][:, :n], ps[gb][:, :n], AF.Silu, bias=zero_c), [f"ps{gb}"], [f"sg{gb}"])
                    P.dve(tt_fn(Ab[ab][jj][:, :n], sg[gb][:, :n], ps[2 + gb][:, :n], ALU.mult), [f"sg{gb}", f"ps{2 + gb}"], [f"A{ab}.{jj}"])

            def YD(ft, ab):
                t0, n = FT[ft]
                col = 2 if ft == 0 else s
                for c in range(8):
                    yb = 4 + ycnt[0] % 3
                    ycnt[0] += 1
                    P.pe(mm_group([(ps[yb][:, :n], wds[:, jj, c * 128:(c + 1) * 128], Ab[ab][jj][:, :n], jj == 0, jj == ncn - 1) for jj in range(ncn)]),
                         [f"ws{s_}"] + [f"A{ab}.{jj}" for jj in range(ncn)], [f"ps{yb}"])
                    P.dve(stt_fn(xT_v[:, c, t0:t0 + n], ps[yb][:, :n], hgt[l][i][:, c, col:col + 1], xT_v[:, c, t0:t0 + n], ALU.mult, ALU.add),
                          [f"ps{yb}", X(c, ft), f"hg{l}"], [X(c, ft)])

            for idx, ft in enumerate(tiles):
                GU(ft, idx % 2)
                if idx > 0:
                    YD(tiles[idx - 1], (idx - 1) % 2)
            YD(tiles[-1], (len(tiles) - 1) % 2)

    def spill_x(tiles):
        xs = v3(xspill, T)
        reads = [X(k, ft) for k in range(8) for ft in range(5)]
        P.dma("sp", dma_fn([(xspill[:, :], xT)]), reads, ["xspill"], key="spill")

    def wout_phase(l, s, tiles, aT_v, a_names, nchunk, wo_load):
        xs = v3(xspill, T)
        s_ = wo_load()
        wo = v3(ws[s_][:, 0:8192], 1024)
        yc = 0
        for ft in tiles:
            t0, n = FT[ft]
            col = 2 if ft == 0 else s
            P.dma("sp", dma_fn([(xT_v[:, :, t0:t0 + n], xs[:, :, t0:t0 + n])]), ["xspill"], [X(k, ft) for k in range(8)], key=f"xrl{ft}")
            for c in range(8):
                yb = 4 + yc % 3
                yc += 1
                P.pe(mm_group([(ps[yb][:, :n], wo[:, h, c * 128:(c + 1) * 128], aT_v[:, h, t0:t0 + n], h == 0, h == nchunk - 1) for h in range(nchunk)]),
                     [f"ws{s_}"] + a_names(ft), [f"ps{yb}"])
                P.dve(stt_fn(xT_v[:, c, t0:t0 + n], ps[yb][:, :n], hgt[l][1][:, c, col:col + 1], xT_v[:, c, t0:t0 + n], ALU.mult, ALU.add),
                      [f"ps{yb}", X(c, ft), f"hg{l}"], [X(c, ft)])

    def mixer1(s):
        l = 1
        P.barrier()
        W = Arena(arena_t[:, :], AR_BYTES)
        W.reset(work_mark)
        hT_v = v3(W.alloc((8 * T,), BF16), T)
        wmark = W.mark()
        norm_to_h(l, 1, s, range(5), hT_v, W)
        spill_x(range(5))
        P.barrier()
        W.reset(wmark)
        aT_v = v3(W.alloc((8 * T,), BF16), T)
        Xa = Arena(arena_t[:, :], AR_BYTES)
        Xa.reset(x_mark)
        xlimit = x_mark + 8 * T * 4
        QT_v = v3(Xa.alloc((4 * S,), BF16), S)
        KT_v = v3(Xa.alloc((4 * T,), BF16), T)
        Vb_v = v3(Xa.alloc((18 * 512,), BF16), 512)
        g1 = W.alloc((G1N,), F32)
        sm_ = [W.alloc((8,), F32) for _ in range(8)]
        P.dma("sp", dma_fn([(g1, gains1[:, :])]), [], ["g1"], key="g1")
        P.dve(ts_fn(sgl, g1[:, 128:129], 1.0 - LAMBDA_INIT1, ALU.mult), ["g1"], ["sgl"])
        tmark = Xa.mark()
        w1r = w_in1.rearrange("(k p) n -> p k n", p=128)
        for hg in range(2):
            Xa.reset(tmark)
            qn_ = [[Xa.alloc((512,), F32) for _ in range(2)] for _ in range(2)]
            qb_ = [[Xa.alloc((512,), BF16) for _ in range(2)] for _ in range(2)]
            U_ = [[Xa.alloc((512,), F32) for _ in range(2)] for _ in range(2)]
            ss8_ = [[sm_[0], sm_[1]], [sm_[2], sm_[3]]]
            rs8_ = [[sm_[4], sm_[5]], [sm_[6], sm_[7]]]
            assert Xa.off <= xlimit, (Xa.off, xlimit)
            s_ = next_slot()
            wv = v3(ws[s_][:, 0:12288], 1536)
            cap_pe, cap_ch, cap_bk = [], [], []
            for tt in range(18):
                lat = tt >= 2
                ft = ft_of_tt(tt)
                tok = tt * 128
                hreads = [H(k, ft) for k in range(8)]
                par = tt % 2
                pb = 3 * par
                jobs = []
                if lat:
                    jobs.append((0, pb + 0, 0))
                jobs.append((1, pb + 1, 512))
                P.capture_begin()
                for (which, bank, coff) in jobs + [(2, pb + 2, 1024)]:
                    P.pe(mm_group([(ps[bank][:, :], hT_v[:, k, tok:tok + 128], wv[:, k, coff:coff + 512], k == 0, k == 7) for k in range(8)]),
                         [f"ws{s_}"] + hreads, [f"ps{bank}"])
                P.act(act_fn(Vb_v[:, tt, :], ps[pb + 2][:, :], AF.Identity, bias=zero_c), [f"ps{pb + 2}"], [f"V.{tt}"])
                cap_pe.append(P.capture_end())
                chains = []
                backs = []
                for (which, bank, coff) in jobs:
                    P.capture_begin()
                    tg = f"{par}{which}"
                    qn, qb, ss8, rs8 = qn_[par][which], qb_[par][which], ss8_[par][which], rs8_[par][which]
                    gt = g1[:, which * 64:(which + 1) * 64].unsqueeze(1).broadcast_to([128, 8, 64])
                    q3 = v3(qn, 64)
                    P.act(act_fn(qn, ps[bank][:, :], AF.Square, bias=zero_c), [f"ps{bank}"], [f"qn{tg}"])
                    P.dve(red_fn(ss8, q3), [f"qn{tg}"], [f"ss8{tg}"])
                    P.act(act_fn(rs8, ss8, AF.Ln, bias=eps_c, scale=1.0 / 64), [f"ss8{tg}"], [f"rs8{tg}"])
                    P.act(act_fn(rs8, rs8, AF.Exp, bias=zero_c, scale=-0.5), [f"rs8{tg}"], [f"rs8{tg}"])
                    P.dve(tt_fn(q3, v3(ps[bank][:, :], 64), rs8.unsqueeze(2).broadcast_to([128, 8, 64]), ALU.mult),
                          [f"ps{bank}", f"rs8{tg}"], [f"qn{tg}"])
                    if lat:
                        j = tt - 2
                        cosb = ropeB_v[:, j, 0:32].unsqueeze(1).broadcast_to([128, 8, 32])
                        sinb = ropeB_v[:, j, 32:64].unsqueeze(1).broadcast_to([128, 8, 32])
                        P.dve(tt_fn(q3, q3, gt, ALU.mult), [f"qn{tg}", "g1"], [f"qn{tg}"])
                        U = U_[par][which]
                        q4 = qn.rearrange("p (g h d) -> p g h d", h=2, d=32)
                        u4 = U.rearrange("p (g h d) -> p g h d", h=2, d=32)
                        cos4 = ropeB_v[:, j, 0:32].unsqueeze(1).unsqueeze(2).broadcast_to([128, 8, 2, 32])
                        sin4 = ropeB_v[:, j, 32:64].unsqueeze(1).unsqueeze(2).broadcast_to([128, 8, 2, 32])
                        u3 = v3(U, 64)
                        qb3 = v3(qb, 64)
                        P.dve(tt_fn(u4, q4, sin4, ALU.mult), [f"qn{tg}", "ropeB"], [f"U{tg}"])
                        P.dve(tt_fn(q4, q4, cos4, ALU.mult), [f"qn{tg}", "ropeB"], [f"qn{tg}"])
                        P.dve(tt_fn(qb3[:, :, 0:32], q3[:, :, 0:32], u3[:, :, 32:64], ALU.subtract), [f"qn{tg}", f"U{tg}"], [f"qb{tg}a"])
                        P.dve(tt_fn(qb3[:, :, 32:64], u3[:, :, 0:32], q3[:, :, 32:64], ALU.add), [f"qn{tg}", f"U{tg}"], [f"qb{tg}b"])
                    else:
                        P.dve(tt_fn(v3(qb, 64), q3, gt, ALU.mult), [f"qn{tg}", "g1"], [f"qb{tg}a", f"qb{tg}b"])
                    tb = 6 + par
                    tcol = which * 512
                    chains.append(P.capture_end())
                    P.capture_begin()
                    P.pe(tr_group([(psb[tb][:, tcol + h * 128:tcol + (h + 1) * 128], qb[:, h * 128:(h + 1) * 128], ident) for h in range(4)]),
                         [f"qb{tg}a", f"qb{tg}b", "ident"], [f"ps{tb}"])
                    if which == 0:
                        dst = QT_v[:, :, tok - LC:tok - LC + 128]
                        nm = f"QT.{tt}"
                    else:
                        dst = KT_v[:, :, tok:tok + 128]
                        nm = f"KT.{tt}"
                    P.act(act_fn(dst, v3(psb[tb][:, tcol:tcol + 512], 128), AF.Identity, bias=zero_c), [f"ps{tb}"], [nm])
                    backs.append(P.capture_end())
                cap_ch.append(chains)
                cap_bk.append(backs)
            P.replay_zip(cap_pe[0:2])
            for i_ in range(0, 18, 2):
                P.replay_zip(cap_ch[i_] + cap_ch[i_ + 1])
                if i_ + 2 < 18:
                    P.replay_zip(cap_pe[i_ + 2:i_ + 4])
                P.replay_zip(cap_bk[i_] + cap_bk[i_ + 1])
            P.barrier()
            Xa.reset(tmark)
            E = [Xa.alloc((512,), BF16) for _ in range(4)]
            r_ = [Xa.alloc((512,), F32) for _ in range(2)]
            t_ = [Xa.alloc((512,), F32) for _ in range(2)]
            osq = Xa.alloc((512,), BF16)
            rs = r_[1]
            Qp = [[Xa.alloc((512,), BF16) for _ in range(2)] for _ in range(2)]
            assert Xa.off <= xlimit, (Xa.off, xlimit)
            for ub in range(2):
                for m in range(2):
                    P.pool(lambda e, a=Qp[ub][m]: e.memset(a, 0.0), [], [f"Qp{ub}"])
            cnts = {"s": 0, "e": 0}
            items = [(hh, qt, kc, m) for hh in range(4) for qt in range(4) for kc in range(18) for m in range(2)]
            deferred = []

            def s1(it):
                hh, qt, kc, m = it
                q0 = qt * 512
                qreads = [f"QT.{2 + 4 * qt + i_}" for i_ in range(4)]
                sb = cnts["s"] % 3
                cnts["s"] += 1
                eb = cnts["e"] % 4
                cnts["e"] += 1
                ub = (hh * 4 + qt) % 2
                if kc == 0 and m == 0:
                    for mm_ in range(2):
                        P.pool(copy_fn(Qp[ub][mm_][mm_ * 64:(mm_ + 1) * 64, :], QT_v[mm_ * 64:(mm_ + 1) * 64, hh, q0:q0 + 512]),
                               qreads, [f"Qp{ub}"])
                P.pe(mm_group([(ps[sb][:, :], KT_v[:, hh, kc * 128:(kc + 1) * 128], Qp[ub][m], True, True)]),
                     [f"KT.{kc}", f"Qp{ub}"], [f"ps{sb}"])
                P.act(act_fn(E[eb], ps[sb][:, :], AF.Exp, bias=zero_c, scale=0.125), [f"ps{sb}"], [f"E{eb}"])
                return eb

            def part_b(hh, qt):
                h = hg * 4 + hh
                q0 = qt * 512
                P.pe(mm_group([(ps[7][:, :], ones, osq, True, True)]), ["osq", "ones"], ["ps7"])
                P.act(act_fn(rs, ps[7][:, :], AF.Ln, bias=eps_c, scale=1.0 / 128), ["ps7"], ["r1"])
                P.act(act_fn(rs, rs, AF.Exp, bias=zero_c, scale=-0.5), ["r1"], ["r1"])
                P.dve(stt_fn(aT_v[:, h, LC + q0:LC + q0 + 512], t_[0], sgl, rs, ALU.mult, ALU.mult), ["t0", "sgl", "r1"], [f"aT.{h}.{1 + qt}"])

            def s2(it, eb):
                hh, qt, kc, m = it
                P.pe(mm_group([(ps[3 + m][:, :], Vb_v[:, kc, hh * 128:(hh + 1) * 128], E[eb], kc == 0, kc == 17),
                               (ps[5 + m][:, :], ones, E[eb], kc == 0, kc == 17)]),
                     [f"V.{kc}", f"E{eb}", "ones"], [f"ps{3 + m}", f"ps{5 + m}"])
                if kc == 17 and m == 1:
                    for mm_ in range(2):
                        P.dve(copy_fn(t_[mm_], ps[3 + mm_][:, :]), [f"ps{3 + mm_}"], [f"t{mm_}"])
                        P.dve(copy_fn(r_[mm_], ps[5 + mm_][:, :]), [f"ps{5 + mm_}"], [f"r{mm_}"])
                    for mm_ in range(2):
                        P.dve(recip_fn(r_[mm_], r_[mm_]), [f"r{mm_}"], [f"r{mm_}"])
                        P.dve(tt_fn(t_[mm_], t_[mm_], r_[mm_], ALU.mult), [f"t{mm_}", f"r{mm_}"], [f"t{mm_}"])
                    P.dve(stt_fn(t_[0], t_[1], neglam, t_[0], ALU.mult, ALU.add), ["t0", "t1", "neglam"], ["t0"])
                    P.pool(tt_fn(osq, t_[0], t_[0], ALU.mult), ["t0"], ["osq"])
                    deferred.append([30, hh, qt])

            def tick(i):
                for d_ in list(deferred):
                    d_[0] -= 1
                    if d_[0] <= 0:
                        deferred.remove(d_)
                        part_b(d_[1], d_[2])
            run_pipe(items, s1, s2, lag=2, tick=tick)
            for d_ in deferred:
                part_b(d_[1], d_[2])
            P.barrier()

        def wo_load():
            return next_slot()
        wout_phase(l, s, [1, 2, 3, 4], aT_v, lambda ft: [f"aT.{h}.{ft}" for h in range(8)], 8, wo_load)
        P.barrier()

    def mixer0(s):
        l = 0
        P.barrier()
        W = Arena(arena_t[:, :], AR_BYTES)
        W.reset(work_mark)
        hT_v = v3(W.alloc((8 * T,), BF16), T)
        wmark = W.mark()
        norm_to_h(l, 1, s, range(5), hT_v, W)
        spill_x(range(5))
        P.barrier()
        W.reset(wmark)
        aT_v = v3(W.alloc((8 * T,), BF16), T)
        Xa = Arena(arena_t[:, :], AR_BYTES)
        Xa.reset(x_mark)
        xlimit = x_mark + 8 * T * 4
        slot_b = (wl_use[0] + 3) % 2
        g0 = ws[slot_b][:, 8192:12288].bitcast(F32)[:, 0:G0N]
        P.dma("sp", dma_fn([(g0, gains0[:, :])]), [], ["g0"], key="g0")
        gmark = Xa.mark()
        w0r = w_in0.rearrange("(k p) n -> p k n", p=128)
        SC_MLA = 96.0 ** -0.5

        def rstd_small(dst, src, inv_n):
            P.act(act_fn(dst, src, AF.Ln, bias=eps_c, scale=inv_n), ["eps_c"], [])
            P.act(act_fn(dst, dst, AF.Exp, bias=zero_c, scale=-0.5), [], [])

        for hg in range(2):
            Xa.reset(gmark)
            QT_v = v3(Xa.alloc((4 * T,), BF16), T)
            KT_v = v3(Xa.alloc((4 * T,), BF16), T)
            Vb = Xa.alloc((18 * 512,), BF16)
            Vb_v = v3(Vb, 512)
            P.dve(lambda e, a=v3(Vb, 128)[:, :, 64:128]: e.memset(a, 1.0), [], ["Vones"])
            tmark = Xa.mark()
            tsets = []
            for _ in range(2):
                d_ = dict(ssq=Xa.alloc((4,), F32), rsl=Xa.alloc((2,), F32), lat_b=Xa.alloc((384,), BF16), krg=Xa.alloc((32,), F32),
                          krr=Xa.alloc((32,), F32), ra=Xa.alloc((64,), F32), rb=Xa.alloc((64,), F32), latT=Xa.alloc((384,), BF16),
                          sqv=Xa.alloc((512,), F32), qn=Xa.alloc((384,), F32), tq=Xa.alloc((128,), F32), qbb=Xa.alloc((384,), BF16),
                          kn=Xa.alloc((256,), F32), kbb=Xa.alloc((384,), BF16), ss4=Xa.alloc((4,), F32), rq4=Xa.alloc((4,), F32),
                          ss4k=Xa.alloc((4,), F32), rk4=Xa.alloc((4,), F32))
                d_["junk"] = d_["sqv"]
                tsets.append(d_)
            assert Xa.off <= xlimit, (Xa.off, xlimit)
            s_ = next_slot()
            win_v = v3(ws[s_][:, 0:3328], 416)
            wqb_v = v3(ws[s_][:, 3328:4096], 384)
            wkvb_v = ws[s_][:, 4096:4608]
            caps0, capsq, capsk, capsqb, capskb = [], [], [], [], []
            for tt in range(18):
                P.capture_begin()
                lat = tt >= 2
                ft = ft_of_tt(tt)
                tok = tt * 128
                pb = tt % 2
                par = pb
                D_ = tsets[par]
                junk, ssq, rsl, lat_b, krg, krr, ra, rb, latT = D_["junk"], D_["ssq"], D_["rsl"], D_["lat_b"], D_["krg"], D_["krr"], D_["ra"], D_["rb"], D_["latT"]
                sqv, qn, tq, qbb, kn, kbb, ss4, rq4, ss4k, rk4 = D_["sqv"], D_["qn"], D_["tq"], D_["qbb"], D_["kn"], D_["kbb"], D_["ss4"], D_["rq4"], D_["ss4k"], D_["rk4"]
                psq = ps[par]
                pskv = ps[2 + par]
                Pp = ps[pb]
                P.pe(mm_group([(Pp[:, 0:416], hT_v[:, k, tok:tok + 128], win_v[:, k, :], k == 0, k == 7) for k in range(8)]),
                     [f"ws{s_}"] + [H(k, ft) for k in range(8)], [f"ps{pb}"])
                P.act(act_fn(junk[:, 0:256], Pp[:, 0:256], AF.Square, bias=zero_c, accum_out=ssq[:, 0:1]), [f"ps{pb}"], [f"sqv_{par}", f"ssq0_{par}"])
                P.act(act_fn(junk[:, 0:128], Pp[:, 256:384], AF.Square, bias=zero_c, accum_out=ssq[:, 1:2]), [f"ps{pb}"], [f"sqv_{par}", f"ssq1_{par}"])
                P.act(act_fn(junk[:, 0:32], Pp[:, 384:416], AF.Square, bias=zero_c, accum_out=ssq[:, 2:3]), [f"ps{pb}"], [f"sqv_{par}", f"ssq2_{par}"])
                P.act(act_fn(rsl[:, 0:1], ssq[:, 0:1], AF.Ln, bias=eps_c, scale=1.0 / 256), [f"ssq0_{par}"], [f"rsl0_{par}"])
                P.act(act_fn(rsl[:, 1:2], ssq[:, 1:2], AF.Ln, bias=eps_c, scale=1.0 / 128), [f"ssq1_{par}"], [f"rsl1_{par}"])
                P.act(act_fn(rsl, rsl, AF.Exp, bias=zero_c, scale=-0.5), [f"rsl0_{par}", f"rsl1_{par}"], [f"rsl0_{par}", f"rsl1_{par}"])
                P.dve(stt_fn(lat_b[:, 0:256], Pp[:, 0:256], rsl[:, 0:1], g0[:, G0_QA:G0_QA + 256], ALU.mult, ALU.mult),
                      [f"ps{pb}", f"rsl0_{par}", "g0"], [f"latb0_{par}"])
                P.dve(stt_fn(lat_b[:, 256:384], Pp[:, 256:384], rsl[:, 1:2], g0[:, G0_KVA:G0_KVA + 128], ALU.mult, ALU.mult),
                      [f"ps{pb}", f"rsl1_{par}", "g0"], [f"latb1_{par}"])
                P.dve(tt_fn(krg, Pp[:, 384:416], g0[:, G0_KR:G0_KR + 32], ALU.mult), [f"ps{pb}", "g0"], [f"krg_{par}"])
                if lat:
                    j = tt - 2
                    cs = ropeA_v[:, j, 0:16]
                    sn = ropeA_v[:, j, 16:32]
                    P.dve(tt_fn(ra[:, 0:16], krg[:, 0:16], cs, ALU.mult), [f"krg_{par}", "ropeA"], [f"ra_{par}"])
                    P.dve(tt_fn(rb[:, 0:16], krg[:, 16:32], sn, ALU.mult), [f"krg_{par}", "ropeA"], [f"rb_{par}"])
                    P.dve(tt_fn(krr[:, 0:16], ra[:, 0:16], rb[:, 0:16], ALU.subtract), [f"ra_{par}", f"rb_{par}"], [f"krr0_{par}"])
                    P.dve(tt_fn(ra[:, 0:16], krg[:, 0:16], sn, ALU.mult), [f"krg_{par}", "ropeA"], [f"ra_{par}"])
                    P.dve(tt_fn(rb[:, 0:16], krg[:, 16:32], cs, ALU.mult), [f"krg_{par}", "ropeA"], [f"rb_{par}"])
                    P.dve(tt_fn(krr[:, 16:32], ra[:, 0:16], rb[:, 0:16], ALU.add), [f"ra_{par}", f"rb_{par}"], [f"krr1_{par}"])
                else:
                    P.dve(copy_fn(krr, krg), [f"krg_{par}"], [f"krr0_{par}", f"krr1_{par}"])
                P.pe(tr_group([(psb[4 + par][:, jb * 128:(jb + 1) * 128], lat_b[:, jb * 128:(jb + 1) * 128], ident) for jb in range(3)]),
                     [f"latb0_{par}", f"latb1_{par}", "ident"], [f"ps{4 + par}"])
                P.act(act_fn(latT, psb[4 + par][:, 0:384], AF.Identity, bias=zero_c), [f"ps{4 + par}"], [f"latT_{par}"])
                latT_v = v3(latT, 128)
                P.pe(mm_group([(psq[:, 0:384], latT_v[:, jb, :], wqb_v[:, jb, :], jb == 0, jb == 1) for jb in range(2)]),
                     [f"latT_{par}", f"ws{s_}"], [f"ps{par}"])
                P.pe(mm_group([(pskv[:, 0:512], latT_v[:, 2, :], wkvb_v, True, True)]), [f"latT_{par}", f"ws{s_}"], [f"ps{2 + par}"])
                caps0.append(P.capture_end())
                P.capture_begin()
                P.act(act_fn(qn, psq[:, 0:384], AF.Square, bias=zero_c), [f"ps{par}"], [f"qn_{par}"])
                P.dve(red_fn(ss4, v3(qn, 96)), [f"qn_{par}"], [f"ss4_{par}"])
                P.act(act_fn(rq4, ss4, AF.Ln, bias=eps_c, scale=1.0 / 96), [f"ss4_{par}"], [f"rq4_{par}"])
                P.act(act_fn(rq4, rq4, AF.Exp, bias=zero_c, scale=-0.5), [f"rq4_{par}"], [f"rq4_{par}"])
                q3 = v3(qn, 96)
                P.dve(tt_fn(q3, v3(psq[:, 0:384], 96), rq4.unsqueeze(2).broadcast_to([128, 4, 96]), ALU.mult), [f"ps{par}", f"rq4_{par}"], [f"qn_{par}"])
                gq3 = v3(g0[:, G0_Q4:G0_Q4 + 384], 96)
                qb3 = v3(qbb, 96)
                if lat:
                    P.dve(tt_fn(qb3[:, :, 0:64], q3[:, :, 0:64], gq3[:, :, 0:64], ALU.mult), [f"qn_{par}", "g0"], [f"qbb0_{par}"])
                    tq3 = v3(tq, 32)
                    P.dve(tt_fn(tq3, q3[:, :, 64:96], gq3[:, :, 64:96], ALU.mult), [f"qn_{par}", "g0"], [f"tq_{par}"])
                    cs4 = ropeA_v[:, j, 0:16].unsqueeze(1).broadcast_to([128, 4, 16])
                    sn4 = ropeA_v[:, j, 16:32].unsqueeze(1).broadcast_to([128, 4, 16])
                    ra3 = v3(ra, 16)
                    rb3 = v3(rb, 16)
                    P.dve(tt_fn(ra3, tq3[:, :, 0:16], cs4, ALU.mult), [f"tq_{par}", "ropeA"], [f"ra_{par}"])
                    P.dve(tt_fn(rb3, tq3[:, :, 16:32], sn4, ALU.mult), [f"tq_{par}", "ropeA"], [f"rb_{par}"])
                    P.dve(tt_fn(qb3[:, :, 64:80], ra3, rb3, ALU.subtract), [f"ra_{par}", f"rb_{par}"], [f"qbb1_{par}"])
                    P.dve(tt_fn(ra3, tq3[:, :, 0:16], sn4, ALU.mult), [f"tq_{par}", "ropeA"], [f"ra_{par}"])
                    P.dve(tt_fn(rb3, tq3[:, :, 16:32], cs4, ALU.mult), [f"tq_{par}", "ropeA"], [f"rb_{par}"])
                    P.dve(tt_fn(qb3[:, :, 80:96], ra3, rb3, ALU.add), [f"ra_{par}", f"rb_{par}"], [f"qbb2_{par}"])
                else:
                    P.dve(tt_fn(qb3, q3, gq3, ALU.mult), [f"qn_{par}", "g0"], [f"qbb0_{par}", f"qbb1_{par}", f"qbb2_{par}"])
                capsq.append(P.capture_end())
                P.capture_begin()
                P.pe(tr_group([(psb[6 + par][0:96, hh * 128:(hh + 1) * 128], qbb[:, hh * 96:(hh + 1) * 96], ident) for hh in range(4)]),
                     [f"qbb0_{par}", f"qbb1_{par}", f"qbb2_{par}", "ident"], [f"ps{6 + par}"])
                P.act(act_fn(QT_v[0:96, :, tok:tok + 128], v3(psb[6 + par][0:96, 0:512], 128), AF.Identity, bias=zero_c[0:96, :]), [f"ps{6 + par}"], [f"QT.{tt}"])
                capsqb.append(P.capture_end())
                P.capture_begin()
                kv3 = v3(pskv[:, 0:512], 128)
                P.act(act_fn(sqv, pskv[:, 0:512], AF.Square, bias=zero_c), [f"ps{2 + par}"], [f"sqv_{par}"])
                P.dve(red_fn(ss4k, v3(sqv, 128)[:, :, 0:64]), [f"sqv_{par}"], [f"ss4k_{par}"])
                P.dve(ts_fn(ss4k, ss4k, ssq[:, 2:3], ALU.add), [f"ss4k_{par}", f"ssq2_{par}"], [f"ss4k_{par}"])
                P.act(act_fn(rk4, ss4k, AF.Ln, bias=eps_c, scale=1.0 / 96), [f"ss4k_{par}"], [f"rk4_{par}"])
                P.act(act_fn(rk4, rk4, AF.Exp, bias=zero_c, scale=-0.5), [f"rk4_{par}"], [f"rk4_{par}"])
                kn3 = v3(kn, 64)
                kb3 = v3(kbb, 96)
                P.dve(tt_fn(kn3, kv3[:, :, 0:64], rk4.unsqueeze(2).broadcast_to([128, 4, 64]), ALU.mult), [f"ps{2 + par}", f"rk4_{par}"], [f"kn_{par}"])
                P.dve(tt_fn(kb3[:, :, 0:64], kn3, v3(g0[:, G0_KN4:G0_KN4 + 256], 64), ALU.mult), [f"kn_{par}", "g0"], [f"kbb0_{par}"])
                P.dve(tt_fn(kb3[:, :, 64:96], krr.unsqueeze(1).broadcast_to([128, 4, 32]), rk4.unsqueeze(2).broadcast_to([128, 4, 32]), ALU.mult),
                      [f"krr0_{par}", f"krr1_{par}", f"rk4_{par}"], [f"kbb1_{par}"])
                P.act(act_fn(v3(Vb_v[:, tt, :], 128)[:, :, 0:64], kv3[:, :, 64:128], AF.Identity, bias=zero_c), [f"ps{2 + par}"], [f"V.{tt}"])
                capsk.append(P.capture_end())
                P.capture_begin()
                P.pe(tr_group([(psb[4 + par][0:96, 512 + hh * 128:512 + (hh + 1) * 128], kbb[:, hh * 96:(hh + 1) * 96], ident) for hh in range(4)]),
                     [f"kbb0_{par}", f"kbb1_{par}", "ident"], [f"ps{4 + par}"])
                P.act(act_fn(KT_v[0:96, :, tok:tok + 128], v3(psb[4 + par][0:96, 512:1024], 128), AF.Identity, bias=zero_c[0:96, :]), [f"ps{4 + par}"], [f"KT.{tt}"])
                capskb.append(P.capture_end())
            P.replay_zip(caps0[0:2])
            for i_ in range(0, 18, 2):
                P.replay_zip(capsq[i_:i_ + 2] + capsk[i_:i_ + 2])
                if i_ + 2 < 18:
                    P.replay_zip(caps0[i_ + 2:i_ + 4])
                P.replay_zip(capsqb[i_:i_ + 2] + capskb[i_:i_ + 2])
            P.barrier()
            Xa.reset(tmark)
            E = [Xa.alloc((512,), BF16) for _ in range(4)]
            r_ = [Xa.alloc((512,), F32) for _ in range(2)]
            assert Xa.off <= xlimit, (Xa.off, xlimit)
            cnts = {"s": 0, "e": 0}
            items = []
            uidx = 0
            for hh in range(4):
                for ft in range(5):
                    kcs = [0, 1] if ft == 0 else list(range(18))
                    for ki, kc in enumerate(kcs):
                        items.append((hh, ft, kc, ki == 0, ki == len(kcs) - 1, uidx))
                    uidx += 1

            def s1(it):
                hh, ft, kc, first, last, u = it
                t0, n = FT[ft]
                qreads = [f"QT.{tt}" for tt in range(t0 // 128, (t0 + n) // 128)]
                sb = cnts["s"] % 3
                cnts["s"] += 1
                eb = cnts["e"] % 4
                cnts["e"] += 1
                P.pe(mm_group([(ps[sb][:, :n], KT_v[0:96, hh, kc * 128:(kc + 1) * 128], QT_v[0:96, hh, t0:t0 + n], True, True)]),
                     [f"KT.{kc}"] + qreads, [f"ps{sb}"])
                P.act(act_fn(E[eb][:, :n], ps[sb][:, :n], AF.Exp, bias=zero_c, scale=SC_MLA), [f"ps{sb}"], [f"E{eb}"])
                return eb

            def s2(it, eb):
                hh, ft, kc, first, last, u = it
                h = hg * 4 + hh
                t0, n = FT[ft]
                ob = 3 + (u % 4)
                rb_ = u % 2
                P.pe(mm_group([(ps[ob][:, :n], Vb_v[:, kc, hh * 128:(hh + 1) * 128], E[eb][:, :n], first, last)]),
                     [f"V.{kc}", "Vones", f"E{eb}"], [f"ps{ob}"])
                if last:
                    P.dve(recip_fn(r_[rb_][0:64, :n], ps[ob][64:128, :n]), [f"ps{ob}"], [f"r{rb_}"])
                    po = (h % 2) * 64
                    P.dve(tt_fn(aT_v[po:po + 64, h // 2, t0:t0 + n], ps[ob][0:64, :n], r_[rb_][0:64, :n], ALU.mult),
                          [f"ps{ob}", f"r{rb_}"], [f"aT.{h // 2}.{ft}.{h % 2}"])
            run_pipe(items, s1, s2, lag=2)
            P.barrier()

        Xa.reset(gmark)
        sqTp = [Xa.alloc((4 * T,), BF16) for _ in range(2)]
        sqT_v = [v3(a_, 512) for a_ in sqTp]
        skT = Xa.alloc((T,), BF16)
        sV = Xa.alloc((18 * 256,), BF16)
        sV_v = v3(sV, 256)
        P.pool(lambda e, a=sqTp[0]: e.memset(a, 0.0), [], ["sqz0"])
        P.pool(lambda e, a=sqTp[1]: e.memset(a, 0.0), [], ["sqz1"])
        P.dve(lambda e, a=v3(sV, 128)[:, :, 64:128]: e.memset(a, 1.0), [], ["Vones"])
        se = Xa.alloc((1024,), F32)
        sk8 = Xa.alloc((8,), F32)
        tmark = Xa.mark()
        tsets = []
        for _ in range(2):
            tsets.append(dict(qn=Xa.alloc((512,), F32), ra=Xa.alloc((256,), F32), rb=Xa.alloc((256,), F32),
                              sqb=Xa.alloc((512,), BF16), kqn=Xa.alloc((128,), F32), ksq=Xa.alloc((128,), F32), skb=Xa.alloc((128,), BF16),
                              rak=Xa.alloc((64,), F32), rbk=Xa.alloc((64,), F32),
                              ss8=Xa.alloc((8,), F32), rs8=Xa.alloc((8,), F32), ss2=Xa.alloc((2,), F32), rs2=Xa.alloc((2,), F32)))
        assert Xa.off <= xlimit, (Xa.off, xlimit)
        P.act(act_fn(sk8, g0[:, G0_SINK:G0_SINK + 8], AF.Exp, bias=zero_c), ["g0"], ["sk8"])
        P.dve(copy_fn(v3(se, 128), sk8.unsqueeze(2).broadcast_to([128, 8, 128])), ["sk8"], ["se"])
        s_ = next_slot()
        wsv = v3(ws[s_][:, 0:6144], 768)
        cpe, cfq, cfk, cbq, cbk = [], [], [], [], []
        for tt in range(18):
            lat = tt >= 2
            ft = ft_of_tt(tt)
            tok = tt * 128
            pb = tt % 2
            par = pb
            ts_ = tsets[par]
            qn, ra, rb, sqb, kqn, ksq, skb, rak, rbk = (ts_["qn"], ts_["ra"], ts_["rb"], ts_["sqb"], ts_["kqn"], ts_["ksq"], ts_["skb"],
                                                         ts_["rak"], ts_["rbk"])
            ss8, rs8, ss2, rs2 = ts_["ss8"], ts_["rs8"], ts_["ss2"], ts_["rs2"]
            hreads = [H(k, ft) for k in range(8)]
            if lat:
                j = tt - 2
            P.capture_begin()
            P.pe(mm_group([(ps[pb][:, :], hT_v[:, k, tok:tok + 128], wsv[:, k, 0:512], k == 0, k == 7) for k in range(8)]),
                 [f"ws{s_}"] + hreads, [f"ps{pb}"])
            P.pe(mm_group([(ps[2 + pb][:, 0:256], hT_v[:, k, tok:tok + 128], wsv[:, k, 512:768], k == 0, k == 7) for k in range(8)]),
                 [f"ws{s_}"] + hreads, [f"ps{2 + pb}"])
            P.act(act_fn(v3(sV_v[:, tt, :], 128)[:, :, 0:64], v3(ps[2 + pb][:, 128:256], 64), AF.Identity, bias=zero_c), [f"ps{2 + pb}"], [f"V.{tt}"])
            cpe.append(P.capture_end())
            P.capture_begin()
            q3 = v3(qn, 64)
            P.act(act_fn(qn, ps[pb][:, :], AF.Square, bias=zero_c), [f"ps{pb}"], [f"qn_{par}"])
            P.dve(red_fn(ss8, q3), [f"qn_{par}"], [f"ss8_{par}"])
            P.act(act_fn(rs8, ss8, AF.Ln, bias=eps_c, scale=1.0 / 64), [f"ss8_{par}"], [f"rs8_{par}"])
            P.act(act_fn(rs8, rs8, AF.Exp, bias=zero_c, scale=-0.5), [f"rs8_{par}"], [f"rs8_{par}"])
            P.dve(tt_fn(q3, v3(ps[pb][:, :], 64), rs8.unsqueeze(2).broadcast_to([128, 8, 64]), ALU.mult), [f"ps{pb}", f"rs8_{par}"], [f"qn_{par}"])
            gq = g0[:, G0_SQ8:G0_SQ8 + 512]
            qb4 = sqb.rearrange("p (pp hf d) -> p hf pp d", pp=4, hf=2)
            if lat:
                cosb = ropeB_v[:, j, 0:32].unsqueeze(1).broadcast_to([128, 8, 32])
                sinb = ropeB_v[:, j, 32:64].unsqueeze(1).broadcast_to([128, 8, 32])
                P.dve(tt_fn(qn, qn, gq, ALU.mult), [f"qn_{par}", "g0"], [f"qn_{par}"])
                ra3 = v3(ra, 32)
                rb3 = v3(rb, 32)
                ra4 = ra.rearrange("p (hf pp d) -> p hf pp d", hf=2, pp=4)
                rb4 = rb.rearrange("p (hf pp d) -> p hf pp d", hf=2, pp=4)
                P.dve(tt_fn(ra3, q3[:, :, 0:32], cosb, ALU.mult), [f"qn_{par}", "ropeB"], [f"ra_{par}"])
                P.dve(tt_fn(rb3, q3[:, :, 32:64], sinb, ALU.mult), [f"qn_{par}", "ropeB"], [f"rb_{par}"])
                P.dve(tt_fn(qb4[:, :, :, 0:32], ra4, rb4, ALU.subtract), [f"ra_{par}", f"rb_{par}"], [f"sqb0_{par}"])
                P.dve(tt_fn(ra3, q3[:, :, 0:32], sinb, ALU.mult), [f"qn_{par}", "ropeB"], [f"ra_{par}"])
                P.dve(tt_fn(rb3, q3[:, :, 32:64], cosb, ALU.mult), [f"qn_{par}", "ropeB"], [f"rb_{par}"])
                P.dve(tt_fn(qb4[:, :, :, 32:64], ra4, rb4, ALU.add), [f"ra_{par}", f"rb_{par}"], [f"sqb1_{par}"])
            else:
                qn4 = qn.rearrange("p (hf pp d) -> p hf pp d", hf=2, pp=4)
                gq4 = gq.rearrange("p (hf pp d) -> p hf pp d", hf=2, pp=4)
                P.dve(tt_fn(qb4, qn4, gq4, ALU.mult), [f"qn_{par}", "g0"], [f"sqb0_{par}", f"sqb1_{par}"])
            cfq.append(P.capture_end())
            P.capture_begin()
            P.pe(tr_group([(psb[4 + par][:, pp * 128:(pp + 1) * 128], sqb[:, pp * 128:(pp + 1) * 128], ident) for pp in range(4)]),
                 [f"sqb0_{par}", f"sqb1_{par}", "ident"], [f"ps{4 + par}"])
            P.act(act_fn(sqT_v[0][0:64, tt, :], psb[4 + par][0:64, 0:512], AF.Identity, bias=zero_c[0:64, :]), [f"ps{4 + par}", "sqz0"], [f"QT.{tt}.0"])
            P.act(act_fn(sqT_v[1][64:128, tt, :], psb[4 + par][64:128, 0:512], AF.Identity, bias=zero_c[64:128, :]), [f"ps{4 + par}", "sqz1"], [f"QT.{tt}.1"])
            cbq.append(P.capture_end())
            P.capture_begin()
            P.act(act_fn(ksq, ps[2 + pb][:, 0:128], AF.Square, bias=zero_c), [f"ps{2 + pb}"], [f"ksq_{par}"])
            P.dve(red_fn(ss2, v3(ksq, 64)), [f"ksq_{par}"], [f"ss2_{par}"])
            P.act(act_fn(rs2, ss2, AF.Ln, bias=eps_c, scale=1.0 / 64), [f"ss2_{par}"], [f"rs2_{par}"])
            P.act(act_fn(rs2, rs2, AF.Exp, bias=zero_c, scale=-0.5), [f"rs2_{par}"], [f"rs2_{par}"])
            k3 = v3(kqn, 64)
            P.dve(tt_fn(k3, v3(ps[2 + pb][:, 0:128], 64), rs2.unsqueeze(2).broadcast_to([128, 2, 64]), ALU.mult), [f"ps{2 + pb}", f"rs2_{par}"], [f"kqn_{par}"])
            gk = g0[:, G0_SK2:G0_SK2 + 128]
            kb3 = v3(skb, 64)
            if lat:
                cos2 = ropeB_v[:, j, 0:32].unsqueeze(1).broadcast_to([128, 2, 32])
                sin2 = ropeB_v[:, j, 32:64].unsqueeze(1).broadcast_to([128, 2, 32])
                P.dve(tt_fn(kqn, kqn, gk, ALU.mult), [f"kqn_{par}", "g0"], [f"kqn_{par}"])
                ra2 = v3(rak, 32)
                rb2 = v3(rbk, 32)
                P.dve(tt_fn(ra2, k3[:, :, 0:32], cos2, ALU.mult), [f"kqn_{par}", "ropeB"], [f"rak_{par}"])
                P.dve(tt_fn(rb2, k3[:, :, 32:64], sin2, ALU.mult), [f"kqn_{par}", "ropeB"], [f"rbk_{par}"])
                P.dve(tt_fn(kb3[:, :, 0:32], ra2, rb2, ALU.subtract), [f"rak_{par}", f"rbk_{par}"], [f"skb0_{par}"])
                P.dve(tt_fn(ra2, k3[:, :, 0:32], sin2, ALU.mult), [f"kqn_{par}", "ropeB"], [f"rak_{par}"])
                P.dve(tt_fn(rb2, k3[:, :, 32:64], cos2, ALU.mult), [f"kqn_{par}", "ropeB"], [f"rbk_{par}"])
                P.dve(tt_fn(kb3[:, :, 32:64], ra2, rb2, ALU.add), [f"rak_{par}", f"rbk_{par}"], [f"skb1_{par}"])
            else:
                P.dve(tt_fn(skb, kqn, gk, ALU.mult), [f"kqn_{par}", "g0"], [f"skb0_{par}", f"skb1_{par}"])
            cfk.append(P.capture_end())
            P.capture_begin()
            P.pe(tr_group([(psb[4 + par][:, 512:640], skb, ident)]), [f"skb0_{par}", f"skb1_{par}", "ident"], [f"ps{4 + par}"])
            P.act(act_fn(skT[:, tok:tok + 128], psb[4 + par][:, 512:640], AF.Identity, bias=zero_c), [f"ps{4 + par}"], [f"KT.{tt}"])
            cbk.append(P.capture_end())
        P.replay_zip(cpe[0:2])
        for i_ in range(0, 18, 2):
            P.replay_zip(cfq[i_:i_ + 2] + cfk[i_:i_ + 2])
            if i_ + 2 < 18:
                P.replay_zip(cpe[i_ + 2:i_ + 4])
            P.replay_zip(cbq[i_:i_ + 2] + cbk[i_:i_ + 2])
        P.barrier()
        Xa.reset(tmark)
        E = [Xa.alloc((512,), BF16) for _ in range(4)]
        r_ = [Xa.alloc((512,), F32) for _ in range(2)]
        assert Xa.off <= xlimit, (Xa.off, xlimit)
        se_v = v3(se, 512)
        cnts = {"s": 0, "e": 0}
        items = []
        uidx = 0
        for g in range(2):
            for tt in range(18):
                if tt < 2:
                    kcs = [(0, None), (1, None)]
                else:
                    n_ = tt - 2
                    kcs = [(0, None), (1, None)]
                    if n_ > 0:
                        kcs.append((tt - 1, maskp))
                    kcs.append((tt, None))
                    if n_ < 15:
                        kcs.append((tt + 1, maskn))
                for ki, (kc, msk) in enumerate(kcs):
                    items.append((g, tt, kc, msk, ki == 0, ki == len(kcs) - 1, uidx))
                uidx += 1

        def s1(it):
            g, tt, kc, msk, first, last, u = it
            gp = slice(g * 64, (g + 1) * 64)
            sb = cnts["s"] % 3
            cnts["s"] += 1
            eb = cnts["e"] % 4
            cnts["e"] += 1
            P.pe(mm_group([(ps[sb][:, :], skT[:, kc * 128:(kc + 1) * 128], sqT_v[g][:, tt, :], True, True)]),
                 [f"KT.{kc}", f"QT.{tt}.{g}", f"sqz{g}"], [f"ps{sb}"])
            P.act(act_fn(E[eb], ps[sb][:, :], AF.Exp, bias=zero_c, scale=0.125), [f"ps{sb}"], [f"E{eb}"])
            if msk is not None:
                P.pool(tt_fn(E[eb], E[eb], msk, ALU.mult), [f"E{eb}", "maskp", "maskn"], [f"E{eb}"])
            return eb

        def s2(it, eb):
            g, tt, kc, msk, first, last, u = it
            gp = slice(g * 64, (g + 1) * 64)
            tok = tt * 128
            ft = ft_of_tt(tt)
            ob = 3 + (u % 4)
            rb_ = u % 2
            P.pe(mm_group([(ps[ob][:, :], sV_v[:, kc, g * 128:(g + 1) * 128], E[eb], first, last)]),
                 [f"V.{kc}", "Vones", f"E{eb}"], [f"ps{ob}"])
            if last:
                P.dve(tt_fn(r_[rb_][0:64, :], ps[ob][64:128, :], se_v[64:128, g, :], ALU.add), [f"ps{ob}", "se"], [f"r{rb_}"])
                P.dve(recip_fn(r_[rb_][0:64, :], r_[rb_][0:64, :]), [f"r{rb_}"], [f"r{rb_}"])
                P.dve(tt_fn(aT_v[gp, 4:8, tok:tok + 128], v3(ps[ob][0:64, :], 128), v3(r_[rb_][0:64, :], 128), ALU.mult),
                      [f"ps{ob}", f"r{rb_}"], [f"aT.{4 + pp}.{ft}.{g}.{tt}" for pp in range(4)])
        run_pipe(items, s1, s2, lag=2)
        P.barrier()

        def wo_load():
            return next_slot()

        def a_names(ft):
            t0, n = FT[ft]
            nm = [f"aT.{c}.{ft}.{hp}" for c in range(4) for hp in range(2)]
            nm += [f"aT.{4 + pp}.{ft}.{g}.{tt}" for pp in range(4) for g in range(2) for tt in range(t0 // 128, (t0 + n) // 128)]
            return nm
        wout_phase(l, s, [0, 1, 2, 3, 4], aT_v, a_names, 8, wo_load)
        P.barrier()

    stages = ["ffn00", "mix0", "ffn01", "ffn10", "mix1", "ffn11"]
    n_stage = len(stages) if stop_after is None else stages.index(stop_after) + 1
    for s in range(nseq):
        for si in range(n_stage):
            st = stages[si]
            if st.startswith("ffn"):
                WL.extend(wl_ffn(int(st[3]), int(st[4]), g) for g in range(6))
            elif st == "mix0":
                WL.extend([wl_mla(0), wl_mla(1), wl_swa(), wl_wout0()])
            else:
                WL.extend([wl_diff(0), wl_diff(1), wl_wout1()])
    for s in range(nseq):
        if s > 0:
            load_x(s)
        for si in range(n_stage):
            st = stages[si]
            if st == "ffn00":
                ffn(0, 0, s, [0, 1, 2, 3, 4])
            elif st == "mix0":
                mixer0(s)
            elif st == "ffn01":
                ffn(0, 1, s, [0, 1, 2, 3, 4])
            elif st == "ffn10":
                ffn(1, 0, s, [0, 1, 2, 3, 4])
            elif st == "mix1":
                mixer1(s)
            elif st == "ffn11":
                ffn(1, 1, s, [1, 2, 3, 4])
            P.barrier()
        P.dma("sp", dma_fn([(outT[s].rearrange("(k p) t -> p k t", p=128), xT_v[:, :, LC:T])]),
              [X(k, ft) for k in range(8) for ft in range(1, 5)], [f"out{s}"], key=f"out{s}")
    P._add("sp", None, [f"out{s}" for s in range(nseq)], [])

    blk = ctx.enter_context(nc.Block())
    semstack = P.finalize_and_emit(blk)
    ctx.enter_context(semstack)
    ctx.close()
    nc._prog_stats = {e: len(P.per_eng[e]) for e in ENGINES}
    nc._n_sems = P.n_sems
    return nc


def _rope_tables():
    pos = np.arange(S)
    row = (pos // 64).astype(np.float32)
    col = (pos % 64).astype(np.float32)

    def tab(rot_dim):
        n_f = rot_dim // 4
        inv = (np.float32(10000.0) ** (-np.arange(n_f, dtype=np.float32) / np.float32(n_f))).astype(np.float32)
        ang = np.concatenate([row[:, None] * inv[None, :], col[:, None] * inv[None, :]], axis=-1).astype(np.float32)
        return np.cos(ang).astype(np.float32), np.sin(ang).astype(np.float32)
    ca, sa = tab(32)
    cb, sb = tab(64)
    A_ = np.concatenate([ca, sa], axis=-1).reshape(16, 128, 32).transpose(1, 0, 2).reshape(128, 512)
    B_ = np.concatenate([cb, sb], axis=-1).reshape(16, 128, 64).transpose(1, 0, 2).reshape(128, 1024)
    return np.ascontiguousarray(np.concatenate([A_, B_], axis=1), dtype=np.float32)


def _const_b():
    ident = np.eye(128, dtype=np.float32)
    ones = np.ones((128, 128), dtype=np.float32)
    a = np.arange(128)[:, None]
    b = np.arange(128)[None, :]
    mp = (b <= a).astype(np.float32)
    mn = (a <= b).astype(np.float32)
    return np.ascontiguousarray(np.concatenate([ident, ones, np.tile(mp, (1, 4)), np.tile(mn, (1, 4))], axis=1), dtype=np.float32)


def _rep(v, times=1):
    v = np.asarray(v, dtype=np.float32).reshape(-1)
    return np.tile(np.tile(v, times)[None, :], (128, 1))


_CACHE = {}


def make_in_maps(inp):
    f = lambda a: np.ascontiguousarray(np.asarray(a, dtype=np.float32))
    shared = {}
    for l in range(2):
        p = f"l{l}_"
        shared[f"ada_w{l}"] = f(inp[p + "ada_w"])
        shared[f"ada_bT{l}"] = f(np.asarray(inp[p + "ada_b"]).reshape(72, 128).T)
        shared[f"norm_gT{l}"] = f(np.asarray(inp[p + "norm_g"]).reshape(3, 8, 128).transpose(2, 0, 1).reshape(128, 24))
        shared[f"wg{l}"] = f(inp[p + "ffn_wg"])
        shared[f"wu{l}"] = f(inp[p + "ffn_wu"])
        shared[f"wd{l}"] = f(inp[p + "ffn_wd"])
    shared["w_in0"] = f(inp["l0_w_in"])
    shared["wqb"] = f(inp["l0_mla_wqb"])
    shared["wkvb"] = f(inp["l0_mla_wkvb"])
    shared["w_out0"] = f(inp["l0_w_out"])
    shared["w_out1"] = f(inp["l1_w_out"])
    shared["w_in1"] = f(inp["l1_w_in"])
    kg = np.asarray(inp["l0_mla_k_g"], dtype=np.float32)
    shared["gains0"] = f(np.concatenate([
        _rep(inp["l0_mla_qa_g"]), _rep(inp["l0_mla_kva_g"]), _rep(inp["l0_mla_q_g"], 4), _rep(kg[:64], 4), _rep(kg[64:]),
        _rep(inp["l0_swa_q_g"], 8), _rep(inp["l0_swa_k_g"], 2), _rep(inp["l0_swa_sink"])], axis=1))
    assert shared["gains0"].shape == (128, G0N)
    shared["gains1"] = f(np.concatenate([_rep(inp["l1_q_g"]), _rep(inp["l1_k_g"]),
                                         np.asarray(inp["l1_subln_g"], dtype=np.float32).reshape(128, 1)], axis=1))
    shared["lam"] = f(np.stack([np.asarray(inp[k], dtype=np.float32) for k in
                                ("l1_lambda_q1", "l1_lambda_k1", "l1_lambda_q2", "l1_lambda_k2")], axis=1))
    shared["consts_f"] = _rope_tables()
    shared["consts_b"] = _const_b()
    x = np.asarray(inp["x"], dtype=np.float32)
    c = np.asarray(inp["c"], dtype=np.float32)
    cx = np.asarray(inp["ctx"], dtype=np.float32)
    cc = np.asarray(inp["c_ctx"], dtype=np.float32)
    maps = []
    for core in range(NCORES):
        b0 = 2 * core
        m = dict(shared)
        m["xT"] = np.ascontiguousarray(x[b0:b0 + 2].transpose(0, 2, 1))
        m["ctxT"] = np.ascontiguousarray(cx[b0:b0 + 2].transpose(0, 2, 1))
        m["c3"] = np.ascontiguousarray(np.stack([c[b0], c[b0 + 1], cc], axis=1))
        maps.append(m)
    return maps


def kernel(**inputs):
    if "nc" not in _CACHE:
        _CACHE["nc"] = build_program()
    nc = _CACHE["nc"]
    in_maps = make_in_maps(inputs)
    res = run_bass_kernel_spmd(nc, in_maps, core_ids=list(range(NCORES)))
    out = np.empty((2 * NCORES, S, D), dtype=np.float32)
    for core in range(NCORES):
        o = np.asarray(res.results[core]["outT"])
        out[2 * core:2 * core + 2] = o.transpose(0, 2, 1)
    return out
```

```python
import math
import numpy as np
import concourse.bass as bass
import concourse.mybir as mybir
from concourse.bass_utils import run_bass_kernel_spmd

F32 = mybir.dt.float32
BF16 = mybir.dt.bfloat16
AF = mybir.ActivationFunctionType
ALU = mybir.AluOpType
AX = mybir.AxisListType

D = 1024
S = 2048
LC = 256
T = S + LC
DFF = 2816
NJ = DFF // 128
EPS = 1e-6
NCORES = 8
LAMBDA_INIT1 = 0.8 - 0.6 * math.exp(-0.3 * 1)

ENGINES = ("pe", "act", "dve", "pool", "sp")
EPOCH = 30000


class Op:
    __slots__ = ("eng", "fn", "deps", "dma_key", "dma_cnt", "inc", "cnt", "waits", "idx", "n_dma")

    def __init__(self, eng, fn):
        self.eng = eng
        self.fn = fn
        self.deps = set()
        self.dma_key = None
        self.dma_cnt = 0
        self.inc = False
        self.cnt = 0
        self.waits = []
        self.n_dma = 1


class Prog:
    def __init__(self, nc):
        self.nc = nc
        self.ops = []
        self.per_eng = {e: [] for e in ENGINES}
        self.last_writer = {}
        self.readers = {}
        self.pending_bar = {e: set() for e in ENGINES}
        self.dma_counts = {}
        self.dma_since_bar = []

    def capture_begin(self):
        self.cap = []

    def capture_end(self):
        c = self.cap
        self.cap = None
        return c

    def replay_zip(self, lists):
        idx = [0] * len(lists)
        while True:
            alive = False
            for li, L in enumerate(lists):
                if idx[li] < len(L):
                    alive = True
                    self._add(*L[idx[li]])
                    idx[li] += 1
            if not alive:
                break

    def _add(self, eng, fn, reads, writes, dma_key=None, n_dma=1):
        if getattr(self, "cap", None) is not None:
            self.cap.append((eng, fn, list(reads), list(writes), dma_key, n_dma))
            return None
        op = Op(eng, fn)
        op.idx = len(self.ops)
        deps = op.deps
        for r in reads:
            w = self.last_writer.get(r)
            if w is not None:
                deps.add(w)
            self.readers.setdefault(r, []).append(op.idx)
        for w_ in writes:
            w = self.last_writer.get(w_)
            if w is not None:
                deps.add(w)
            rl = self.readers.get(w_)
            if rl:
                deps.update(rl)
            self.last_writer[w_] = op.idx
            self.readers[w_] = []
        if self.pending_bar[eng]:
            deps.update(self.pending_bar[eng])
            self.pending_bar[eng] = set()
        deps.discard(op.idx)
        if dma_key is not None:
            op.dma_key = dma_key
            op.n_dma = n_dma
            c = self.dma_counts.get(dma_key, 0) + n_dma
            self.dma_counts[dma_key] = c
            op.dma_cnt = c
            self.dma_since_bar.append(op.idx)
        self.ops.append(op)
        self.per_eng[eng].append(op)
        return op

    def pe(self, fn, reads, writes):
        return self._add("pe", fn, reads, writes)

    def act(self, fn, reads, writes):
        return self._add("act", fn, reads, writes)

    def dve(self, fn, reads, writes):
        return self._add("dve", fn, reads, writes)

    def pool(self, fn, reads, writes):
        return self._add("pool", fn, reads, writes)

    def dma(self, queue, fn, reads, writes, key, n_dma=1):
        return self._add(queue, fn, reads, writes, dma_key=key, n_dma=n_dma)

    def barrier(self):
        last = set()
        for e in ENGINES:
            for op in reversed(self.per_eng[e]):
                if op.dma_key is None:
                    last.add(op.idx)
                    break
        last.update(self.dma_since_bar)
        self.dma_since_bar = []
        for e in ENGINES:
            self.pending_bar[e] = set(last)

    def finalize_and_emit(self, block):
        nc = self.nc
        ops = self.ops
        for op in ops:
            for d in op.deps:
                dop = ops[d]
                if dop.dma_key is not None:
                    continue
                if dop.eng == "pe" and op.eng == "pe" and op.dma_key is None:
                    continue
                dop.inc = True
        counts = {e: 0 for e in ENGINES}
        for e in ENGINES:
            for op in self.per_eng[e]:
                if op.dma_key is None and op.inc:
                    counts[e] += 1
                    op.cnt = counts[e]
        import contextlib
        stack = contextlib.ExitStack()
        eng_sems = {}
        for e in ENGINES:
            n_ep = counts[e] // EPOCH + 1
            eng_sems[e] = [stack.enter_context(nc.semaphore(f"s_{e}{i}")) for i in range(n_ep)]
        dma_sems = {k: stack.enter_context(nc.semaphore(f"d_{k}")) for k in self.dma_counts}
        self.n_sems = sum(len(v) for v in eng_sems.values()) + len(dma_sems)
        for e in ENGINES:
            seen = {}
            for op in self.per_eng[e]:
                need = {}
                for d in op.deps:
                    dop = ops[d]
                    if dop.dma_key is not None:
                        key = ("d", dop.dma_key)
                        val = 16 * dop.dma_cnt
                    else:
                        if dop.eng == "pe" and e == "pe" and op.dma_key is None:
                            continue
                        c = dop.cnt - 1
                        key = ("e", dop.eng, c // EPOCH)
                        val = c % EPOCH + 1
                    if seen.get(key, 0) >= val:
                        continue
                    if need.get(key, 0) < val:
                        need[key] = val
                for key, val in need.items():
                    seen[key] = val
                    sem = dma_sems[key[1]] if key[0] == "d" else eng_sems[key[1]][key[2]]
                    op.waits.append((sem, val))

        def emit(engname, eng):
            for op in self.per_eng[engname]:
                for sem, val in op.waits:
                    eng.wait_ge(sem, val)
                if op.fn is None:
                    continue
                ins = op.fn(eng)
                if op.dma_key is not None:
                    assert len(ins) == op.n_dma, (len(ins), op.n_dma)
                    for i_ in ins:
                        i_.then_inc(dma_sems[op.dma_key], 16)
                elif op.inc:
                    c = op.cnt - 1
                    ins.then_inc(eng_sems[engname][c // EPOCH], 1)

        @block.tensor
        def _(eng):
            emit("pe", eng)

        @block.scalar
        def _(eng):
            emit("act", eng)

        @block.vector
        def _(eng):
            emit("dve", eng)

        @block.gpsimd
        def _(eng):
            emit("pool", eng)

        @block.sync
        def _(eng):
            emit("sp", eng)

        return stack


class Arena:
    def __init__(self, ap_f32, nbytes):
        self.ap = ap_f32
        self.nbytes = nbytes
        self.off = 0
        self.peak = 0

    def mark(self):
        return self.off

    def reset(self, m):
        self.off = m

    def alloc(self, shape_free, dtype):
        esz = 4 if dtype == F32 else 2
        n = int(np.prod(shape_free))
        nb = (n * esz + 31) // 32 * 32
        assert self.off + nb <= self.nbytes, f"arena overflow {self.off}+{nb}>{self.nbytes}"
        a = self.ap[:, self.off // 4:(self.off + nb) // 4]
        self.off += nb
        self.peak = max(self.peak, self.off)
        if dtype != F32:
            a = a.bitcast(dtype)
        a = a[:, 0:n]
        if len(shape_free) == 2:
            a = a.rearrange("p (a b) -> p a b", b=shape_free[1])
        elif len(shape_free) == 3:
            a = a.rearrange("p (a b c) -> p a b c", b=shape_free[1], c=shape_free[2])
        return a


def mm_group(args):
    def fn(e):
        ins = None
        for (o, l, r, st, sp) in args:
            ins = e.matmul(o, lhsT=l, rhs=r, start=st, stop=sp)
        return ins
    return fn


def tr_group(args):
    def fn(e):
        ins = None
        for (o, i, idn) in args:
            ins = e.transpose(o, i, idn)
        return ins
    return fn


def act_fn(out, in_, func, bias=None, scale=None, accum_out=None):
    def fn(e):
        kw = {}
        if bias is not None:
            kw["bias"] = bias
        if scale is not None:
            kw["scale"] = scale
        if accum_out is not None:
            kw["accum_out"] = accum_out
        return e.activation(out=out, in_=in_, func=func, **kw)
    return fn


def tt_fn(out, in0, in1, op):
    return lambda e: e.tensor_tensor(out=out, in0=in0, in1=in1, op=op)


def ts_fn(out, in0, s1, op0, s2=None, op1=None):
    if op1 is None:
        return lambda e: e.tensor_scalar(out=out, in0=in0, scalar1=s1, scalar2=None, op0=op0)
    return lambda e: e.tensor_scalar(out=out, in0=in0, scalar1=s1, scalar2=s2, op0=op0, op1=op1)


def stt_fn(out, in0, scalar, in1, op0, op1):
    return lambda e: e.scalar_tensor_tensor(out=out, in0=in0, scalar=scalar, in1=in1, op0=op0, op1=op1)


def copy_fn(out, in_):
    return lambda e: e.tensor_copy(out=out, in_=in_)


def red_fn(out, in_, op=None):
    return lambda e: e.tensor_reduce(out=out, in_=in_, axis=AX.X, op=(op or ALU.add))


def recip_fn(out, in_):
    return lambda e: e.reciprocal(out=out, in_=in_)


def run_pipe(items, s1, s2, lag=2, tick=None):
    toks = {}
    n = len(items)
    for i in range(n + lag):
        if i < n:
            toks[i] = s1(items[i])
        if i >= lag:
            s2(items[i - lag], toks.pop(i - lag))
        if tick is not None:
            tick(i)


def dma_fn(pairs):
    def fn(e):
        return [e.dma_start(out=o, in_=i) for (o, i) in pairs]
    return fn


FT = [(0, 256), (256, 512), (768, 512), (1280, 512), (1792, 512)]


def ft_of_tt(tt):
    return 0 if tt < 2 else 1 + (tt - 2) // 4


def v3(ap, b):
    return ap.rearrange("p (a b) -> p a b", b=b)


G0_QA, G0_KVA, G0_Q4, G0_KN4, G0_KR, G0_SQ8, G0_SK2, G0_SINK, G0N = 0, 256, 384, 768, 1024, 1056, 1568, 1696, 1704
G1N = 129
AR_BYTES = 210944


def build_program(stop_after=None, nseq=2):
    nc = bass.Bass("TRN2", target_bir_lowering=False)

    def din(name, shape):
        return nc.dram_tensor(name, list(shape), F32, kind="ExternalInput").ap()

    xT_in = din("xT", [2, D, S])
    ctxT_in = din("ctxT", [2, D, LC])
    c3_in = din("c3", [D, 3])
    ada_w = [din(f"ada_w{l}", [D, 9 * D]) for l in range(2)]
    ada_b = [din(f"ada_bT{l}", [128, 72]) for l in range(2)]
    norm_g = [din(f"norm_gT{l}", [128, 24]) for l in range(2)]
    wg = [din(f"wg{l}", [2, D, DFF]) for l in range(2)]
    wu = [din(f"wu{l}", [2, D, DFF]) for l in range(2)]
    wd = [din(f"wd{l}", [2, DFF, D]) for l in range(2)]
    w_in0 = din("w_in0", [D, 1184])
    wqb_in = din("wqb", [256, 768])
    wkvb_in = din("wkvb", [128, 1024])
    w_out = [din("w_out0", [D, D]), din("w_out1", [D, D])]
    gains0 = din("gains0", [128, G0N])
    w_in1 = din("w_in1", [D, 3072])
    gains1 = din("gains1", [128, G1N])
    lam_in = din("lam", [64, 4])
    consts_f = din("consts_f", [128, 1536])
    consts_b = din("consts_b", [128, 1280])
    outT = nc.dram_tensor("outT", [2, D, S], F32, kind="ExternalOutput").ap()
    xspill = nc.dram_tensor("xspill", [128, 8 * T], F32, kind="Internal").ap()

    import contextlib
    ctx = contextlib.ExitStack()
    arena_t = ctx.enter_context(nc.sbuf_tensor("arena", [128, AR_BYTES // 4], F32))
    ps = [ctx.enter_context(nc.psum_tensor(f"ps{i}", [128, 512], F32)) for i in range(8)]
    psb = [p_[:, :].bitcast(BF16) for p_ in ps]

    A = Arena(arena_t[:, :], AR_BYTES)
    ident = A.alloc((128,), BF16)
    ones = A.alloc((128,), BF16)
    maskp = A.alloc((512,), BF16)
    maskn = A.alloc((512,), BF16)
    onesf = A.alloc((128,), F32)
    ropeA = A.alloc((512,), F32)
    ropeB = A.alloc((1024,), F32)
    ropeA_v = v3(ropeA, 32)
    ropeB_v = v3(ropeB, 64)
    mod = [A.alloc((216,), F32) for _ in range(2)]
    mod_v = [v3(m_, 3) for m_ in mod]
    gs = [[v3(A.alloc((24,), F32), 3) for _ in range(3)] for _ in range(2)]
    hgt = [[v3(A.alloc((24,), F32), 3) for _ in range(3)] for _ in range(2)]
    adab = [A.alloc((72,), F32) for _ in range(2)]
    normg = [A.alloc((24,), F32) for _ in range(2)]
    c3s = A.alloc((24,), F32)
    sc = A.alloc((24,), F32)
    zero_c = A.alloc((1,), F32)
    eps_c = A.alloc((1,), F32)
    lamv = A.alloc((4,), F32)
    lamp = A.alloc((2,), F32)
    e12 = A.alloc((2,), F32)
    neglam = A.alloc((1,), F32)
    sgl = A.alloc((1,), F32)
    ws = [A.alloc((12288,), BF16) for _ in range(2)]
    x_mark = A.mark()
    xT = A.alloc((8 * T,), F32)
    xT_v = v3(xT, T)
    work_mark = A.mark()

    P = Prog(nc)
    WL = []
    wl_use = [0]
    wl_rec = [0]

    def next_slot():
        i = wl_use[0]
        wl_use[0] += 1
        while wl_rec[0] < len(WL) and wl_rec[0] <= i + 1:
            j = wl_rec[0]
            WL[j](j % 2)
            wl_rec[0] += 1
        return i % 2

    def X(k, ft):
        return f"x{k}.{ft}"

    def H(k, ft):
        return f"h{k}.{ft}"

    P.dma("sp", dma_fn([(ropeA, consts_f[:, 0:512]), (ropeB, consts_f[:, 512:1536])]), [], ["ropeA", "ropeB"], key="c0", n_dma=2)
    P.dma("pool", dma_fn([(ident, consts_b[:, 0:128]), (ones, consts_b[:, 128:256]),
                          (maskp, consts_b[:, 256:768]), (maskn, consts_b[:, 768:1280])]),
          [], ["ident", "ones", "maskp", "maskn"], key="c1", n_dma=4)
    P.dve(lambda e: e.memset(zero_c, 0.0), [], ["zero_c"])
    P.dve(lambda e: e.memset(eps_c, EPS), [], ["eps_c"])
    P.dve(lambda e: e.memset(onesf, 1.0), [], ["onesf"])
    P.dma("sp", dma_fn([(v3(c3s, 3), c3_in.rearrange("(k p) n -> p k n", p=128)),
                        (adab[0], ada_b[0][:, :]), (adab[1], ada_b[1][:, :]),
                        (normg[0], norm_g[0][:, :]), (normg[1], norm_g[1][:, :]),
                        (lamv[0:64, :], lam_in[:, :])]),
          [], ["c3s", "adab", "normg", "lamv"], key="c2", n_dma=6)
    scb = sc.bitcast(BF16)[:, 0:24]
    P.act(act_fn(scb, c3s, AF.Silu, bias=zero_c), ["c3s", "zero_c"], ["sc"])
    sc_v = v3(scb, 3)
    P.dve(tt_fn(lamp[0:64, 0:1], lamv[0:64, 0:1], lamv[0:64, 1:2], ALU.mult), ["lamv"], ["lamp0"])
    P.dve(tt_fn(lamp[0:64, 1:2], lamv[0:64, 2:3], lamv[0:64, 3:4], ALU.mult), ["lamv"], ["lamp1"])
    P.pe(mm_group([(ps[2][:, 0:2], onesf[0:64, :], lamp[0:64, :], True, True)]), ["lamp0", "lamp1", "onesf"], ["ps2"])
    P.act(act_fn(e12, ps[2][:, 0:2], AF.Exp, bias=zero_c), ["ps2", "zero_c"], ["e12"])
    P.dve(stt_fn(neglam, e12[:, 1:2], -LAMBDA_INIT1, e12[:, 0:1], ALU.add, ALU.subtract), ["e12"], ["neglam"])
    def load_x(s):
        allx = [X(k, ft) for k in range(8) for ft in range(5)]
        P.dma("sp", dma_fn([(xT_v[:, :, LC:T], xT_in[s].rearrange("(k p) t -> p k t", p=128)),
                            (xT_v[:, :, 0:LC], ctxT_in[s].rearrange("(k p) t -> p k t", p=128))]),
              [], allx, key="xload", n_dma=2)

    load_x(0)
    _wa = Arena(arena_t[:, :], AR_BYTES)
    _wa.reset(work_mark)
    adas = [v3(_wa.alloc((9216,), BF16), 1152) for _ in range(2)]
    for l in range(2):
        awr = ada_w[l].rearrange("(k p) n -> p k n", p=128)
        for blk in range(8):
            s_ = blk % 2
            P.dma("pool", dma_fn([(adas[s_], awr[:, :, blk * 1152:(blk + 1) * 1152])]), [], [f"adas{s_}"], key=f"adas{s_}")
            args = []
            for j in range(9):
                jj = blk * 9 + j
                for k in range(8):
                    args.append((ps[l][:, jj * 3:(jj + 1) * 3], adas[s_][:, k, j * 128:(j + 1) * 128], sc_v[:, k, :], k == 0, k == 7))
            P.pe(mm_group(args), [f"adas{s_}", "sc"], [f"ps{l}"])
        P.dve(tt_fn(mod_v[l], v3(ps[l][:, 0:216], 3), adab[l].unsqueeze(2).broadcast_to([128, 72, 3]), ALU.add),
              [f"ps{l}", "adab"], [f"mod{l}"])
        for i in range(3):
            for col in range(3):
                P.dve(stt_fn(gs[l][i][:, :, col], mod_v[l][:, (3 * i + 1) * 8:(3 * i + 2) * 8, col], 1.0,
                             normg[l][:, i * 8:(i + 1) * 8], ALU.add, ALU.mult), [f"mod{l}", "normg"], [f"gs{l}"])
            P.dve(ts_fn(hgt[l][i], mod_v[l][:, (3 * i + 2) * 8:(3 * i + 3) * 8, :], (1.0 if i == 1 else 0.5), ALU.mult),
                  [f"mod{l}"], [f"hg{l}"])
    P.barrier()

    def norm_to_h(l, i, s, tiles, hT_v, W):
        sq = [[W.alloc((512,), BF16) for _ in range(2)] for _ in range(2)]
        lnv = [W.alloc((512,), F32) for _ in range(2)]
        rstd = [W.alloc((512,), F32) for _ in range(2)]
        tn = [[W.alloc((512,), F32) for _ in range(2)] for _ in range(2)]
        caps = []
        for ti, ft in enumerate(tiles):
            P.capture_begin()
            par = ti % 2
            pbank = ps[6 + par]
            t0, n = FT[ft]
            col = 2 if ft == 0 else s
            for k in range(8):
                b = k % 2
                if k % 2 == 0:
                    P.act(act_fn(sq[par][b][:, :n], xT_v[:, k, t0:t0 + n], AF.Square, bias=zero_c), [X(k, ft)], [f"nsq{par}{b}"])
                else:
                    P.dve(tt_fn(sq[par][b][:, :n], xT_v[:, k, t0:t0 + n], xT_v[:, k, t0:t0 + n], ALU.mult), [X(k, ft)], [f"nsq{par}{b}"])
                P.pe(mm_group([(pbank[:, :n], ones, sq[par][b][:, :n], k == 0, k == 7)]), [f"nsq{par}{b}", "ones"], [f"ps{6 + par}"])
            P.act(act_fn(lnv[par][:, :n], pbank[:, :n], AF.Ln, bias=eps_c, scale=1.0 / D), [f"ps{6 + par}", "eps_c"], [f"nlnv{par}"])
            P.act(act_fn(rstd[par][:, :n], lnv[par][:, :n], AF.Exp, bias=zero_c, scale=-0.5), [f"nlnv{par}"], [f"nrstd{par}"])
            for k in range(8):
                b = k % 2
                P.dve(tt_fn(tn[par][b][:, :n], xT_v[:, k, t0:t0 + n], rstd[par][:, :n], ALU.mult), [X(k, ft), f"nrstd{par}"], [f"ntn{par}{b}"])
                P.act(act_fn(hT_v[:, k, t0:t0 + n], tn[par][b][:, :n], AF.Identity,
                             bias=mod_v[l][:, 3 * i * 8 + k, col:col + 1], scale=gs[l][i][:, k, col:col + 1]),
                      [f"ntn{par}{b}", f"mod{l}", f"gs{l}"], [H(k, ft)])
            caps.append(P.capture_end())
        for i_ in range(0, len(caps), 2):
            P.replay_zip(caps[i_:i_ + 2])

    FFN_GROUPS = [(0, 4), (4, 4), (8, 4), (12, 4), (16, 4), (20, 2)]

    def wl_ffn(l, f, g):
        def rec(s_):
            j0, ncn = FFN_GROUPS[g]
            wgr = wg[l][f].rearrange("(k p) n -> p k n", p=128)
            wur = wu[l][f].rearrange("(k p) n -> p k n", p=128)
            wdr = wd[l][f].rearrange("(j p) n -> p j n", p=128)
            wgs = v3(ws[s_][:, 0:4096], 512)
            wus = v3(ws[s_][:, 4096:8192], 512)
            wds = v3(ws[s_][:, 8192:12288], 1024)
            P.dma("pool", dma_fn([(wgs[:, :, 0:ncn * 128], wgr[:, :, j0 * 128:(j0 + ncn) * 128]),
                                  (wus[:, :, 0:ncn * 128], wur[:, :, j0 * 128:(j0 + ncn) * 128]),
                                  (wds[:, 0:ncn, :], wdr[:, j0:j0 + ncn, :])]),
                  [], [f"ws{s_}"], key=f"ws{s_}", n_dma=3)
        return rec

    def wl_diff(hg):
        def rec(s_):
            w1r = w_in1.rearrange("(k p) n -> p k n", p=128)
            wv = v3(ws[s_][:, 0:12288], 1536)
            P.dma("pool", dma_fn([(wv[:, :, 0:512], w1r[:, :, hg * 512:(hg + 1) * 512]),
                                  (wv[:, :, 512:1024], w1r[:, :, 1024 + hg * 512:1024 + (hg + 1) * 512]),
                                  (wv[:, :, 1024:1536], w1r[:, :, 2048 + hg * 512:2048 + (hg + 1) * 512])]),
                  [], [f"ws{s_}"], key=f"ws{s_}", n_dma=3)
        return rec

    def wl_wout1():
        def rec(s_):
            P.dma("pool", dma_fn([(v3(ws[s_][:, 0:8192], 1024), w_out[1].rearrange("(h p) n -> p h n", p=128))]), [], [f"ws{s_}"], key=f"ws{s_}")
        return rec

    def wl_mla(hg):
        def rec(s_):
            w0r = w_in0.rearrange("(k p) n -> p k n", p=128)
            win_v = v3(ws[s_][:, 0:3328], 416)
            wqb_v = v3(ws[s_][:, 3328:4096], 384)
            wkvb_v = ws[s_][:, 4096:4608]
            P.dma("pool", dma_fn([(win_v, w0r[:, :, 0:416]),
                                  (wqb_v, wqb_in.rearrange("(j p) n -> p j n", p=128)[:, :, hg * 384:(hg + 1) * 384]),
                                  (wkvb_v, wkvb_in[:, hg * 512:(hg + 1) * 512])]),
                  [], [f"ws{s_}"], key=f"ws{s_}", n_dma=3)
        return rec

    def wl_swa():
        def rec(s_):
            w0r = w_in0.rearrange("(k p) n -> p k n", p=128)
            wsv = v3(ws[s_][:, 0:6144], 768)
            P.dma("pool", dma_fn([(wsv, w0r[:, :, 416:1184])]), [], [f"ws{s_}"], key=f"ws{s_}")
        return rec

    def wl_wout0():
        def rec(s_):
            wo = v3(ws[s_][:, 0:8192], 1024)
            P.dma("pool", dma_fn([(wo[:, 0:4, :], w_out[0][0:512, :].rearrange("(h p) n -> p h n", p=128)),
                                  (wo[0:64, 4:8, :], w_out[0][512:768, :].rearrange("(h p) n -> p h n", p=64)),
                                  (wo[64:128, 4:8, :], w_out[0][768:1024, :].rearrange("(h p) n -> p h n", p=64))]),
                  [], [f"ws{s_}"], key=f"ws{s_}", n_dma=3)
        return rec

    def ffn(l, f, s, tiles):
        i = 0 if f == 0 else 2
        W = Arena(arena_t[:, :], AR_BYTES)
        W.reset(work_mark)
        hT = W.alloc((8 * T,), BF16)
        hT_v = v3(hT, T)
        sg = [W.alloc((512,), F32) for _ in range(2)]
        Ab = [[W.alloc((512,), BF16) for _ in range(4)] for _ in range(2)]
        norm_to_h(l, i, s, tiles, hT_v, W)
        groups = FFN_GROUPS
        ycnt = [0]
        for g in range(len(groups)):
            j0, ncn = groups[g]
            s_ = next_slot()
            wgs = v3(ws[s_][:, 0:4096], 512)
            wus = v3(ws[s_][:, 4096:8192], 512)
            wds = v3(ws[s_][:, 8192:12288], 1024)

            def GU(ft, ab):
                t0, n = FT[ft]
                for jj in range(ncn):
                    gb = jj % 2
                    hreads = [H(k, ft) for k in range(8)]
                    P.pe(mm_group([(ps[gb][:, :n], wgs[:, k, jj * 128:(jj + 1) * 128], hT_v[:, k, t0:t0 + n], k == 0, k == 7) for k in range(8)]),
                         [f"ws{s_}"] + hreads, [f"ps{gb}"])
                    P.pe(mm_group([(ps[2 + gb][:, :n], wus[:, k, jj * 128:(jj + 1) * 128], hT_v[:, k, t0:t0 + n], k == 0, k == 7) for k in range(8)]),
                         [f"ws{s_}"] + hreads, [f"ps{2 + gb}"])
                    P.act(act_fn(sg[gb][:, :n], ps[gb][:, :n], AF.Silu, bias=zero_c), [f"ps{gb}"], [f"sg{gb}"])
                    P.dve(tt_fn(Ab[ab][jj][:, :n], sg[gb][:, :n], ps[2 + gb][:, :n], ALU.mult), [f"sg{gb}", f"ps{2 + gb}"], [f"A{ab}.{jj}"])

            def YD(ft, ab):
                t0, n = FT[ft]
                col = 2 if ft == 0 else s
                for c in range(8):
                    yb = 4 + ycnt[0] % 3
                    ycnt[0] += 1
                    P.pe(mm_group([(ps[yb][:, :n], wds[:, jj, c * 128:(c + 1) * 128], Ab[ab][jj][:, :n], jj == 0, jj == ncn - 1) for jj in range(ncn)]),
                         [f"ws{s_}"] + [f"A{ab}.{jj}" for jj in range(ncn)], [f"ps{yb}"])
                    P.dve(stt_fn(xT_v[:, c, t0:t0 + n], ps[yb][:, :n], hgt[l][i][:, c, col:col + 1], xT_v[:, c, t0:t0 + n], ALU.mult, ALU.add),
                          [f"ps{yb}", X(c, ft), f"hg{l}"], [X(c, ft)])

            for idx, ft in enumerate(tiles):
                GU(ft, idx % 2)
                if idx > 0:
                    YD(tiles[idx - 1], (idx - 1) % 2)
            YD(tiles[-1], (len(tiles) - 1) % 2)

    def spill_x(tiles):
        xs = v3(xspill, T)
        reads = [X(k, ft) for k in range(8) for ft in range(5)]
        P.dma("sp", dma_fn([(xspill[:, :], xT)]), reads, ["xspill"], key="spill")

    def wout_phase(l, s, tiles, aT_v, a_names, nchunk, wo_load):
        xs = v3(xspill, T)
        s_ = wo_load()
        wo = v3(ws[s_][:, 0:8192], 1024)
        yc = 0
        for ft in tiles:
            t0, n = FT[ft]
            col = 2 if ft == 0 else s
            P.dma("sp", dma_fn([(xT_v[:, :, t0:t0 + n], xs[:, :, t0:t0 + n])]), ["xspill"], [X(k, ft) for k in range(8)], key=f"xrl{ft}")
            for c in range(8):
                yb = 4 + yc % 3
                yc += 1
                P.pe(mm_group([(ps[yb][:, :n], wo[:, h, c * 128:(c + 1) * 128], aT_v[:, h, t0:t0 + n], h == 0, h == nchunk - 1) for h in range(nchunk)]),
                     [f"ws{s_}"] + a_names(ft), [f"ps{yb}"])
                P.dve(stt_fn(xT_v[:, c, t0:t0 + n], ps[yb][:, :n], hgt[l][1][:, c, col:col + 1], xT_v[:, c, t0:t0 + n], ALU.mult, ALU.add),
                      [f"ps{yb}", X(c, ft), f"hg{l}"], [X(c, ft)])

    def mixer1(s):
        l = 1
        P.barrier()
        W = Arena(arena_t[:, :], AR_BYTES)
        W.reset(work_mark)
        hT_v = v3(W.alloc((8 * T,), BF16), T)
        wmark = W.mark()
        norm_to_h(l, 1, s, range(5), hT_v, W)
        spill_x(range(5))
        P.barrier()
        W.reset(wmark)
        aT_v = v3(W.alloc((8 * T,), BF16), T)
        Xa = Arena(arena_t[:, :], AR_BYTES)
        Xa.reset(x_mark)
        xlimit = x_mark + 8 * T * 4
        QT_v = v3(Xa.alloc((4 * S,), BF16), S)
        KT_v = v3(Xa.alloc((4 * T,), BF16), T)
        Vb_v = v3(Xa.alloc((18 * 512,), BF16), 512)
        g1 = W.alloc((G1N,), F32)
        sm_ = [W.alloc((8,), F32) for _ in range(8)]
        P.dma("sp", dma_fn([(g1, gains1[:, :])]), [], ["g1"], key="g1")
        P.dve(ts_fn(sgl, g1[:, 128:129], 1.0 - LAMBDA_INIT1, ALU.mult), ["g1"], ["sgl"])
        tmark = Xa.mark()
        w1r = w_in1.rearrange("(k p) n -> p k n", p=128)
        for hg in range(2):
            Xa.reset(tmark)
            qn_ = [[Xa.alloc((512,), F32) for _ in range(2)] for _ in range(2)]
            qb_ = [[Xa.alloc((512,), BF16) for _ in range(2)] for _ in range(2)]
            U_ = [[Xa.alloc((512,), F32) for _ in range(2)] for _ in range(2)]
            ss8_ = [[sm_[0], sm_[1]], [sm_[2], sm_[3]]]
            rs8_ = [[sm_[4], sm_[5]], [sm_[6], sm_[7]]]
            assert Xa.off <= xlimit, (Xa.off, xlimit)
            s_ = next_slot()
            wv = v3(ws[s_][:, 0:12288], 1536)
            cap_pe, cap_ch, cap_bk = [], [], []
            for tt in range(18):
                lat = tt >= 2
                ft = ft_of_tt(tt)
                tok = tt * 128
                hreads = [H(k, ft) for k in range(8)]
                par = tt % 2
                pb = 3 * par
                jobs = []
                if lat:
                    jobs.append((0, pb + 0, 0))
                jobs.append((1, pb + 1, 512))
                P.capture_begin()
                for (which, bank, coff) in jobs + [(2, pb + 2, 1024)]:
                    P.pe(mm_group([(ps[bank][:, :], hT_v[:, k, tok:tok + 128], wv[:, k, coff:coff + 512], k == 0, k == 7) for k in range(8)]),
                         [f"ws{s_}"] + hreads, [f"ps{bank}"])
                P.act(act_fn(Vb_v[:, tt, :], ps[pb + 2][:, :], AF.Identity, bias=zero_c), [f"ps{pb + 2}"], [f"V.{tt}"])
                cap_pe.append(P.capture_end())
                chains = []
                backs = []
                for (which, bank, coff) in jobs:
                    P.capture_begin()
                    tg = f"{par}{which}"
                    qn, qb, ss8, rs8 = qn_[par][which], qb_[par][which], ss8_[par][which], rs8_[par][which]
                    gt = g1[:, which * 64:(which + 1) * 64].unsqueeze(1).broadcast_to([128, 8, 64])
                    q3 = v3(qn, 64)
                    P.act(act_fn(qn, ps[bank][:, :], AF.Square, bias=zero_c), [f"ps{bank}"], [f"qn{tg}"])
                    P.dve(red_fn(ss8, q3), [f"qn{tg}"], [f"ss8{tg}"])
                    P.act(act_fn(rs8, ss8, AF.Ln, bias=eps_c, scale=1.0 / 64), [f"ss8{tg}"], [f"rs8{tg}"])
                    P.act(act_fn(rs8, rs8, AF.Exp, bias=zero_c, scale=-0.5), [f"rs8{tg}"], [f"rs8{tg}"])
                    P.dve(tt_fn(q3, v3(ps[bank][:, :], 64), rs8.unsqueeze(2).broadcast_to([128, 8, 64]), ALU.mult),
                          [f"ps{bank}", f"rs8{tg}"], [f"qn{tg}"])
                    if lat:
                        j = tt - 2
                        cosb = ropeB_v[:, j, 0:32].unsqueeze(1).broadcast_to([128, 8, 32])
                        sinb = ropeB_v[:, j, 32:64].unsqueeze(1).broadcast_to([128, 8, 32])
                        P.dve(tt_fn(q3, q3, gt, ALU.mult), [f"qn{tg}", "g1"], [f"qn{tg}"])
                        U = U_[par][which]
                        q4 = qn.rearrange("p (g h d) -> p g h d", h=2, d=32)
                        u4 = U.rearrange("p (g h d) -> p g h d", h=2, d=32)
                        cos4 = ropeB_v[:, j, 0:32].unsqueeze(1).unsqueeze(2).broadcast_to([128, 8, 2, 32])
                        sin4 = ropeB_v[:, j, 32:64].unsqueeze(1).unsqueeze(2).broadcast_to([128, 8, 2, 32])
                        u3 = v3(U, 64)
                        qb3 = v3(qb, 64)
                        P.dve(tt_fn(u4, q4, sin4, ALU.mult), [f"qn{tg}", "ropeB"], [f"U{tg}"])
                        P.dve(tt_fn(q4, q4, cos4, ALU.mult), [f"qn{tg}", "ropeB"], [f"qn{tg}"])
                        P.dve(tt_fn(qb3[:, :, 0:32], q3[:, :, 0:32], u3[:, :, 32:64], ALU.subtract), [f"qn{tg}", f"U{tg}"], [f"qb{tg}a"])
                        P.dve(tt_fn(qb3[:, :, 32:64], u3[:, :, 0:32], q3[:, :, 32:64], ALU.add), [f"qn{tg}", f"U{tg}"], [f"qb{tg}b"])
                    else:
                        P.dve(tt_fn(v3(qb, 64), q3, gt, ALU.mult), [f"qn{tg}", "g1"], [f"qb{tg}a", f"qb{tg}b"])
                    tb = 6 + par
                    tcol = which * 512
                    chains.append(P.capture_end())
                    P.capture_begin()
                    P.pe(tr_group([(psb[tb][:, tcol + h * 128:tcol + (h + 1) * 128], qb[:, h * 128:(h + 1) * 128], ident) for h in range(4)]),
                         [f"qb{tg}a", f"qb{tg}b", "ident"], [f"ps{tb}"])
                    if which == 0:
                        dst = QT_v[:, :, tok - LC:tok - LC + 128]
                        nm = f"QT.{tt}"
                    else:
                        dst = KT_v[:, :, tok:tok + 128]
                        nm = f"KT.{tt}"
                    P.act(act_fn(dst, v3(psb[tb][:, tcol:tcol + 512], 128), AF.Identity, bias=zero_c), [f"ps{tb}"], [nm])
                    backs.append(P.capture_end())
                cap_ch.append(chains)
                cap_bk.append(backs)
            P.replay_zip(cap_pe[0:2])
            for i_ in range(0, 18, 2):
                P.replay_zip(cap_ch[i_] + cap_ch[i_ + 1])
                if i_ + 2 < 18:
                    P.replay_zip(cap_pe[i_ + 2:i_ + 4])
                P.replay_zip(cap_bk[i_] + cap_bk[i_ + 1])
            P.barrier()
            Xa.reset(tmark)
            E = [Xa.alloc((512,), BF16) for _ in range(4)]
            r_ = [Xa.alloc((512,), F32) for _ in range(2)]
            t_ = [Xa.alloc((512,), F32) for _ in range(2)]
            osq = Xa.alloc((512,), BF16)
            rs = r_[1]
            Qp = [[Xa.alloc((512,), BF16) for _ in range(2)] for _ in range(2)]
            assert Xa.off <= xlimit, (Xa.off, xlimit)
            for ub in range(2):
                for m in range(2):
                    P.pool(lambda e, a=Qp[ub][m]: e.memset(a, 0.0), [], [f"Qp{ub}"])
            cnts = {"s": 0, "e": 0}
            items = [(hh, qt, kc, m) for hh in range(4) for qt in range(4) for kc in range(18) for m in range(2)]
            deferred = []

            def s1(it):
                hh, qt, kc, m = it
                q0 = qt * 512
                qreads = [f"QT.{2 + 4 * qt + i_}" for i_ in range(4)]
                sb = cnts["s"] % 3
                cnts["s"] += 1
                eb = cnts["e"] % 4
                cnts["e"] += 1
                ub = (hh * 4 + qt) % 2
                if kc == 0 and m == 0:
                    for mm_ in range(2):
                        P.pool(copy_fn(Qp[ub][mm_][mm_ * 64:(mm_ + 1) * 64, :], QT_v[mm_ * 64:(mm_ + 1) * 64, hh, q0:q0 + 512]),
                               qreads, [f"Qp{ub}"])
                P.pe(mm_group([(ps[sb][:, :], KT_v[:, hh, kc * 128:(kc + 1) * 128], Qp[ub][m], True, True)]),
                     [f"KT.{kc}", f"Qp{ub}"], [f"ps{sb}"])
                P.act(act_fn(E[eb], ps[sb][:, :], AF.Exp, bias=zero_c, scale=0.125), [f"ps{sb}"], [f"E{eb}"])
                return eb

            def part_b(hh, qt):
                h = hg * 4 + hh
                q0 = qt * 512
                P.pe(mm_group([(ps[7][:, :], ones, osq, True, True)]), ["osq", "ones"], ["ps7"])
                P.act(act_fn(rs, ps[7][:, :], AF.Ln, bias=eps_c, scale=1.0 / 128), ["ps7"], ["r1"])
                P.act(act_fn(rs, rs, AF.Exp, bias=zero_c, scale=-0.5), ["r1"], ["r1"])
                P.dve(stt_fn(aT_v[:, h, LC + q0:LC + q0 + 512], t_[0], sgl, rs, ALU.mult, ALU.mult), ["t0", "sgl", "r1"], [f"aT.{h}.{1 + qt}"])

            def s2(it, eb):
                hh, qt, kc, m = it
                P.pe(mm_group([(ps[3 + m][:, :], Vb_v[:, kc, hh * 128:(hh + 1) * 128], E[eb], kc == 0, kc == 17),
                               (ps[5 + m][:, :], ones, E[eb], kc == 0, kc == 17)]),
                     [f"V.{kc}", f"E{eb}", "ones"], [f"ps{3 + m}", f"ps{5 + m}"])
                if kc == 17 and m == 1:
                    for mm_ in range(2):
                        P.dve(copy_fn(t_[mm_], ps[3 + mm_][:, :]), [f"ps{3 + mm_}"], [f"t{mm_}"])
                        P.dve(copy_fn(r_[mm_], ps[5 + mm_][:, :]), [f"ps{5 + mm_}"], [f"r{mm_}"])
                    for mm_ in range(2):
                        P.dve(recip_fn(r_[mm_], r_[mm_]), [f"r{mm_}"], [f"r{mm_}"])
                        P.dve(tt_fn(t_[mm_], t_[mm_], r_[mm_], ALU.mult), [f"t{mm_}", f"r{mm_}"], [f"t{mm_}"])
                    P.dve(stt_fn(t_[0], t_[1], neglam, t_[0], ALU.mult, ALU.add), ["t0", "t1", "neglam"], ["t0"])
                    P.pool(tt_fn(osq, t_[0], t_[0], ALU.mult), ["t0"], ["osq"])
                    deferred.append([30, hh, qt])

            def tick(i):
                for d_ in list(deferred):
                    d_[0] -= 1
                    if d_[0] <= 0:
                        deferred.remove(d_)
                        part_b(d_[1], d_[2])
            run_pipe(items, s1, s2, lag=2, tick=tick)
            for d_ in deferred:
                part_b(d_[1], d_[2])
            P.barrier()

        def wo_load():
            return next_slot()
        wout_phase(l, s, [1, 2, 3, 4], aT_v, lambda ft: [f"aT.{h}.{ft}" for h in range(8)], 8, wo_load)
        P.barrier()

    def mixer0(s):
        l = 0
        P.barrier()
        W = Arena(arena_t[:, :], AR_BYTES)
        W.reset(work_mark)
        hT_v = v3(W.alloc((8 * T,), BF16), T)
        wmark = W.mark()
        norm_to_h(l, 1, s, range(5), hT_v, W)
        spill_x(range(5))
        P.barrier()
        W.reset(wmark)
        aT_v = v3(W.alloc((8 * T,), BF16), T)
        Xa = Arena(arena_t[:, :], AR_BYTES)
        Xa.reset(x_mark)
        xlimit = x_mark + 8 * T * 4
        slot_b = (wl_use[0] + 3) % 2
        g0 = ws[slot_b][:, 8192:12288].bitcast(F32)[:, 0:G0N]
        P.dma("sp", dma_fn([(g0, gains0[:, :])]), [], ["g0"], key="g0")
        gmark = Xa.mark()
        w0r = w_in0.rearrange("(k p) n -> p k n", p=128)
        SC_MLA = 96.0 ** -0.5

        def rstd_small(dst, src, inv_n):
            P.act(act_fn(dst, src, AF.Ln, bias=eps_c, scale=inv_n), ["eps_c"], [])
            P.act(act_fn(dst, dst, AF.Exp, bias=zero_c, scale=-0.5), [], [])

        for hg in range(2):
            Xa.reset(gmark)
            QT_v = v3(Xa.alloc((4 * T,), BF16), T)
            KT_v = v3(Xa.alloc((4 * T,), BF16), T)
            Vb = Xa.alloc((18 * 512,), BF16)
            Vb_v = v3(Vb, 512)
            P.dve(lambda e, a=v3(Vb, 128)[:, :, 64:128]: e.memset(a, 1.0), [], ["Vones"])
            tmark = Xa.mark()
            tsets = []
            for _ in range(2):
                d_ = dict(ssq=Xa.alloc((4,), F32), rsl=Xa.alloc((2,), F32), lat_b=Xa.alloc((384,), BF16), krg=Xa.alloc((32,), F32),
                          krr=Xa.alloc((32,), F32), ra=Xa.alloc((64,), F32), rb=Xa.alloc((64,), F32), latT=Xa.alloc((384,), BF16),
                          sqv=Xa.alloc((512,), F32), qn=Xa.alloc((384,), F32), tq=Xa.alloc((128,), F32), qbb=Xa.alloc((384,), BF16),
                          kn=Xa.alloc((256,), F32), kbb=Xa.alloc((384,), BF16), ss4=Xa.alloc((4,), F32), rq4=Xa.alloc((4,), F32),
                          ss4k=Xa.alloc((4,), F32), rk4=Xa.alloc((4,), F32))
                d_["junk"] = d_["sqv"]
                tsets.append(d_)
            assert Xa.off <= xlimit, (Xa.off, xlimit)
            s_ = next_slot()
            win_v = v3(ws[s_][:, 0:3328], 416)
            wqb_v = v3(ws[s_][:, 3328:4096], 384)
            wkvb_v = ws[s_][:, 4096:4608]
            caps0, capsq, capsk, capsqb, capskb = [], [], [], [], []
            for tt in range(18):
                P.capture_begin()
                lat = tt >= 2
                ft = ft_of_tt(tt)
                tok = tt * 128
                pb = tt % 2
                par = pb
                D_ = tsets[par]
                junk, ssq, rsl, lat_b, krg, krr, ra, rb, latT = D_["junk"], D_["ssq"], D_["rsl"], D_["lat_b"], D_["krg"], D_["krr"], D_["ra"], D_["rb"], D_["latT"]
                sqv, qn, tq, qbb, kn, kbb, ss4, rq4, ss4k, rk4 = D_["sqv"], D_["qn"], D_["tq"], D_["qbb"], D_["kn"], D_["kbb"], D_["ss4"], D_["rq4"], D_["ss4k"], D_["rk4"]
                psq = ps[par]
                pskv = ps[2 + par]
                Pp = ps[pb]
                P.pe(mm_group([(Pp[:, 0:416], hT_v[:, k, tok:tok + 128], win_v[:, k, :], k == 0, k == 7) for k in range(8)]),
                     [f"ws{s_}"] + [H(k, ft) for k in range(8)], [f"ps{pb}"])
                P.act(act_fn(junk[:, 0:256], Pp[:, 0:256], AF.Square, bias=zero_c, accum_out=ssq[:, 0:1]), [f"ps{pb}"], [f"sqv_{par}", f"ssq0_{par}"])
                P.act(act_fn(junk[:, 0:128], Pp[:, 256:384], AF.Square, bias=zero_c, accum_out=ssq[:, 1:2]), [f"ps{pb}"], [f"sqv_{par}", f"ssq1_{par}"])
                P.act(act_fn(junk[:, 0:32], Pp[:, 384:416], AF.Square, bias=zero_c, accum_out=ssq[:, 2:3]), [f"ps{pb}"], [f"sqv_{par}", f"ssq2_{par}"])
                P.act(act_fn(rsl[:, 0:1], ssq[:, 0:1], AF.Ln, bias=eps_c, scale=1.0 / 256), [f"ssq0_{par}"], [f"rsl0_{par}"])
                P.act(act_fn(rsl[:, 1:2], ssq[:, 1:2], AF.Ln, bias=eps_c, scale=1.0 / 128), [f"ssq1_{par}"], [f"rsl1_{par}"])
                P.act(act_fn(rsl, rsl, AF.Exp, bias=zero_c, scale=-0.5), [f"rsl0_{par}", f"rsl1_{par}"], [f"rsl0_{par}", f"rsl1_{par}"])
                P.dve(stt_fn(lat_b[:, 0:256], Pp[:, 0:256], rsl[:, 0:1], g0[:, G0_QA:G0_QA + 256], ALU.mult, ALU.mult),
                      [f"ps{pb}", f"rsl0_{par}", "g0"], [f"latb0_{par}"])
                P.dve(stt_fn(lat_b[:, 256:384], Pp[:, 256:384], rsl[:, 1:2], g0[:, G0_KVA:G0_KVA + 128], ALU.mult, ALU.mult),
                      [f"ps{pb}", f"rsl1_{par}", "g0"], [f"latb1_{par}"])
                P.dve(tt_fn(krg, Pp[:, 384:416], g0[:, G0_KR:G0_KR + 32], ALU.mult), [f"ps{pb}", "g0"], [f"krg_{par}"])
                if lat:
                    j = tt - 2
                    cs = ropeA_v[:, j, 0:16]
                    sn = ropeA_v[:, j, 16:32]
                    P.dve(tt_fn(ra[:, 0:16], krg[:, 0:16], cs, ALU.mult), [f"krg_{par}", "ropeA"], [f"ra_{par}"])
                    P.dve(tt_fn(rb[:, 0:16], krg[:, 16:32], sn, ALU.mult), [f"krg_{par}", "ropeA"], [f"rb_{par}"])
                    P.dve(tt_fn(krr[:, 0:16], ra[:, 0:16], rb[:, 0:16], ALU.subtract), [f"ra_{par}", f"rb_{par}"], [f"krr0_{par}"])
                    P.dve(tt_fn(ra[:, 0:16], krg[:, 0:16], sn, ALU.mult), [f"krg_{par}", "ropeA"], [f"ra_{par}"])
                    P.dve(tt_fn(rb[:, 0:16], krg[:, 16:32], cs, ALU.mult), [f"krg_{par}", "ropeA"], [f"rb_{par}"])
                    P.dve(tt_fn(krr[:, 16:32], ra[:, 0:16], rb[:, 0:16], ALU.add), [f"ra_{par}", f"rb_{par}"], [f"krr1_{par}"])
                else:
                    P.dve(copy_fn(krr, krg), [f"krg_{par}"], [f"krr0_{par}", f"krr1_{par}"])
                P.pe(tr_group([(psb[4 + par][:, jb * 128:(jb + 1) * 128], lat_b[:, jb * 128:(jb + 1) * 128], ident) for jb in range(3)]),
                     [f"latb0_{par}", f"latb1_{par}", "ident"], [f"ps{4 + par}"])
                P.act(act_fn(latT, psb[4 + par][:, 0:384], AF.Identity, bias=zero_c), [f"ps{4 + par}"], [f"latT_{par}"])
                latT_v = v3(latT, 128)
                P.pe(mm_group([(psq[:, 0:384], latT_v[:, jb, :], wqb_v[:, jb, :], jb == 0, jb == 1) for jb in range(2)]),
                     [f"latT_{par}", f"ws{s_}"], [f"ps{par}"])
                P.pe(mm_group([(pskv[:, 0:512], latT_v[:, 2, :], wkvb_v, True, True)]), [f"latT_{par}", f"ws{s_}"], [f"ps{2 + par}"])
                caps0.append(P.capture_end())
                P.capture_begin()
                P.act(act_fn(qn, psq[:, 0:384], AF.Square, bias=zero_c), [f"ps{par}"], [f"qn_{par}"])
                P.dve(red_fn(ss4, v3(qn, 96)), [f"qn_{par}"], [f"ss4_{par}"])
                P.act(act_fn(rq4, ss4, AF.Ln, bias=eps_c, scale=1.0 / 96), [f"ss4_{par}"], [f"rq4_{par}"])
                P.act(act_fn(rq4, rq4, AF.Exp, bias=zero_c, scale=-0.5), [f"rq4_{par}"], [f"rq4_{par}"])
                q3 = v3(qn, 96)
                P.dve(tt_fn(q3, v3(psq[:, 0:384], 96), rq4.unsqueeze(2).broadcast_to([128, 4, 96]), ALU.mult), [f"ps{par}", f"rq4_{par}"], [f"qn_{par}"])
                gq3 = v3(g0[:, G0_Q4:G0_Q4 + 384], 96)
                qb3 = v3(qbb, 96)
                if lat:
                    P.dve(tt_fn(qb3[:, :, 0:64], q3[:, :, 0:64], gq3[:, :, 0:64], ALU.mult), [f"qn_{par}", "g0"], [f"qbb0_{par}"])
                    tq3 = v3(tq, 32)
                    P.dve(tt_fn(tq3, q3[:, :, 64:96], gq3[:, :, 64:96], ALU.mult), [f"qn_{par}", "g0"], [f"tq_{par}"])
                    cs4 = ropeA_v[:, j, 0:16].unsqueeze(1).broadcast_to([128, 4, 16])
                    sn4 = ropeA_v[:, j, 16:32].unsqueeze(1).broadcast_to([128, 4, 16])
                    ra3 = v3(ra, 16)
                    rb3 = v3(rb, 16)
                    P.dve(tt_fn(ra3, tq3[:, :, 0:16], cs4, ALU.mult), [f"tq_{par}", "ropeA"], [f"ra_{par}"])
                    P.dve(tt_fn(rb3, tq3[:, :, 16:32], sn4, ALU.mult), [f"tq_{par}", "ropeA"], [f"rb_{par}"])
                    P.dve(tt_fn(qb3[:, :, 64:80], ra3, rb3, ALU.subtract), [f"ra_{par}", f"rb_{par}"], [f"qbb1_{par}"])
                    P.dve(tt_fn(ra3, tq3[:, :, 0:16], sn4, ALU.mult), [f"tq_{par}", "ropeA"], [f"ra_{par}"])
                    P.dve(tt_fn(rb3, tq3[:, :, 16:32], cs4, ALU.mult), [f"tq_{par}", "ropeA"], [f"rb_{par}"])
                    P.dve(tt_fn(qb3[:, :, 80:96], ra3, rb3, ALU.add), [f"ra_{par}", f"rb_{par}"], [f"qbb2_{par}"])
                else:
                    P.dve(tt_fn(qb3, q3, gq3, ALU.mult), [f"qn_{par}", "g0"], [f"qbb0_{par}", f"qbb1_{par}", f"qbb2_{par}"])
                capsq.append(P.capture_end())
                P.capture_begin()
                P.pe(tr_group([(psb[6 + par][0:96, hh * 128:(hh + 1) * 128], qbb[:, hh * 96:(hh + 1) * 96], ident) for hh in range(4)]),
                     [f"qbb0_{par}", f"qbb1_{par}", f"qbb2_{par}", "ident"], [f"ps{6 + par}"])
                P.act(act_fn(QT_v[0:96, :, tok:tok + 128], v3(psb[6 + par][0:96, 0:512], 128), AF.Identity, bias=zero_c[0:96, :]), [f"ps{6 + par}"], [f"QT.{tt}"])
                capsqb.append(P.capture_end())
                P.capture_begin()
                kv3 = v3(pskv[:, 0:512], 128)
                P.act(act_fn(sqv, pskv[:, 0:512], AF.Square, bias=zero_c), [f"ps{2 + par}"], [f"sqv_{par}"])
                P.dve(red_fn(ss4k, v3(sqv, 128)[:, :, 0:64]), [f"sqv_{par}"], [f"ss4k_{par}"])
                P.dve(ts_fn(ss4k, ss4k, ssq[:, 2:3], ALU.add), [f"ss4k_{par}", f"ssq2_{par}"], [f"ss4k_{par}"])
                P.act(act_fn(rk4, ss4k, AF.Ln, bias=eps_c, scale=1.0 / 96), [f"ss4k_{par}"], [f"rk4_{par}"])
                P.act(act_fn(rk4, rk4, AF.Exp, bias=zero_c, scale=-0.5), [f"rk4_{par}"], [f"rk4_{par}"])
                kn3 = v3(kn, 64)
                kb3 = v3(kbb, 96)
                P.dve(tt_fn(kn3, kv3[:, :, 0:64], rk4.unsqueeze(2).broadcast_to([128, 4, 64]), ALU.mult), [f"ps{2 + par}", f"rk4_{par}"], [f"kn_{par}"])
                P.dve(tt_fn(kb3[:, :, 0:64], kn3, v3(g0[:, G0_KN4:G0_KN4 + 256], 64), ALU.mult), [f"kn_{par}", "g0"], [f"kbb0_{par}"])
                P.dve(tt_fn(kb3[:, :, 64:96], krr.unsqueeze(1).broadcast_to([128, 4, 32]), rk4.unsqueeze(2).broadcast_to([128, 4, 32]), ALU.mult),
                      [f"krr0_{par}", f"krr1_{par}", f"rk4_{par}"], [f"kbb1_{par}"])
                P.act(act_fn(v3(Vb_v[:, tt, :], 128)[:, :, 0:64], kv3[:, :, 64:128], AF.Identity, bias=zero_c), [f"ps{2 + par}", f"kn_{par}"], [f"V.{tt}"])
                capsk.append(P.capture_end())
                P.capture_begin()
                P.pe(tr_group([(psb[4 + par][0:96, 512 + hh * 128:512 + (hh + 1) * 128], kbb[:, hh * 96:(hh + 1) * 96], ident) for hh in range(4)]),
                     [f"kbb0_{par}", f"kbb1_{par}", "ident"], [f"ps{4 + par}"])
                P.act(act_fn(KT_v[0:96, :, tok:tok + 128], v3(psb[4 + par][0:96, 512:1024], 128), AF.Identity, bias=zero_c[0:96, :]), [f"ps{4 + par}"], [f"KT.{tt}"])
                capskb.append(P.capture_end())
            P.replay_zip(caps0[0:2])
            for i_ in range(0, 18, 2):
                P.replay_zip(capsq[i_:i_ + 2] + capsk[i_:i_ + 2])
                if i_ + 2 < 18:
                    P.replay_zip(caps0[i_ + 2:i_ + 4])
                P.replay_zip(capsqb[i_:i_ + 2] + capskb[i_:i_ + 2])
            P.barrier()
            Xa.reset(tmark)
            E = [Xa.alloc((512,), BF16) for _ in range(4)]
            r_ = [Xa.alloc((512,), F32) for _ in range(2)]
            assert Xa.off <= xlimit, (Xa.off, xlimit)
            cnts = {"s": 0, "e": 0}
            items = []
            uidx = 0
            for hh in range(4):
                for ft in range(5):
                    kcs = [0, 1] if ft == 0 else list(range(18))
                    for ki, kc in enumerate(kcs):
                        items.append((hh, ft, kc, ki == 0, ki == len(kcs) - 1, uidx))
                    uidx += 1

            def s1(it):
                hh, ft, kc, first, last, u = it
                t0, n = FT[ft]
                qreads = [f"QT.{tt}" for tt in range(t0 // 128, (t0 + n) // 128)]
                sb = cnts["s"] % 3
                cnts["s"] += 1
                eb = cnts["e"] % 4
                cnts["e"] += 1
                P.pe(mm_group([(ps[sb][:, :n], KT_v[0:96, hh, kc * 128:(kc + 1) * 128], QT_v[0:96, hh, t0:t0 + n], True, True)]),
                     [f"KT.{kc}"] + qreads, [f"ps{sb}"])
                P.act(act_fn(E[eb][:, :n], ps[sb][:, :n], AF.Exp, bias=zero_c, scale=SC_MLA), [f"ps{sb}"], [f"E{eb}"])
                return eb

            def s2(it, eb):
                hh, ft, kc, first, last, u = it
                h = hg * 4 + hh
                t0, n = FT[ft]
                ob = 3 + (u % 4)
                rb_ = u % 2
                P.pe(mm_group([(ps[ob][:, :n], Vb_v[:, kc, hh * 128:(hh + 1) * 128], E[eb][:, :n], first, last)]),
                     [f"V.{kc}", "Vones", f"E{eb}"], [f"ps{ob}"])
                if last:
                    P.dve(recip_fn(r_[rb_][0:64, :n], ps[ob][64:128, :n]), [f"ps{ob}"], [f"r{rb_}"])
                    po = (h % 2) * 64
                    P.dve(tt_fn(aT_v[po:po + 64, h // 2, t0:t0 + n], ps[ob][0:64, :n], r_[rb_][0:64, :n], ALU.mult),
                          [f"ps{ob}", f"r{rb_}"], [f"aT.{h // 2}.{ft}.{h % 2}"])
            run_pipe(items, s1, s2, lag=2)
            P.barrier()

        Xa.reset(gmark)
        sqTp = [Xa.alloc((4 * T,), BF16) for _ in range(2)]
        sqT_v = [v3(a_, 512) for a_ in sqTp]
        skT = Xa.alloc((T,), BF16)
        sV = Xa.alloc((18 * 256,), BF16)
        sV_v = v3(sV, 256)
        P.pool(lambda e, a=sqTp[0]: e.memset(a, 0.0), [], ["sqz0"])
        P.pool(lambda e, a=sqTp[1]: e.memset(a, 0.0), [], ["sqz1"])
        P.dve(lambda e, a=v3(sV, 128)[:, :, 64:128]: e.memset(a, 1.0), [], ["Vones"])
        se = Xa.alloc((1024,), F32)
        sk8 = Xa.alloc((8,), F32)
        tmark = Xa.mark()
        tsets = []
        for _ in range(2):
            tsets.append(dict(qn=Xa.alloc((512,), F32), ra=Xa.alloc((256,), F32), rb=Xa.alloc((256,), F32),
                              sqb=Xa.alloc((512,), BF16), kqn=Xa.alloc((128,), F32), ksq=Xa.alloc((128,), F32), skb=Xa.alloc((128,), BF16),
                              rak=Xa.alloc((64,), F32), rbk=Xa.alloc((64,), F32),
                              ss8=Xa.alloc((8,), F32), rs8=Xa.alloc((8,), F32), ss2=Xa.alloc((2,), F32), rs2=Xa.alloc((2,), F32)))
        assert Xa.off <= xlimit, (Xa.off, xlimit)
        P.act(act_fn(sk8, g0[:, G0_SINK:G0_SINK + 8], AF.Exp, bias=zero_c), ["g0"], ["sk8"])
        P.dve(copy_fn(v3(se, 128), sk8.unsqueeze(2).broadcast_to([128, 8, 128])), ["sk8"], ["se"])
        s_ = next_slot()
        wsv = v3(ws[s_][:, 0:6144], 768)
        cpe, cfq, cfk, cbq, cbk = [], [], [], [], []
        for tt in range(18):
            lat = tt >= 2
            ft = ft_of_tt(tt)
            tok = tt * 128
            pb = tt % 2
            par = pb
            ts_ = tsets[par]
            qn, ra, rb, sqb, kqn, ksq, skb, rak, rbk = (ts_["qn"], ts_["ra"], ts_["rb"], ts_["sqb"], ts_["kqn"], ts_["ksq"], ts_["skb"],
                                                         ts_["rak"], ts_["rbk"])
            ss8, rs8, ss2, rs2 = ts_["ss8"], ts_["rs8"], ts_["ss2"], ts_["rs2"]
            hreads = [H(k, ft) for k in range(8)]
            if lat:
                j = tt - 2
            P.capture_begin()
            P.pe(mm_group([(ps[pb][:, :], hT_v[:, k, tok:tok + 128], wsv[:, k, 0:512], k == 0, k == 7) for k in range(8)]),
                 [f"ws{s_}"] + hreads, [f"ps{pb}"])
            P.pe(mm_group([(ps[2 + pb][:, 0:256], hT_v[:, k, tok:tok + 128], wsv[:, k, 512:768], k == 0, k == 7) for k in range(8)]),
                 [f"ws{s_}"] + hreads, [f"ps{2 + pb}"])
            P.act(act_fn(v3(sV_v[:, tt, :], 128)[:, :, 0:64], v3(ps[2 + pb][:, 128:256], 64), AF.Identity, bias=zero_c), [f"ps{2 + pb}"], [f"V.{tt}"])
            cpe.append(P.capture_end())
            P.capture_begin()
            q3 = v3(qn, 64)
            P.act(act_fn(qn, ps[pb][:, :], AF.Square, bias=zero_c), [f"ps{pb}"], [f"qn_{par}"])
            P.dve(red_fn(ss8, q3), [f"qn_{par}"], [f"ss8_{par}"])
            P.act(act_fn(rs8, ss8, AF.Ln, bias=eps_c, scale=1.0 / 64), [f"ss8_{par}"], [f"rs8_{par}"])
            P.act(act_fn(rs8, rs8, AF.Exp, bias=zero_c, scale=-0.5), [f"rs8_{par}"], [f"rs8_{par}"])
            P.dve(tt_fn(q3, v3(ps[pb][:, :], 64), rs8.unsqueeze(2).broadcast_to([128, 8, 64]), ALU.mult), [f"ps{pb}", f"rs8_{par}"], [f"qn_{par}"])
            gq = g0[:, G0_SQ8:G0_SQ8 + 512]
            qb4 = sqb.rearrange("p (pp hf d) -> p hf pp d", pp=4, hf=2)
            if lat:
                cosb = ropeB_v[:, j, 0:32].unsqueeze(1).broadcast_to([128, 8, 32])
                sinb = ropeB_v[:, j, 32:64].unsqueeze(1).broadcast_to([128, 8, 32])
                P.dve(tt_fn(qn, qn, gq, ALU.mult), [f"qn_{par}", "g0"], [f"qn_{par}"])
                ra3 = v3(ra, 32)
                rb3 = v3(rb, 32)
                ra4 = ra.rearrange("p (hf pp d) -> p hf pp d", hf=2, pp=4)
                rb4 = rb.rearrange("p (hf pp d) -> p hf pp d", hf=2, pp=4)
                P.dve(tt_fn(ra3, q3[:, :, 0:32], cosb, ALU.mult), [f"qn_{par}", "ropeB"], [f"ra_{par}"])
                P.dve(tt_fn(rb3, q3[:, :, 32:64], sinb, ALU.mult), [f"qn_{par}", "ropeB"], [f"rb_{par}"])
                P.dve(tt_fn(qb4[:, :, :, 0:32], ra4, rb4, ALU.subtract), [f"ra_{par}", f"rb_{par}"], [f"sqb0_{par}"])
                P.dve(tt_fn(ra3, q3[:, :, 0:32], sinb, ALU.mult), [f"qn_{par}", "ropeB"], [f"ra_{par}"])
                P.dve(tt_fn(rb3, q3[:, :, 32:64], cosb, ALU.mult), [f"qn_{par}", "ropeB"], [f"rb_{par}"])
                P.dve(tt_fn(qb4[:, :, :, 32:64], ra4, rb4, ALU.add), [f"ra_{par}", f"rb_{par}"], [f"sqb1_{par}"])
            else:
                qn4 = qn.rearrange("p (hf pp d) -> p hf pp d", hf=2, pp=4)
                gq4 = gq.rearrange("p (hf pp d) -> p hf pp d", hf=2, pp=4)
                P.dve(tt_fn(qb4, qn4, gq4, ALU.mult), [f"qn_{par}", "g0"], [f"sqb0_{par}", f"sqb1_{par}"])
            cfq.append(P.capture_end())
            P.capture_begin()
            P.pe(tr_group([(psb[4 + par][:, pp * 128:(pp + 1) * 128], sqb[:, pp * 128:(pp + 1) * 128], ident) for pp in range(4)]),
                 [f"sqb0_{par}", f"sqb1_{par}", "ident"], [f"ps{4 + par}"])
            P.act(act_fn(sqT_v[0][0:64, tt, :], psb[4 + par][0:64, 0:512], AF.Identity, bias=zero_c[0:64, :]), [f"ps{4 + par}", "sqz0"], [f"QT.{tt}.0"])
            P.act(act_fn(sqT_v[1][64:128, tt, :], psb[4 + par][64:128, 0:512], AF.Identity, bias=zero_c[64:128, :]), [f"ps{4 + par}", "sqz1"], [f"QT.{tt}.1"])
            cbq.append(P.capture_end())
            P.capture_begin()
            P.act(act_fn(ksq, ps[2 + pb][:, 0:128], AF.Square, bias=zero_c), [f"ps{2 + pb}"], [f"ksq_{par}"])
            P.dve(red_fn(ss2, v3(ksq, 64)), [f"ksq_{par}"], [f"ss2_{par}"])
            P.act(act_fn(rs2, ss2, AF.Ln, bias=eps_c, scale=1.0 / 64), [f"ss2_{par}"], [f"rs2_{par}"])
            P.act(act_fn(rs2, rs2, AF.Exp, bias=zero_c, scale=-0.5), [f"rs2_{par}"], [f"rs2_{par}"])
            k3 = v3(kqn, 64)
            P.dve(tt_fn(k3, v3(ps[2 + pb][:, 0:128], 64), rs2.unsqueeze(2).broadcast_to([128, 2, 64]), ALU.mult), [f"ps{2 + pb}", f"rs2_{par}"], [f"kqn_{par}"])
            gk = g0[:, G0_SK2:G0_SK2 + 128]
            kb3 = v3(skb, 64)
            if lat:
                cos2 = ropeB_v[:, j, 0:32].unsqueeze(1).broadcast_to([128, 2, 32])
                sin2 = ropeB_v[:, j, 32:64].unsqueeze(1).broadcast_to([128, 2, 32])
                P.dve(tt_fn(kqn, kqn, gk, ALU.mult), [f"kqn_{par}", "g0"], [f"kqn_{par}"])
                ra2 = v3(rak, 32)
                rb2 = v3(rbk, 32)
                P.dve(tt_fn(ra2, k3[:, :, 0:32], cos2, ALU.mult), [f"kqn_{par}", "ropeB"], [f"rak_{par}"])
                P.dve(tt_fn(rb2, k3[:, :, 32:64], sin2, ALU.mult), [f"kqn_{par}", "ropeB"], [f"rbk_{par}"])
                P.dve(tt_fn(kb3[:, :, 0:32], ra2, rb2, ALU.subtract), [f"rak_{par}", f"rbk_{par}"], [f"skb0_{par}"])
                P.dve(tt_fn(ra2, k3[:, :, 0:32], sin2, ALU.mult), [f"kqn_{par}", "ropeB"], [f"rak_{par}"])
                P.dve(tt_fn(rb2, k3[:, :, 32:64], cos2, ALU.mult), [f"kqn_{par}", "ropeB"], [f"rbk_{par}"])
                P.dve(tt_fn(kb3[:, :, 32:64], ra2, rb2, ALU.add), [f"rak_{par}", f"rbk_{par}"], [f"skb1_{par}"])
            else:
                P.dve(tt_fn(skb, kqn, gk, ALU.mult), [f"kqn_{par}", "g0"], [f"skb0_{par}", f"skb1_{par}"])
            cfk.append(P.capture_end())
            P.capture_begin()
            P.pe(tr_group([(psb[4 + par][:, 512:640], skb, ident)]), [f"skb0_{par}", f"skb1_{par}", "ident"], [f"ps{4 + par}"])
            P.act(act_fn(skT[:, tok:tok + 128], psb[4 + par][:, 512:640], AF.Identity, bias=zero_c), [f"ps{4 + par}"], [f"KT.{tt}"])
            cbk.append(P.capture_end())
        P.replay_zip(cpe[0:2])
        for i_ in range(0, 18, 2):
            P.replay_zip(cfq[i_:i_ + 2] + cfk[i_:i_ + 2])
            if i_ + 2 < 18:
                P.replay_zip(cpe[i_ + 2:i_ + 4])
            P.replay_zip(cbq[i_:i_ + 2] + cbk[i_:i_ + 2])
        P.barrier()
        Xa.reset(tmark)
        E = [Xa.alloc((512,), BF16) for _ in range(4)]
        r_ = [Xa.alloc((512,), F32) for _ in range(2)]
        assert Xa.off <= xlimit, (Xa.off, xlimit)
        se_v = v3(se, 512)
        cnts = {"s": 0, "e": 0}
        items = []
        uidx = 0
        for g in range(2):
            for tt in range(18):
                if tt < 2:
                    kcs = [(0, None), (1, None)]
                else:
                    n_ = tt - 2
                    kcs = [(0, None), (1, None)]
                    if n_ > 0:
                        kcs.append((tt - 1, maskp))
                    kcs.append((tt, None))
                    if n_ < 15:
                        kcs.append((tt + 1, maskn))
                for ki, (kc, msk) in enumerate(kcs):
                    items.append((g, tt, kc, msk, ki == 0, ki == len(kcs) - 1, uidx))
                uidx += 1

        def s1(it):
            g, tt, kc, msk, first, last, u = it
            gp = slice(g * 64, (g + 1) * 64)
            sb = cnts["s"] % 3
            cnts["s"] += 1
            eb = cnts["e"] % 4
            cnts["e"] += 1
            P.pe(mm_group([(ps[sb][:, :], skT[:, kc * 128:(kc + 1) * 128], sqT_v[g][:, tt, :], True, True)]),
                 [f"KT.{kc}", f"QT.{tt}.{g}", f"sqz{g}"], [f"ps{sb}"])
            P.act(act_fn(E[eb], ps[sb][:, :], AF.Exp, bias=zero_c, scale=0.125), [f"ps{sb}"], [f"E{eb}"])
            if msk is not None:
                P.pool(tt_fn(E[eb], E[eb], msk, ALU.mult), [f"E{eb}", "maskp", "maskn"], [f"E{eb}"])
            return eb

        def s2(it, eb):
            g, tt, kc, msk, first, last, u = it
            gp = slice(g * 64, (g + 1) * 64)
            tok = tt * 128
            ft = ft_of_tt(tt)
            ob = 3 + (u % 4)
            rb_ = u % 2
            P.pe(mm_group([(ps[ob][:, :], sV_v[:, kc, g * 128:(g + 1) * 128], E[eb], first, last)]),
                 [f"V.{kc}", "Vones", f"E{eb}"], [f"ps{ob}"])
            if last:
                P.dve(tt_fn(r_[rb_][0:64, :], ps[ob][64:128, :], se_v[64:128, g, :], ALU.add), [f"ps{ob}", "se"], [f"r{rb_}"])
                P.dve(recip_fn(r_[rb_][0:64, :], r_[rb_][0:64, :]), [f"r{rb_}"], [f"r{rb_}"])
                P.dve(tt_fn(aT_v[gp, 4:8, tok:tok + 128], v3(ps[ob][0:64, :], 128), v3(r_[rb_][0:64, :], 128), ALU.mult),
                      [f"ps{ob}", f"r{rb_}"], [f"aT.{4 + pp}.{ft}.{g}.{tt}" for pp in range(4)])
        run_pipe(items, s1, s2, lag=2)
        P.barrier()

        def wo_load():
            return next_slot()

        def a_names(ft):
            t0, n = FT[ft]
            nm = [f"aT.{c}.{ft}.{hp}" for c in range(4) for hp in range(2)]
            nm += [f"aT.{4 + pp}.{ft}.{g}.{tt}" for pp in range(4) for g in range(2) for tt in range(t0 // 128, (t0 + n) // 128)]
            return nm
        wout_phase(l, s, [0, 1, 2, 3, 4], aT_v, a_names, 8, wo_load)
        P.barrier()

    stages = ["ffn00", "mix0", "ffn01", "ffn10", "mix1", "ffn11"]
    n_stage = len(stages) if stop_after is None else stages.index(stop_after) + 1
    for s in range(nseq):
        for si in range(n_stage):
            st = stages[si]
            if st.startswith("ffn"):
                WL.extend(wl_ffn(int(st[3]), int(st[4]), g) for g in range(6))
            elif st == "mix0":
                WL.extend([wl_mla(0), wl_mla(1), wl_swa(), wl_wout0()])
            else:
                WL.extend([wl_diff(0), wl_diff(1), wl_wout1()])
    for s in range(nseq):
        if s > 0:
            load_x(s)
        for si in range(n_stage):
            st = stages[si]
            if st == "ffn00":
                ffn(0, 0, s, [0, 1, 2, 3, 4])
            elif st == "mix0":
                mixer0(s)
            elif st == "ffn01":
                ffn(0, 1, s, [0, 1, 2, 3, 4])
            elif st == "ffn10":
                ffn(1, 0, s, [0, 1, 2, 3, 4])
            elif st == "mix1":
                mixer1(s)
            elif st == "ffn11":
                ffn(1, 1, s, [1, 2, 3, 4])
            P.barrier()
        P.dma("sp", dma_fn([(outT[s].rearrange("(k p) t -> p k t", p=128), xT_v[:, :, LC:T])]),
              [X(k, ft) for k in range(8) for ft in range(1, 5)], [f"out{s}"], key=f"out{s}")
    P._add("sp", None, [f"out{s}" for s in range(nseq)], [])

    blk = ctx.enter_context(nc.Block())
    semstack = P.finalize_and_emit(blk)
    ctx.enter_context(semstack)
    ctx.close()
    nc._prog_stats = {e: len(P.per_eng[e]) for e in ENGINES}
    nc._n_sems = P.n_sems
    return nc


def _rope_tables():
    pos = np.arange(S)
    row = (pos // 64).astype(np.float32)
    col = (pos % 64).astype(np.float32)

    def tab(rot_dim):
        n_f = rot_dim // 4
        inv = (np.float32(10000.0) ** (-np.arange(n_f, dtype=np.float32) / np.float32(n_f))).astype(np.float32)
        ang = np.concatenate([row[:, None] * inv[None, :], col[:, None] * inv[None, :]], axis=-1).astype(np.float32)
        return np.cos(ang).astype(np.float32), np.sin(ang).astype(np.float32)
    ca, sa = tab(32)
    cb, sb = tab(64)
    A_ = np.concatenate([ca, sa], axis=-1).reshape(16, 128, 32).transpose(1, 0, 2).reshape(128, 512)
    B_ = np.concatenate([cb, sb], axis=-1).reshape(16, 128, 64).transpose(1, 0, 2).reshape(128, 1024)
    return np.ascontiguousarray(np.concatenate([A_, B_], axis=1), dtype=np.float32)


def _const_b():
    ident = np.eye(128, dtype=np.float32)
    ones = np.ones((128, 128), dtype=np.float32)
    a = np.arange(128)[:, None]
    b = np.arange(128)[None, :]
    mp = (b <= a).astype(np.float32)
    mn = (a <= b).astype(np.float32)
    return np.ascontiguousarray(np.concatenate([ident, ones, np.tile(mp, (1, 4)), np.tile(mn, (1, 4))], axis=1), dtype=np.float32)


def _rep(v, times=1):
    v = np.asarray(v, dtype=np.float32).reshape(-1)
    return np.tile(np.tile(v, times)[None, :], (128, 1))


_CACHE = {}


def make_in_maps(inp):
    f = lambda a: np.ascontiguousarray(np.asarray(a, dtype=np.float32))
    shared = {}
    for l in range(2):
        p = f"l{l}_"
        shared[f"ada_w{l}"] = f(inp[p + "ada_w"])
        shared[f"ada_bT{l}"] = f(np.asarray(inp[p + "ada_b"]).reshape(72, 128).T)
        shared[f"norm_gT{l}"] = f(np.asarray(inp[p + "norm_g"]).reshape(3, 8, 128).transpose(2, 0, 1).reshape(128, 24))
        shared[f"wg{l}"] = f(inp[p + "ffn_wg"])
        shared[f"wu{l}"] = f(inp[p + "ffn_wu"])
        shared[f"wd{l}"] = f(inp[p + "ffn_wd"])
    shared["w_in0"] = f(inp["l0_w_in"])
    shared["wqb"] = f(inp["l0_mla_wqb"])
    shared["wkvb"] = f(inp["l0_mla_wkvb"])
    shared["w_out0"] = f(inp["l0_w_out"])
    shared["w_out1"] = f(inp["l1_w_out"])
    shared["w_in1"] = f(inp["l1_w_in"])
    kg = np.asarray(inp["l0_mla_k_g"], dtype=np.float32)
    shared["gains0"] = f(np.concatenate([
        _rep(inp["l0_mla_qa_g"]), _rep(inp["l0_mla_kva_g"]), _rep(inp["l0_mla_q_g"], 4), _rep(kg[:64], 4), _rep(kg[64:]),
        _rep(inp["l0_swa_q_g"], 8), _rep(inp["l0_swa_k_g"], 2), _rep(inp["l0_swa_sink"])], axis=1))
    assert shared["gains0"].shape == (128, G0N)
    shared["gains1"] = f(np.concatenate([_rep(inp["l1_q_g"]), _rep(inp["l1_k_g"]),
                                         np.asarray(inp["l1_subln_g"], dtype=np.float32).reshape(128, 1)], axis=1))
    shared["lam"] = f(np.stack([np.asarray(inp[k], dtype=np.float32) for k in
                                ("l1_lambda_q1", "l1_lambda_k1", "l1_lambda_q2", "l1_lambda_k2")], axis=1))
    shared["consts_f"] = _rope_tables()
    shared["consts_b"] = _const_b()
    x = np.asarray(inp["x"], dtype=np.float32)
    c = np.asarray(inp["c"], dtype=np.float32)
    cx = np.asarray(inp["ctx"], dtype=np.float32)
    cc = np.asarray(inp["c_ctx"], dtype=np.float32)
    maps = []
    for core in range(NCORES):
        b0 = 2 * core
        m = dict(shared)
        m["xT"] = np.ascontiguousarray(x[b0:b0 + 2].transpose(0, 2, 1))
        m["ctxT"] = np.ascontiguousarray(cx[b0:b0 + 2].transpose(0, 2, 1))
        m["c3"] = np.ascontiguousarray(np.stack([c[b0], c[b0 + 1], cc], axis=1))
        maps.append(m)
    return maps


def kernel(**inputs):
    if "nc" not in _CACHE:
        _CACHE["nc"] = build_program()
    nc = _CACHE["nc"]
    in_maps = make_in_maps(inputs)
    res = run_bass_kernel_spmd(nc, in_maps, core_ids=list(range(NCORES)))
    out = np.empty((2 * NCORES, S, D), dtype=np.float32)
    for core in range(NCORES):
        o = np.asarray(res.results[core]["outT"])
        out[2 * core:2 * core + 2] = o.transpose(0, 2, 1)
    return out
```

```python
import math
import numpy as np
import concourse.bass as bass
import concourse.mybir as mybir
from concourse.bass_utils import run_bass_kernel_spmd

F32 = mybir.dt.float32
BF16 = mybir.dt.bfloat16
AF = mybir.ActivationFunctionType
ALU = mybir.AluOpType
AX = mybir.AxisListType

D = 1024
S = 2048
LC = 256
T = S + LC
DFF = 2816
NJ = DFF // 128
EPS = 1e-6
NCORES = 8
LAMBDA_INIT1 = 0.8 - 0.6 * math.exp(-0.3 * 1)

ENGINES = ("pe", "act", "dve", "pool", "sp")
EPOCH = 30000


class Op:
    __slots__ = ("eng", "fn", "deps", "dma_key", "dma_cnt", "inc", "cnt", "waits", "idx", "n_dma")

    def __init__(self, eng, fn):
        self.eng = eng
        self.fn = fn
        self.deps = set()
        self.dma_key = None
        self.dma_cnt = 0
        self.inc = False
        self.cnt = 0
        self.waits = []
        self.n_dma = 1


class Prog:
    def __init__(self, nc):
        self.nc = nc
        self.ops = []
        self.per_eng = {e: [] for e in ENGINES}
        self.last_writer = {}
        self.readers = {}
        self.pending_bar = {e: set() for e in ENGINES}
        self.dma_counts = {}
        self.dma_since_bar = []

    def capture_begin(self):
        self.cap = []

    def capture_end(self):
        c = self.cap
        self.cap = None
        return c

    def replay_zip(self, lists):
        idx = [0] * len(lists)
        while True:
            alive = False
            for li, L in enumerate(lists):
                if idx[li] < len(L):
                    alive = True
                    self._add(*L[idx[li]])
                    idx[li] += 1
            if not alive:
                break

    def _add(self, eng, fn, reads, writes, dma_key=None, n_dma=1):
        if getattr(self, "cap", None) is not None:
            self.cap.append((eng, fn, list(reads), list(writes), dma_key, n_dma))
            return None
        op = Op(eng, fn)
        op.idx = len(self.ops)
        deps = op.deps
        for r in reads:
            w = self.last_writer.get(r)
            if w is not None:
                deps.add(w)
            self.readers.setdefault(r, []).append(op.idx)
        for w_ in writes:
            w = self.last_writer.get(w_)
            if w is not None:
                deps.add(w)
            rl = self.readers.get(w_)
            if rl:
                deps.update(rl)
            self.last_writer[w_] = op.idx
            self.readers[w_] = []
        if self.pending_bar[eng]:
            deps.update(self.pending_bar[eng])
            self.pending_bar[eng] = set()
        deps.discard(op.idx)
        if dma_key is not None:
            op.dma_key = dma_key
            op.n_dma = n_dma
            c = self.dma_counts.get(dma_key, 0) + n_dma
            self.dma_counts[dma_key] = c
            op.dma_cnt = c
            self.dma_since_bar.append(op.idx)
        self.ops.append(op)
        self.per_eng[eng].append(op)
        return op

    def pe(self, fn, reads, writes):
        return self._add("pe", fn, reads, writes)

    def act(self, fn, reads, writes):
        return self._add("act", fn, reads, writes)

    def dve(self, fn, reads, writes):
        return self._add("dve", fn, reads, writes)

    def pool(self, fn, reads, writes):
        return self._add("pool", fn, reads, writes)

    def dma(self, queue, fn, reads, writes, key, n_dma=1):
        return self._add(queue, fn, reads, writes, dma_key=key, n_dma=n_dma)

    def barrier(self):
        last = set()
        for e in ENGINES:
            for op in reversed(self.per_eng[e]):
                if op.dma_key is None:
                    last.add(op.idx)
                    break
        last.update(self.dma_since_bar)
        self.dma_since_bar = []
        for e in ENGINES:
            self.pending_bar[e] = set(last)

    def finalize_and_emit(self, block):
        nc = self.nc
        ops = self.ops
        for op in ops:
            for d in op.deps:
                dop = ops[d]
                if dop.dma_key is not None:
                    continue
                if dop.eng == "pe" and op.eng == "pe" and op.dma_key is None:
                    continue
                dop.inc = True
        counts = {e: 0 for e in ENGINES}
        for e in ENGINES:
            for op in self.per_eng[e]:
                if op.dma_key is None and op.inc:
                    counts[e] += 1
                    op.cnt = counts[e]
        import contextlib
        stack = contextlib.ExitStack()
        eng_sems = {}
        for e in ENGINES:
            n_ep = counts[e] // EPOCH + 1
            eng_sems[e] = [stack.enter_context(nc.semaphore(f"s_{e}{i}")) for i in range(n_ep)]
        dma_sems = {k: stack.enter_context(nc.semaphore(f"d_{k}")) for k in self.dma_counts}
        self.n_sems = sum(len(v) for v in eng_sems.values()) + len(dma_sems)
        for e in ENGINES:
            seen = {}
            for op in self.per_eng[e]:
                need = {}
                for d in op.deps:
                    dop = ops[d]
                    if dop.dma_key is not None:
                        key = ("d", dop.dma_key)
                        val = 16 * dop.dma_cnt
                    else:
                        if dop.eng == "pe" and e == "pe" and op.dma_key is None:
                            continue
                        c = dop.cnt - 1
                        key = ("e", dop.eng, c // EPOCH)
                        val = c % EPOCH + 1
                    if seen.get(key, 0) >= val:
                        continue
                    if need.get(key, 0) < val:
                        need[key] = val
                for key, val in need.items():
                    seen[key] = val
                    sem = dma_sems[key[1]] if key[0] == "d" else eng_sems[key[1]][key[2]]
                    op.waits.append((sem, val))

        def emit(engname, eng):
            for op in self.per_eng[engname]:
                for sem, val in op.waits:
                    eng.wait_ge(sem, val)
                if op.fn is None:
                    continue
                ins = op.fn(eng)
                if op.dma_key is not None:
                    assert len(ins) == op.n_dma, (len(ins), op.n_dma)
                    for i_ in ins:
                        i_.then_inc(dma_sems[op.dma_key], 16)
                elif op.inc:
                    c = op.cnt - 1
                    ins.then_inc(eng_sems[engname][c // EPOCH], 1)

        @block.tensor
        def _(eng):
            emit("pe", eng)

        @block.scalar
        def _(eng):
            emit("act", eng)

        @block.vector
        def _(eng):
            emit("dve", eng)

        @block.gpsimd
        def _(eng):
            emit("pool", eng)

        @block.sync
        def _(eng):
            emit("sp", eng)

        return stack


class Arena:
    def __init__(self, ap_f32, nbytes):
        self.ap = ap_f32
        self.nbytes = nbytes
        self.off = 0
        self.peak = 0

    def mark(self):
        return self.off

    def reset(self, m):
        self.off = m

    def alloc(self, shape_free, dtype):
        esz = 4 if dtype == F32 else 2
        n = int(np.prod(shape_free))
        nb = (n * esz + 31) // 32 * 32
        assert self.off + nb <= self.nbytes, f"arena overflow {self.off}+{nb}>{self.nbytes}"
        a = self.ap[:, self.off // 4:(self.off + nb) // 4]
        self.off += nb
        self.peak = max(self.peak, self.off)
        if dtype != F32:
            a = a.bitcast(dtype)
        a = a[:, 0:n]
        if len(shape_free) == 2:
            a = a.rearrange("p (a b) -> p a b", b=shape_free[1])
        elif len(shape_free) == 3:
            a = a.rearrange("p (a b c) -> p a b c", b=shape_free[1], c=shape_free[2])
        return a


def mm_group(args):
    def fn(e):
        ins = None
        for (o, l, r, st, sp) in args:
            ins = e.matmul(o, lhsT=l, rhs=r, start=st, stop=sp)
        return ins
    return fn


def tr_group(args):
    def fn(e):
        ins = None
        for (o, i, idn) in args:
            ins = e.transpose(o, i, idn)
        return ins
    return fn


def act_fn(out, in_, func, bias=None, scale=None, accum_out=None):
    def fn(e):
        kw = {}
        if bias is not None:
            kw["bias"] = bias
        if scale is not None:
            kw["scale"] = scale
        if accum_out is not None:
            kw["accum_out"] = accum_out
        return e.activation(out=out, in_=in_, func=func, **kw)
    return fn


def tt_fn(out, in0, in1, op):
    return lambda e: e.tensor_tensor(out=out, in0=in0, in1=in1, op=op)


def ts_fn(out, in0, s1, op0, s2=None, op1=None):
    if op1 is None:
        return lambda e: e.tensor_scalar(out=out, in0=in0, scalar1=s1, scalar2=None, op0=op0)
    return lambda e: e.tensor_scalar(out=out, in0=in0, scalar1=s1, scalar2=s2, op0=op0, op1=op1)


def stt_fn(out, in0, scalar, in1, op0, op1):
    return lambda e: e.scalar_tensor_tensor(out=out, in0=in0, scalar=scalar, in1=in1, op0=op0, op1=op1)


def copy_fn(out, in_):
    return lambda e: e.tensor_copy(out=out, in_=in_)


def red_fn(out, in_, op=None):
    return lambda e: e.tensor_reduce(out=out, in_=in_, axis=AX.X, op=(op or ALU.add))


def recip_fn(out, in_):
    return lambda e: e.reciprocal(out=out, in_=in_)


def run_pipe(items, s1, s2, lag=2, tick=None):
    toks = {}
    n = len(items)
    for i in range(n + lag):
        if i < n:
            toks[i] = s1(items[i])
        if i >= lag:
            s2(items[i - lag], toks.pop(i - lag))
        if tick is not None:
            tick(i)


def dma_fn(pairs):
    def fn(e):
        return [e.dma_start(out=o, in_=i) for (o, i) in pairs]
    return fn


FT = [(0, 256), (256, 512), (768, 512), (1280, 512), (1792, 512)]


def ft_of_tt(tt):
    return 0 if tt < 2 else 1 + (tt - 2) // 4


def v3(ap, b):
    return ap.rearrange("p (a b) -> p a b", b=b)


G0_QA, G0_KVA, G0_Q4, G0_KN4, G0_KR, G0_SQ8, G0_SK2, G0_SINK, G0N = 0, 256, 384, 768, 1024, 1056, 1568, 1696, 1704
G1N = 129
AR_BYTES = 210944


def build_program(stop_after=None, nseq=2):
    nc = bass.Bass("TRN2", target_bir_lowering=False)

    def din(name, shape):
        return nc.dram_tensor(name, list(shape), F32, kind="ExternalInput").ap()

    xT_in = din("xT", [2, D, S])
    ctxT_in = din("ctxT", [2, D, LC])
    c3_in = din("c3", [D, 3])
    ada_w = [din(f"ada_w{l}", [D, 9 * D]) for l in range(2)]
    ada_b = [din(f"ada_bT{l}", [128, 72]) for l in range(2)]
    norm_g = [din(f"norm_gT{l}", [128, 24]) for l in range(2)]
    wg = [din(f"wg{l}", [2, D, DFF]) for l in range(2)]
    wu = [din(f"wu{l}", [2, D, DFF]) for l in range(2)]
    wd = [din(f"wd{l}", [2, DFF, D]) for l in range(2)]
    w_in0 = din("w_in0", [D, 1184])
    wqb_in = din("wqb", [256, 768])
    wkvb_in = din("wkvb", [128, 1024])
    w_out = [din("w_out0", [D, D]), din("w_out1", [D, D])]
    gains0 = din("gains0", [128, G0N])
    w_in1 = din("w_in1", [D, 3072])
    gains1 = din("gains1", [128, G1N])
    lam_in = din("lam", [64, 4])
    consts_f = din("consts_f", [128, 1536])
    consts_b = din("consts_b", [128, 1280])
    outT = nc.dram_tensor("outT", [2, D, S], F32, kind="ExternalOutput").ap()
    xspill = nc.dram_tensor("xspill", [128, 8 * T], F32, kind="Internal").ap()

    import contextlib
    ctx = contextlib.ExitStack()
    arena_t = ctx.enter_context(nc.sbuf_tensor("arena", [128, AR_BYTES // 4], F32))
    ps = [ctx.enter_context(nc.psum_tensor(f"ps{i}", [128, 512], F32)) for i in range(8)]
    psb = [p_[:, :].bitcast(BF16) for p_ in ps]

    A = Arena(arena_t[:, :], AR_BYTES)
    ident = A.alloc((128,), BF16)
    ones = A.alloc((128,), BF16)
    maskp = A.alloc((512,), BF16)
    maskn = A.alloc((512,), BF16)
    onesf = A.alloc((128,), F32)
    ropeA = A.alloc((512,), F32)
    ropeB = A.alloc((1024,), F32)
    ropeA_v = v3(ropeA, 32)
    ropeB_v = v3(ropeB, 64)
    mod = [A.alloc((216,), F32) for _ in range(2)]
    mod_v = [v3(m_, 3) for m_ in mod]
    gs = [[v3(A.alloc((24,), F32), 3) for _ in range(3)] for _ in range(2)]
    hgt = [[v3(A.alloc((24,), F32), 3) for _ in range(3)] for _ in range(2)]
    adab = [A.alloc((72,), F32) for _ in range(2)]
    normg = [A.alloc((24,), F32) for _ in range(2)]
    c3s = A.alloc((24,), F32)
    sc = A.alloc((24,), F32)
    zero_c = A.alloc((1,), F32)
    eps_c = A.alloc((1,), F32)
    lamv = A.alloc((4,), F32)
    lamp = A.alloc((2,), F32)
    e12 = A.alloc((2,), F32)
    neglam = A.alloc((1,), F32)
    sgl = A.alloc((1,), F32)
    ws = [A.alloc((12288,), BF16) for _ in range(2)]
    x_mark = A.mark()
    xT = A.alloc((8 * T,), F32)
    xT_v = v3(xT, T)
    work_mark = A.mark()

    P = Prog(nc)
    WL = []
    wl_use = [0]
    wl_rec = [0]

    def next_slot():
        i = wl_use[0]
        wl_use[0] += 1
        while wl_rec[0] < len(WL) and wl_rec[0] <= i + 1:
            j = wl_rec[0]
            WL[j](j % 2)
            wl_rec[0] += 1
        return i % 2

    def X(k, ft):
        return f"x{k}.{ft}"

    def H(k, ft):
        return f"h{k}.{ft}"

    P.dma("sp", dma_fn([(ropeA, consts_f[:, 0:512]), (ropeB, consts_f[:, 512:1536])]), [], ["ropeA", "ropeB"], key="c0", n_dma=2)
    P.dma("pool", dma_fn([(ident, consts_b[:, 0:128]), (ones, consts_b[:, 128:256]),
                          (maskp, consts_b[:, 256:768]), (maskn, consts_b[:, 768:1280])]),
          [], ["ident", "ones", "maskp", "maskn"], key="c1", n_dma=4)
    P.dve(lambda e: e.memset(zero_c, 0.0), [], ["zero_c"])
    P.dve(lambda e: e.memset(eps_c, EPS), [], ["eps_c"])
    P.dve(lambda e: e.memset(onesf, 1.0), [], ["onesf"])
    P.dma("sp", dma_fn([(v3(c3s, 3), c3_in.rearrange("(k p) n -> p k n", p=128)),
                        (adab[0], ada_b[0][:, :]), (adab[1], ada_b[1][:, :]),
                        (normg[0], norm_g[0][:, :]), (normg[1], norm_g[1][:, :]),
                        (lamv[0:64, :], lam_in[:, :])]),
          [], ["c3s", "adab", "normg", "lamv"], key="c2", n_dma=6)
    scb = sc.bitcast(BF16)[:, 0:24]
    P.act(act_fn(scb, c3s, AF.Silu, bias=zero_c), ["c3s", "zero_c"], ["sc"])
    sc_v = v3(scb, 3)
    P.dve(tt_fn(lamp[0:64, 0:1], lamv[0:64, 0:1], lamv[0:64, 1:2], ALU.mult), ["lamv"], ["lamp0"])
    P.dve(tt_fn(lamp[0:64, 1:2], lamv[0:64, 2:3], lamv[0:64, 3:4], ALU.mult), ["lamv"], ["lamp1"])
    P.pe(mm_group([(ps[2][:, 0:2], onesf[0:64, :], lamp[0:64, :], True, True)]), ["lamp0", "lamp1", "onesf"], ["ps2"])
    P.act(act_fn(e12, ps[2][:, 0:2], AF.Exp, bias=zero_c), ["ps2", "zero_c"], ["e12"])
    P.dve(stt_fn(neglam, e12[:, 1:2], -LAMBDA_INIT1, e12[:, 0:1], ALU.add, ALU.subtract), ["e12"], ["neglam"])
    def load_x(s):
        xr = xT_in[s].rearrange("(k p) t -> p k t", p=128)
        P.dma("sp", dma_fn([(xT_v[:, :, 0:LC], ctxT_in[s].rearrange("(k p) t -> p k t", p=128))]),
              [], [X(k, 0) for k in range(8)], key="xload0")
        for ft in range(1, 5):
            t0, n = FT[ft]
            P.dma("sp", dma_fn([(xT_v[:, :, t0:t0 + n], xr[:, :, t0 - LC:t0 - LC + n])]), [], [X(k, ft) for k in range(8)], key=f"xload{ft}")

    load_x(0)
    _wa = Arena(arena_t[:, :], AR_BYTES)
    _wa.reset(work_mark)
    adas = [v3(_wa.alloc((9216,), BF16), 1152) for _ in range(2)]
    for l in range(2):
        awr = ada_w[l].rearrange("(k p) n -> p k n", p=128)
        for blk in range(8):
            s_ = blk % 2
            P.dma("pool", dma_fn([(adas[s_], awr[:, :, blk * 1152:(blk + 1) * 1152])]), [], [f"adas{s_}"], key=f"adas{s_}")
            args = []
            for j in range(9):
                jj = blk * 9 + j
                for k in range(8):
                    args.append((ps[l][:, jj * 3:(jj + 1) * 3], adas[s_][:, k, j * 128:(j + 1) * 128], sc_v[:, k, :], k == 0, k == 7))
            P.pe(mm_group(args), [f"adas{s_}", "sc"], [f"ps{l}"])
        P.dve(tt_fn(mod_v[l], v3(ps[l][:, 0:216], 3), adab[l].unsqueeze(2).broadcast_to([128, 72, 3]), ALU.add),
              [f"ps{l}", "adab"], [f"mod{l}"])
        for i in range(3):
            for col in range(3):
                P.dve(stt_fn(gs[l][i][:, :, col], mod_v[l][:, (3 * i + 1) * 8:(3 * i + 2) * 8, col], 1.0,
                             normg[l][:, i * 8:(i + 1) * 8], ALU.add, ALU.mult), [f"mod{l}", "normg"], [f"gs{l}"])
            P.dve(ts_fn(hgt[l][i], mod_v[l][:, (3 * i + 2) * 8:(3 * i + 3) * 8, :], (1.0 if i == 1 else 0.5), ALU.mult),
                  [f"mod{l}"], [f"hg{l}"])
    P.barrier()

    def norm_to_h(l, i, s, tiles, hT_v, W):
        sq = [[W.alloc((512,), BF16) for _ in range(2)] for _ in range(2)]
        lnv = [W.alloc((512,), F32) for _ in range(2)]
        rstd = [W.alloc((512,), F32) for _ in range(2)]
        tn = [[W.alloc((512,), F32) for _ in range(2)] for _ in range(2)]
        caps = []
        for ti, ft in enumerate(tiles):
            P.capture_begin()
            par = ti % 2
            pbank = ps[6 + par]
            t0, n = FT[ft]
            col = 2 if ft == 0 else s
            for k in range(8):
                b = k % 2
                if k % 2 == 0:
                    P.act(act_fn(sq[par][b][:, :n], xT_v[:, k, t0:t0 + n], AF.Square, bias=zero_c), [X(k, ft)], [f"nsq{par}{b}"])
                else:
                    P.dve(tt_fn(sq[par][b][:, :n], xT_v[:, k, t0:t0 + n], xT_v[:, k, t0:t0 + n], ALU.mult), [X(k, ft)], [f"nsq{par}{b}"])
                P.pe(mm_group([(pbank[:, :n], ones, sq[par][b][:, :n], k == 0, k == 7)]), [f"nsq{par}{b}", "ones"], [f"ps{6 + par}"])
            P.act(act_fn(lnv[par][:, :n], pbank[:, :n], AF.Ln, bias=eps_c, scale=1.0 / D), [f"ps{6 + par}", "eps_c"], [f"nlnv{par}"])
            P.act(act_fn(rstd[par][:, :n], lnv[par][:, :n], AF.Exp, bias=zero_c, scale=-0.5), [f"nlnv{par}"], [f"nrstd{par}"])
            for k in range(8):
                b = k % 2
                P.dve(tt_fn(tn[par][b][:, :n], xT_v[:, k, t0:t0 + n], rstd[par][:, :n], ALU.mult), [X(k, ft), f"nrstd{par}"], [f"ntn{par}{b}"])
                P.act(act_fn(hT_v[:, k, t0:t0 + n], tn[par][b][:, :n], AF.Identity,
                             bias=mod_v[l][:, 3 * i * 8 + k, col:col + 1], scale=gs[l][i][:, k, col:col + 1]),
                      [f"ntn{par}{b}", f"mod{l}", f"gs{l}"], [H(k, ft)])
            caps.append(P.capture_end())
        for i_ in range(0, len(caps), 2):
            P.replay_zip(caps[i_:i_ + 2])

    FFN_GROUPS = [(0, 4), (4, 4), (8, 4), (12, 4), (16, 4), (20, 2)]

    def wl_ffn(l, f, g):
        def rec(s_):
            j0, ncn = FFN_GROUPS[g]
            wgr = wg[l][f].rearrange("(k p) n -> p k n", p=128)
            wur = wu[l][f].rearrange("(k p) n -> p k n", p=128)
            wdr = wd[l][f].rearrange("(j p) n -> p j n", p=128)
            wgs = v3(ws[s_][:, 0:4096], 512)
            wus = v3(ws[s_][:, 4096:8192], 512)
            wds = v3(ws[s_][:, 8192:12288], 1024)
            P.dma("pool", dma_fn([(wgs[:, :, 0:ncn * 128], wgr[:, :, j0 * 128:(j0 + ncn) * 128]),
                                  (wus[:, :, 0:ncn * 128], wur[:, :, j0 * 128:(j0 + ncn) * 128]),
                                  (wds[:, 0:ncn, :], wdr[:, j0:j0 + ncn, :])]),
                  [], [f"ws{s_}"], key=f"ws{s_}", n_dma=3)
        return rec

    def wl_diff(hg):
        def rec(s_):
            w1r = w_in1.rearrange("(k p) n -> p k n", p=128)
            wv = v3(ws[s_][:, 0:12288], 1536)
            P.dma("pool", dma_fn([(wv[:, :, 0:512], w1r[:, :, hg * 512:(hg + 1) * 512]),
                                  (wv[:, :, 512:1024], w1r[:, :, 1024 + hg * 512:1024 + (hg + 1) * 512]),
                                  (wv[:, :, 1024:1536], w1r[:, :, 2048 + hg * 512:2048 + (hg + 1) * 512])]),
                  [], [f"ws{s_}"], key=f"ws{s_}", n_dma=3)
        return rec

    def wl_wout1():
        def rec(s_):
            P.dma("pool", dma_fn([(v3(ws[s_][:, 0:8192], 1024), w_out[1].rearrange("(h p) n -> p h n", p=128))]), [], [f"ws{s_}"], key=f"ws{s_}")
        return rec

    def wl_mla(hg):
        def rec(s_):
            w0r = w_in0.rearrange("(k p) n -> p k n", p=128)
            win_v = v3(ws[s_][:, 0:3328], 416)
            wqb_v = v3(ws[s_][:, 3328:4096], 384)
            wkvb_v = ws[s_][:, 4096:4608]
            P.dma("pool", dma_fn([(win_v, w0r[:, :, 0:416]),
                                  (wqb_v, wqb_in.rearrange("(j p) n -> p j n", p=128)[:, :, hg * 384:(hg + 1) * 384]),
                                  (wkvb_v, wkvb_in[:, hg * 512:(hg + 1) * 512])]),
                  [], [f"ws{s_}"], key=f"ws{s_}", n_dma=3)
        return rec

    def wl_swa():
        def rec(s_):
            w0r = w_in0.rearrange("(k p) n -> p k n", p=128)
            wsv = v3(ws[s_][:, 0:6144], 768)
            P.dma("pool", dma_fn([(wsv, w0r[:, :, 416:1184])]), [], [f"ws{s_}"], key=f"ws{s_}")
        return rec

    def wl_wout0():
        def rec(s_):
            wo = v3(ws[s_][:, 0:8192], 1024)
            P.dma("pool", dma_fn([(wo[:, 0:4, :], w_out[0][0:512, :].rearrange("(h p) n -> p h n", p=128)),
                                  (wo[0:64, 4:8, :], w_out[0][512:768, :].rearrange("(h p) n -> p h n", p=64)),
                                  (wo[64:128, 4:8, :], w_out[0][768:1024, :].rearrange("(h p) n -> p h n", p=64))]),
                  [], [f"ws{s_}"], key=f"ws{s_}", n_dma=3)
        return rec

    def ffn(l, f, s, tiles):
        i = 0 if f == 0 else 2
        W = Arena(arena_t[:, :], AR_BYTES)
        W.reset(work_mark)
        hT = W.alloc((8 * T,), BF16)
        hT_v = v3(hT, T)
        sg = [W.alloc((512,), F32) for _ in range(2)]
        Ab = [[W.alloc((512,), BF16) for _ in range(4)] for _ in range(2)]
        norm_to_h(l, i, s, tiles, hT_v, W)
        groups = FFN_GROUPS
        ycnt = [0]
        for g in range(len(groups)):
            j0, ncn = groups[g]
            s_ = next_slot()
            wgs = v3(ws[s_][:, 0:4096], 512)
            wus = v3(ws[s_][:, 4096:8192], 512)
            wds = v3(ws[s_][:, 8192:12288], 1024)

            def GU(ft, ab):
                t0, n = FT[ft]
                for jj in range(ncn):
                    gb = jj % 2
                    hreads = [H(k, ft) for k in range(8)]
                    P.pe(mm_group([(ps[gb][:, :n], wgs[:, k, jj * 128:(jj + 1) * 128], hT_v[:, k, t0:t0 + n], k == 0, k == 7) for k in range(8)]),
                         [f"ws{s_}"] + hreads, [f"ps{gb}"])
                    P.pe(mm_group([(ps[2 + gb][:, :n], wus[:, k, jj * 128:(jj + 1) * 128], hT_v[:, k, t0:t0 + n], k == 0, k == 7) for k in range(8)]),
                         [f"ws{s_}"] + hreads, [f"ps{2 + gb}"])
                    P.act(act_fn(sg[gb][:, :n], ps[gb][:, :n], AF.Silu, bias=zero_c), [f"ps{gb}"], [f"sg{gb}"])
                    P.dve(tt_fn(Ab[ab][jj][:, :n], sg[gb][:, :n], ps[2 + gb][:, :n], ALU.mult), [f"sg{gb}", f"ps{2 + gb}"], [f"A{ab}.{jj}"])

            def YD(ft, ab):
                t0, n = FT[ft]
                col = 2 if ft == 0 else s
                for c in range(8):
                    yb = 4 + ycnt[0] % 3
                    ycnt[0] += 1
                    P.pe(mm_group([(ps[yb][:, :n], wds[:, jj, c * 128:(c + 1) * 128], Ab[ab][jj][:, :n], jj == 0, jj == ncn - 1) for jj in range(ncn)]),
                         [f"ws{s_}"] + [f"A{ab}.{jj}" for jj in range(ncn)], [f"ps{yb}"])
                    P.dve(stt_fn(xT_v[:, c, t0:t0 + n], ps[yb][:, :n], hgt[l][i][:, c, col:col + 1], xT_v[:, c, t0:t0 + n], ALU.mult, ALU.add),
                          [f"ps{yb}", X(c, ft), f"hg{l}"], [X(c, ft)])

            for idx, ft in enumerate(tiles):
                GU(ft, idx % 2)
                if idx > 0:
                    YD(tiles[idx - 1], (idx - 1) % 2)
            YD(tiles[-1], (len(tiles) - 1) % 2)

    def spill_x(tiles):
        xs = v3(xspill, T)
        reads = [X(k, ft) for k in range(8) for ft in range(5)]
        P.dma("sp", dma_fn([(xspill[:, :], xT)]), reads, ["xspill"], key="spill")

    def wout_phase(l, s, tiles, aT_v, a_names, nchunk, wo_load):
        xs = v3(xspill, T)
        s_ = wo_load()
        wo = v3(ws[s_][:, 0:8192], 1024)
        yc = 0
        for ft in tiles:
            t0, n = FT[ft]
            col = 2 if ft == 0 else s
            P.dma("sp", dma_fn([(xT_v[:, :, t0:t0 + n], xs[:, :, t0:t0 + n])]), ["xspill"], [X(k, ft) for k in range(8)], key=f"xrl{ft}")
            for c in range(8):
                yb = 4 + yc % 3
                yc += 1
                P.pe(mm_group([(ps[yb][:, :n], wo[:, h, c * 128:(c + 1) * 128], aT_v[:, h, t0:t0 + n], h == 0, h == nchunk - 1) for h in range(nchunk)]),
                     [f"ws{s_}"] + a_names(ft), [f"ps{yb}"])
                P.dve(stt_fn(xT_v[:, c, t0:t0 + n], ps[yb][:, :n], hgt[l][1][:, c, col:col + 1], xT_v[:, c, t0:t0 + n], ALU.mult, ALU.add),
                      [f"ps{yb}", X(c, ft), f"hg{l}"], [X(c, ft)])

    def mixer1(s):
        l = 1
        P.barrier()
        W = Arena(arena_t[:, :], AR_BYTES)
        W.reset(work_mark)
        hT_v = v3(W.alloc((8 * T,), BF16), T)
        wmark = W.mark()
        norm_to_h(l, 1, s, range(5), hT_v, W)
        spill_x(range(5))
        P.barrier()
        W.reset(wmark)
        aT_v = v3(W.alloc((8 * T,), BF16), T)
        Xa = Arena(arena_t[:, :], AR_BYTES)
        Xa.reset(x_mark)
        xlimit = x_mark + 8 * T * 4
        QT_v = v3(Xa.alloc((4 * S,), BF16), S)
        KT_v = v3(Xa.alloc((4 * T,), BF16), T)
        Vb_v = v3(Xa.alloc((18 * 512,), BF16), 512)
        g1 = W.alloc((G1N,), F32)
        sm_ = [W.alloc((8,), F32) for _ in range(8)]
        P.dma("sp", dma_fn([(g1, gains1[:, :])]), [], ["g1"], key="g1")
        P.dve(ts_fn(sgl, g1[:, 128:129], 1.0 - LAMBDA_INIT1, ALU.mult), ["g1"], ["sgl"])
        tmark = Xa.mark()
        w1r = w_in1.rearrange("(k p) n -> p k n", p=128)
        for hg in range(2):
            Xa.reset(tmark)
            qn_ = [[Xa.alloc((512,), F32) for _ in range(2)] for _ in range(2)]
            qb_ = [[Xa.alloc((512,), BF16) for _ in range(2)] for _ in range(2)]
            U_ = [[Xa.alloc((512,), F32) for _ in range(2)] for _ in range(2)]
            ss8_ = [[sm_[0], sm_[1]], [sm_[2], sm_[3]]]
            rs8_ = [[sm_[4], sm_[5]], [sm_[6], sm_[7]]]
            assert Xa.off <= xlimit, (Xa.off, xlimit)
            s_ = next_slot()
            wv = v3(ws[s_][:, 0:12288], 1536)
            cap_pe, cap_ch, cap_bk = [], [], []
            for tt in range(18):
                lat = tt >= 2
                ft = ft_of_tt(tt)
                tok = tt * 128
                hreads = [H(k, ft) for k in range(8)]
                par = tt % 2
                pb = 3 * par
                jobs = []
                if lat:
                    jobs.append((0, pb + 0, 0))
                jobs.append((1, pb + 1, 512))
                P.capture_begin()
                for (which, bank, coff) in jobs + [(2, pb + 2, 1024)]:
                    P.pe(mm_group([(ps[bank][:, :], hT_v[:, k, tok:tok + 128], wv[:, k, coff:coff + 512], k == 0, k == 7) for k in range(8)]),
                         [f"ws{s_}"] + hreads, [f"ps{bank}"])
                P.act(act_fn(Vb_v[:, tt, :], ps[pb + 2][:, :], AF.Identity, bias=zero_c), [f"ps{pb + 2}"], [f"V.{tt}"])
                cap_pe.append(P.capture_end())
                chains = []
                backs = []
                for (which, bank, coff) in jobs:
                    P.capture_begin()
                    tg = f"{par}{which}"
                    qn, qb, ss8, rs8 = qn_[par][which], qb_[par][which], ss8_[par][which], rs8_[par][which]
                    gt = g1[:, which * 64:(which + 1) * 64].unsqueeze(1).broadcast_to([128, 8, 64])
                    q3 = v3(qn, 64)
                    P.act(act_fn(qn, ps[bank][:, :], AF.Square, bias=zero_c), [f"ps{bank}"], [f"qn{tg}"])
                    P.dve(red_fn(ss8, q3), [f"qn{tg}"], [f"ss8{tg}"])
                    P.act(act_fn(rs8, ss8, AF.Ln, bias=eps_c, scale=1.0 / 64), [f"ss8{tg}"], [f"rs8{tg}"])
                    P.act(act_fn(rs8, rs8, AF.Exp, bias=zero_c, scale=-0.5), [f"rs8{tg}"], [f"rs8{tg}"])
                    P.dve(tt_fn(q3, v3(ps[bank][:, :], 64), rs8.unsqueeze(2).broadcast_to([128, 8, 64]), ALU.mult),
                          [f"ps{bank}", f"rs8{tg}"], [f"qn{tg}"])
                    if lat:
                        j = tt - 2
                        cosb = ropeB_v[:, j, 0:32].unsqueeze(1).broadcast_to([128, 8, 32])
                        sinb = ropeB_v[:, j, 32:64].unsqueeze(1).broadcast_to([128, 8, 32])
                        P.dve(tt_fn(q3, q3, gt, ALU.mult), [f"qn{tg}", "g1"], [f"qn{tg}"])
                        U = U_[par][which]
                        q4 = qn.rearrange("p (g h d) -> p g h d", h=2, d=32)
                        u4 = U.rearrange("p (g h d) -> p g h d", h=2, d=32)
                        cos4 = ropeB_v[:, j, 0:32].unsqueeze(1).unsqueeze(2).broadcast_to([128, 8, 2, 32])
                        sin4 = ropeB_v[:, j, 32:64].unsqueeze(1).unsqueeze(2).broadcast_to([128, 8, 2, 32])
                        u3 = v3(U, 64)
                        qb3 = v3(qb, 64)
                        P.dve(tt_fn(u4, q4, sin4, ALU.mult), [f"qn{tg}", "ropeB"], [f"U{tg}"])
                        P.dve(tt_fn(q4, q4, cos4, ALU.mult), [f"qn{tg}", "ropeB"], [f"qn{tg}"])
                        P.dve(tt_fn(qb3[:, :, 0:32], q3[:, :, 0:32], u3[:, :, 32:64], ALU.subtract), [f"qn{tg}", f"U{tg}"], [f"qb{tg}a"])
                        P.dve(tt_fn(qb3[:, :, 32:64], u3[:, :, 0:32], q3[:, :, 32:64], ALU.add), [f"qn{tg}", f"U{tg}"], [f"qb{tg}b"])
                    else:
                        P.dve(tt_fn(v3(qb, 64), q3, gt, ALU.mult), [f"qn{tg}", "g1"], [f"qb{tg}a", f"qb{tg}b"])
                    tb = 6 + par
                    tcol = which * 512
                    chains.append(P.capture_end())
                    P.capture_begin()
                    P.pe(tr_group([(psb[tb][:, tcol + h * 128:tcol + (h + 1) * 128], qb[:, h * 128:(h + 1) * 128], ident) for h in range(4)]),
                         [f"qb{tg}a", f"qb{tg}b", "ident"], [f"ps{tb}"])
                    if which == 0:
                        dst = QT_v[:, :, tok - LC:tok - LC + 128]
                        nm = f"QT.{tt}"
                    else:
                        dst = KT_v[:, :, tok:tok + 128]
                        nm = f"KT.{tt}"
                    P.act(act_fn(dst, v3(psb[tb][:, tcol:tcol + 512], 128), AF.Identity, bias=zero_c), [f"ps{tb}"], [nm])
                    backs.append(P.capture_end())
                cap_ch.append(chains)
                cap_bk.append(backs)
            P.replay_zip(cap_pe[0:2])
            for i_ in range(0, 18, 2):
                P.replay_zip(cap_ch[i_] + cap_ch[i_ + 1])
                if i_ + 2 < 18:
                    P.replay_zip(cap_pe[i_ + 2:i_ + 4])
                P.replay_zip(cap_bk[i_] + cap_bk[i_ + 1])
            P.barrier()
            Xa.reset(tmark)
            E = [Xa.alloc((512,), BF16) for _ in range(4)]
            r_ = [Xa.alloc((512,), F32) for _ in range(2)]
            t_ = [Xa.alloc((512,), F32) for _ in range(2)]
            osq = Xa.alloc((512,), BF16)
            rs = r_[1]
            Qp = [[Xa.alloc((512,), BF16) for _ in range(2)] for _ in range(2)]
            assert Xa.off <= xlimit, (Xa.off, xlimit)
            for ub in range(2):
                for m in range(2):
                    P.pool(lambda e, a=Qp[ub][m]: e.memset(a, 0.0), [], [f"Qp{ub}"])
            cnts = {"s": 0, "e": 0}
            items = [(hh, qt, kc, m) for hh in range(4) for qt in range(4) for kc in range(18) for m in range(2)]
            deferred = []

            def s1(it):
                hh, qt, kc, m = it
                q0 = qt * 512
                qreads = [f"QT.{2 + 4 * qt + i_}" for i_ in range(4)]
                sb = cnts["s"] % 3
                cnts["s"] += 1
                eb = cnts["e"] % 4
                cnts["e"] += 1
                ub = (hh * 4 + qt) % 2
                if kc == 0 and m == 0:
                    for mm_ in range(2):
                        P.pool(copy_fn(Qp[ub][mm_][mm_ * 64:(mm_ + 1) * 64, :], QT_v[mm_ * 64:(mm_ + 1) * 64, hh, q0:q0 + 512]),
                               qreads, [f"Qp{ub}"])
                P.pe(mm_group([(ps[sb][:, :], KT_v[:, hh, kc * 128:(kc + 1) * 128], Qp[ub][m], True, True)]),
                     [f"KT.{kc}", f"Qp{ub}"], [f"ps{sb}"])
                P.act(act_fn(E[eb], ps[sb][:, :], AF.Exp, bias=zero_c, scale=0.125), [f"ps{sb}"], [f"E{eb}"])
                return eb

            def part_b(hh, qt):
                h = hg * 4 + hh
                q0 = qt * 512
                P.pe(mm_group([(ps[7][:, :], ones, osq, True, True)]), ["osq", "ones"], ["ps7"])
                P.act(act_fn(rs, ps[7][:, :], AF.Ln, bias=eps_c, scale=1.0 / 128), ["ps7"], ["r1"])
                P.act(act_fn(rs, rs, AF.Exp, bias=zero_c, scale=-0.5), ["r1"], ["r1"])
                P.dve(stt_fn(aT_v[:, h, LC + q0:LC + q0 + 512], t_[0], sgl, rs, ALU.mult, ALU.mult), ["t0", "sgl", "r1"], [f"aT.{h}.{1 + qt}"])

            def s2(it, eb):
                hh, qt, kc, m = it
                P.pe(mm_group([(ps[3 + m][:, :], Vb_v[:, kc, hh * 128:(hh + 1) * 128], E[eb], kc == 0, kc == 17),
                               (ps[5 + m][:, :], ones, E[eb], kc == 0, kc == 17)]),
                     [f"V.{kc}", f"E{eb}", "ones"], [f"ps{3 + m}", f"ps{5 + m}"])
                if kc == 17 and m == 1:
                    for mm_ in range(2):
                        P.dve(copy_fn(t_[mm_], ps[3 + mm_][:, :]), [f"ps{3 + mm_}"], [f"t{mm_}"])
                        P.dve(copy_fn(r_[mm_], ps[5 + mm_][:, :]), [f"ps{5 + mm_}"], [f"r{mm_}"])
                    for mm_ in range(2):
                        P.dve(recip_fn(r_[mm_], r_[mm_]), [f"r{mm_}"], [f"r{mm_}"])
                        P.dve(tt_fn(t_[mm_], t_[mm_], r_[mm_], ALU.mult), [f"t{mm_}", f"r{mm_}"], [f"t{mm_}"])
                    P.dve(stt_fn(t_[0], t_[1], neglam, t_[0], ALU.mult, ALU.add), ["t0", "t1", "neglam"], ["t0"])
                    P.pool(tt_fn(osq, t_[0], t_[0], ALU.mult), ["t0"], ["osq"])
                    deferred.append([30, hh, qt])

            def tick(i):
                for d_ in list(deferred):
                    d_[0] -= 1
                    if d_[0] <= 0:
                        deferred.remove(d_)
                        part_b(d_[1], d_[2])
            run_pipe(items, s1, s2, lag=2, tick=tick)
            for d_ in deferred:
                part_b(d_[1], d_[2])
            P.barrier()

        def wo_load():
            return next_slot()
        wout_phase(l, s, [1, 2, 3, 4], aT_v, lambda ft: [f"aT.{h}.{ft}" for h in range(8)], 8, wo_load)
        P.barrier()

    def mixer0(s):
        l = 0
        P.barrier()
        W = Arena(arena_t[:, :], AR_BYTES)
        W.reset(work_mark)
        hT_v = v3(W.alloc((8 * T,), BF16), T)
        wmark = W.mark()
        norm_to_h(l, 1, s, range(5), hT_v, W)
        spill_x(range(5))
        P.barrier()
        W.reset(wmark)
        aT_v = v3(W.alloc((8 * T,), BF16), T)
        Xa = Arena(arena_t[:, :], AR_BYTES)
        Xa.reset(x_mark)
        xlimit = x_mark + 8 * T * 4
        slot_b = (wl_use[0] + 3) % 2
        g0 = ws[slot_b][:, 8192:12288].bitcast(F32)[:, 0:G0N]
        P.dma("sp", dma_fn([(g0, gains0[:, :])]), [], ["g0"], key="g0")
        gmark = Xa.mark()
        w0r = w_in0.rearrange("(k p) n -> p k n", p=128)
        SC_MLA = 96.0 ** -0.5

        def rstd_small(dst, src, inv_n):
            P.act(act_fn(dst, src, AF.Ln, bias=eps_c, scale=inv_n), ["eps_c"], [])
            P.act(act_fn(dst, dst, AF.Exp, bias=zero_c, scale=-0.5), [], [])

        for hg in range(2):
            Xa.reset(gmark)
            QT_v = v3(Xa.alloc((4 * T,), BF16), T)
            KT_v = v3(Xa.alloc((4 * T,), BF16), T)
            Vb = Xa.alloc((18 * 512,), BF16)
            Vb_v = v3(Vb, 512)
            P.dve(lambda e, a=v3(Vb, 128)[:, :, 64:128]: e.memset(a, 1.0), [], ["Vones"])
            tmark = Xa.mark()
            tsets = []
            for _ in range(2):
                d_ = dict(ssq=Xa.alloc((4,), F32), rsl=Xa.alloc((2,), F32), lat_b=Xa.alloc((384,), BF16), krg=Xa.alloc((32,), F32),
                          krr=Xa.alloc((32,), F32), ra=Xa.alloc((64,), F32), rb=Xa.alloc((64,), F32), latT=Xa.alloc((384,), BF16),
                          sqv=Xa.alloc((512,), F32), qn=Xa.alloc((384,), F32), tq=Xa.alloc((128,), F32), qbb=Xa.alloc((384,), BF16),
                          kn=Xa.alloc((256,), F32), kbb=Xa.alloc((384,), BF16), ss4=Xa.alloc((4,), F32), rq4=Xa.alloc((4,), F32),
                          ss4k=Xa.alloc((4,), F32), rk4=Xa.alloc((4,), F32))
                d_["junk"] = d_["sqv"]
                tsets.append(d_)
            assert Xa.off <= xlimit, (Xa.off, xlimit)
            s_ = next_slot()
            win_v = v3(ws[s_][:, 0:3328], 416)
            wqb_v = v3(ws[s_][:, 3328:4096], 384)
            wkvb_v = ws[s_][:, 4096:4608]
            caps0, capsq, capsk, capsqb, capskb = [], [], [], [], []
            for tt in range(18):
                P.capture_begin()
                lat = tt >= 2
                ft = ft_of_tt(tt)
                tok = tt * 128
                pb = tt % 2
                par = pb
                D_ = tsets[par]
                junk, ssq, rsl, lat_b, krg, krr, ra, rb, latT = D_["junk"], D_["ssq"], D_["rsl"], D_["lat_b"], D_["krg"], D_["krr"], D_["ra"], D_["rb"], D_["latT"]
                sqv, qn, tq, qbb, kn, kbb, ss4, rq4, ss4k, rk4 = D_["sqv"], D_["qn"], D_["tq"], D_["qbb"], D_["kn"], D_["kbb"], D_["ss4"], D_["rq4"], D_["ss4k"], D_["rk4"]
                psq = ps[par]
                pskv = ps[2 + par]
                Pp = ps[pb]
                P.pe(mm_group([(Pp[:, 0:416], hT_v[:, k, tok:tok + 128], win_v[:, k, :], k == 0, k == 7) for k in range(8)]),
                     [f"ws{s_}"] + [H(k, ft) for k in range(8)], [f"ps{pb}"])
                P.act(act_fn(junk[:, 0:256], Pp[:, 0:256], AF.Square, bias=zero_c, accum_out=ssq[:, 0:1]), [f"ps{pb}"], [f"sqv_{par}", f"ssq0_{par}"])
                P.act(act_fn(junk[:, 0:128], Pp[:, 256:384], AF.Square, bias=zero_c, accum_out=ssq[:, 1:2]), [f"ps{pb}"], [f"sqv_{par}", f"ssq1_{par}"])
                P.act(act_fn(junk[:, 0:32], Pp[:, 384:416], AF.Square, bias=zero_c, accum_out=ssq[:, 2:3]), [f"ps{pb}"], [f"sqv_{par}", f"ssq2_{par}"])
                P.act(act_fn(rsl[:, 0:1], ssq[:, 0:1], AF.Ln, bias=eps_c, scale=1.0 / 256), [f"ssq0_{par}"], [f"rsl0_{par}"])
                P.act(act_fn(rsl[:, 1:2], ssq[:, 1:2], AF.Ln, bias=eps_c, scale=1.0 / 128), [f"ssq1_{par}"], [f"rsl1_{par}"])
                P.act(act_fn(rsl, rsl, AF.Exp, bias=zero_c, scale=-0.5), [f"rsl0_{par}", f"rsl1_{par}"], [f"rsl0_{par}", f"rsl1_{par}"])
                P.dve(stt_fn(lat_b[:, 0:256], Pp[:, 0:256], rsl[:, 0:1], g0[:, G0_QA:G0_QA + 256], ALU.mult, ALU.mult),
                      [f"ps{pb}", f"rsl0_{par}", "g0"], [f"latb0_{par}"])
                P.dve(stt_fn(lat_b[:, 256:384], Pp[:, 256:384], rsl[:, 1:2], g0[:, G0_KVA:G0_KVA + 128], ALU.mult, ALU.mult),
                      [f"ps{pb}", f"rsl1_{par}", "g0"], [f"latb1_{par}"])
                P.dve(tt_fn(krg, Pp[:, 384:416], g0[:, G0_KR:G0_KR + 32], ALU.mult), [f"ps{pb}", "g0"], [f"krg_{par}"])
                if lat:
                    j = tt - 2
                    cs = ropeA_v[:, j, 0:16]
                    sn = ropeA_v[:, j, 16:32]
                    P.dve(tt_fn(ra[:, 0:16], krg[:, 0:16], cs, ALU.mult), [f"krg_{par}", "ropeA"], [f"ra_{par}"])
                    P.dve(tt_fn(rb[:, 0:16], krg[:, 16:32], sn, ALU.mult), [f"krg_{par}", "ropeA"], [f"rb_{par}"])
                    P.dve(tt_fn(krr[:, 0:16], ra[:, 0:16], rb[:, 0:16], ALU.subtract), [f"ra_{par}", f"rb_{par}"], [f"krr0_{par}"])
                    P.dve(tt_fn(ra[:, 0:16], krg[:, 0:16], sn, ALU.mult), [f"krg_{par}", "ropeA"], [f"ra_{par}"])
                    P.dve(tt_fn(rb[:, 0:16], krg[:, 16:32], cs, ALU.mult), [f"krg_{par}", "ropeA"], [f"rb_{par}"])
                    P.dve(tt_fn(krr[:, 16:32], ra[:, 0:16], rb[:, 0:16], ALU.add), [f"ra_{par}", f"rb_{par}"], [f"krr1_{par}"])
                else:
                    P.dve(copy_fn(krr, krg), [f"krg_{par}"], [f"krr0_{par}", f"krr1_{par}"])
                P.pe(tr_group([(psb[4 + par][:, jb * 128:(jb + 1) * 128], lat_b[:, jb * 128:(jb + 1) * 128], ident) for jb in range(3)]),
                     [f"latb0_{par}", f"latb1_{par}", "ident"], [f"ps{4 + par}"])
                P.act(act_fn(latT, psb[4 + par][:, 0:384], AF.Identity, bias=zero_c), [f"ps{4 + par}"], [f"latT_{par}"])
                latT_v = v3(latT, 128)
                P.pe(mm_group([(psq[:, 0:384], latT_v[:, jb, :], wqb_v[:, jb, :], jb == 0, jb == 1) for jb in range(2)]),
                     [f"latT_{par}", f"ws{s_}"], [f"ps{par}"])
                P.pe(mm_group([(pskv[:, 0:512], latT_v[:, 2, :], wkvb_v, True, True)]), [f"latT_{par}", f"ws{s_}"], [f"ps{2 + par}"])
                caps0.append(P.capture_end())
                P.capture_begin()
                P.act(act_fn(qn, psq[:, 0:384], AF.Square, bias=zero_c), [f"ps{par}"], [f"qn_{par}"])
                P.dve(red_fn(ss4, v3(qn, 96)), [f"qn_{par}"], [f"ss4_{par}"])
                P.act(act_fn(rq4, ss4, AF.Ln, bias=eps_c, scale=1.0 / 96), [f"ss4_{par}"], [f"rq4_{par}"])
                P.act(act_fn(rq4, rq4, AF.Exp, bias=zero_c, scale=-0.5), [f"rq4_{par}"], [f"rq4_{par}"])
                q3 = v3(qn, 96)
                P.dve(tt_fn(q3, v3(psq[:, 0:384], 96), rq4.unsqueeze(2).broadcast_to([128, 4, 96]), ALU.mult), [f"ps{par}", f"rq4_{par}"], [f"qn_{par}"])
                gq3 = v3(g0[:, G0_Q4:G0_Q4 + 384], 96)
                qb3 = v3(qbb, 96)
                if lat:
                    P.dve(tt_fn(qb3[:, :, 0:64], q3[:, :, 0:64], gq3[:, :, 0:64], ALU.mult), [f"qn_{par}", "g0"], [f"qbb0_{par}"])
                    tq3 = v3(tq, 32)
                    P.dve(tt_fn(tq3, q3[:, :, 64:96], gq3[:, :, 64:96], ALU.mult), [f"qn_{par}", "g0"], [f"tq_{par}"])
                    cs4 = ropeA_v[:, j, 0:16].unsqueeze(1).broadcast_to([128, 4, 16])
                    sn4 = ropeA_v[:, j, 16:32].unsqueeze(1).broadcast_to([128, 4, 16])
                    ra3 = v3(ra, 16)
                    rb3 = v3(rb, 16)
                    P.dve(tt_fn(ra3, tq3[:, :, 0:16], cs4, ALU.mult), [f"tq_{par}", "ropeA"], [f"ra_{par}"])
                    P.dve(tt_fn(rb3, tq3[:, :, 16:32], sn4, ALU.mult), [f"tq_{par}", "ropeA"], [f"rb_{par}"])
                    P.dve(tt_fn(qb3[:, :, 64:80], ra3, rb3, ALU.subtract), [f"ra_{par}", f"rb_{par}"], [f"qbb1_{par}"])
                    P.dve(tt_fn(ra3, tq3[:, :, 0:16], sn4, ALU.mult), [f"tq_{par}", "ropeA"], [f"ra_{par}"])
                    P.dve(tt_fn(rb3, tq3[:, :, 16:32], cs4, ALU.mult), [f"tq_{par}", "ropeA"], [f"rb_{par}"])
                    P.dve(tt_fn(qb3[:, :, 80:96], ra3, rb3, ALU.add), [f"ra_{par}", f"rb_{par}"], [f"qbb2_{par}"])
                else:
                    P.dve(tt_fn(qb3, q3, gq3, ALU.mult), [f"qn_{par}", "g0"], [f"qbb0_{par}", f"qbb1_{par}", f"qbb2_{par}"])
                capsq.append(P.capture_end())
                P.capture_begin()
                P.pe(tr_group([(psb[6 + par][0:96, hh * 128:(hh + 1) * 128], qbb[:, hh * 96:(hh + 1) * 96], ident) for hh in range(4)]),
                     [f"qbb0_{par}", f"qbb1_{par}", f"qbb2_{par}", "ident"], [f"ps{6 + par}"])
                P.act(act_fn(QT_v[0:96, :, tok:tok + 128], v3(psb[6 + par][0:96, 0:512], 128), AF.Identity, bias=zero_c[0:96, :]), [f"ps{6 + par}"], [f"QT.{tt}"])
                capsqb.append(P.capture_end())
                P.capture_begin()
                kv3 = v3(pskv[:, 0:512], 128)
                P.act(act_fn(sqv, pskv[:, 0:512], AF.Square, bias=zero_c), [f"ps{2 + par}"], [f"sqv_{par}"])
                P.dve(red_fn(ss4k, v3(sqv, 128)[:, :, 0:64]), [f"sqv_{par}"], [f"ss4k_{par}"])
                P.dve(ts_fn(ss4k, ss4k, ssq[:, 2:3], ALU.add), [f"ss4k_{par}", f"ssq2_{par}"], [f"ss4k_{par}"])
                P.act(act_fn(rk4, ss4k, AF.Ln, bias=eps_c, scale=1.0 / 96), [f"ss4k_{par}"], [f"rk4_{par}"])
                P.act(act_fn(rk4, rk4, AF.Exp, bias=zero_c, scale=-0.5), [f"rk4_{par}"], [f"rk4_{par}"])
                kn3 = v3(kn, 64)
                kb3 = v3(kbb, 96)
                P.dve(tt_fn(kn3, kv3[:, :, 0:64], rk4.unsqueeze(2).broadcast_to([128, 4, 64]), ALU.mult), [f"ps{2 + par}", f"rk4_{par}"], [f"kn_{par}"])
                P.dve(tt_fn(kb3[:, :, 0:64], kn3, v3(g0[:, G0_KN4:G0_KN4 + 256], 64), ALU.mult), [f"kn_{par}", "g0"], [f"kbb0_{par}"])
                P.dve(tt_fn(kb3[:, :, 64:96], krr.unsqueeze(1).broadcast_to([128, 4, 32]), rk4.unsqueeze(2).broadcast_to([128, 4, 32]), ALU.mult),
                      [f"krr0_{par}", f"krr1_{par}", f"rk4_{par}"], [f"kbb1_{par}"])
                P.act(act_fn(v3(Vb_v[:, tt, :], 128)[:, :, 0:64], kv3[:, :, 64:128], AF.Identity, bias=zero_c), [f"ps{2 + par}", f"kn_{par}"], [f"V.{tt}"])
                capsk.append(P.capture_end())
                P.capture_begin()
                P.pe(tr_group([(psb[4 + par][0:96, 512 + hh * 128:512 + (hh + 1) * 128], kbb[:, hh * 96:(hh + 1) * 96], ident) for hh in range(4)]),
                     [f"kbb0_{par}", f"kbb1_{par}", "ident"], [f"ps{4 + par}"])
                P.act(act_fn(KT_v[0:96, :, tok:tok + 128], v3(psb[4 + par][0:96, 512:1024], 128), AF.Identity, bias=zero_c[0:96, :]), [f"ps{4 + par}"], [f"KT.{tt}"])
                capskb.append(P.capture_end())
            P.replay_zip(caps0[0:2])
            for i_ in range(0, 18, 2):
                P.replay_zip(capsq[i_:i_ + 2] + capsk[i_:i_ + 2])
                if i_ + 2 < 18:
                    P.replay_zip(caps0[i_ + 2:i_ + 4])
                P.replay_zip(capsqb[i_:i_ + 2] + capskb[i_:i_ + 2])
            P.barrier()
            Xa.reset(tmark)
            E = [Xa.alloc((512,), BF16) for _ in range(4)]
            r_ = [Xa.alloc((512,), F32) for _ in range(2)]
            assert Xa.off <= xlimit, (Xa.off, xlimit)
            cnts = {"s": 0, "e": 0}
            items = []
            uidx = 0
            for hh in range(4):
                for ft in range(5):
                    kcs = [0, 1] if ft == 0 else list(range(18))
                    for ki, kc in enumerate(kcs):
                        items.append((hh, ft, kc, ki == 0, ki == len(kcs) - 1, uidx))
                    uidx += 1

            def s1(it):
                hh, ft, kc, first, last, u = it
                t0, n = FT[ft]
                qreads = [f"QT.{tt}" for tt in range(t0 // 128, (t0 + n) // 128)]
                sb = cnts["s"] % 3
                cnts["s"] += 1
                eb = cnts["e"] % 4
                cnts["e"] += 1
                P.pe(mm_group([(ps[sb][:, :n], KT_v[0:96, hh, kc * 128:(kc + 1) * 128], QT_v[0:96, hh, t0:t0 + n], True, True)]),
                     [f"KT.{kc}"] + qreads, [f"ps{sb}"])
                P.act(act_fn(E[eb][:, :n], ps[sb][:, :n], AF.Exp, bias=zero_c, scale=SC_MLA), [f"ps{sb}"], [f"E{eb}"])
                return eb

            def s2(it, eb):
                hh, ft, kc, first, last, u = it
                h = hg * 4 + hh
                t0, n = FT[ft]
                ob = 3 + (u % 4)
                rb_ = u % 2
                P.pe(mm_group([(ps[ob][:, :n], Vb_v[:, kc, hh * 128:(hh + 1) * 128], E[eb][:, :n], first, last)]),
                     [f"V.{kc}", "Vones", f"E{eb}"], [f"ps{ob}"])
                if last:
                    P.dve(recip_fn(r_[rb_][0:64, :n], ps[ob][64:128, :n]), [f"ps{ob}"], [f"r{rb_}"])
                    po = (h % 2) * 64
                    P.dve(tt_fn(aT_v[po:po + 64, h // 2, t0:t0 + n], ps[ob][0:64, :n], r_[rb_][0:64, :n], ALU.mult),
                          [f"ps{ob}", f"r{rb_}"], [f"aT.{h // 2}.{ft}.{h % 2}"])
            run_pipe(items, s1, s2, lag=2)
            P.barrier()

        Xa.reset(gmark)
        sqTp = [Xa.alloc((4 * T,), BF16) for _ in range(2)]
        sqT_v = [v3(a_, 512) for a_ in sqTp]
        skT = Xa.alloc((T,), BF16)
        sV = Xa.alloc((18 * 256,), BF16)
        sV_v = v3(sV, 256)
        P.pool(lambda e, a=sqTp[0]: e.memset(a, 0.0), [], ["sqz0"])
        P.pool(lambda e, a=sqTp[1]: e.memset(a, 0.0), [], ["sqz1"])
        P.dve(lambda e, a=v3(sV, 128)[:, :, 64:128]: e.memset(a, 1.0), [], ["Vones"])
        se = Xa.alloc((1024,), F32)
        sk8 = Xa.alloc((8,), F32)
        tmark = Xa.mark()
        tsets = []
        for _ in range(2):
            tsets.append(dict(qn=Xa.alloc((512,), F32), ra=Xa.alloc((256,), F32), rb=Xa.alloc((256,), F32),
                              sqb=Xa.alloc((512,), BF16), kqn=Xa.alloc((128,), F32), ksq=Xa.alloc((128,), F32), skb=Xa.alloc((128,), BF16),
                              rak=Xa.alloc((64,), F32), rbk=Xa.alloc((64,), F32),
                              ss8=Xa.alloc((8,), F32), rs8=Xa.alloc((8,), F32), ss2=Xa.alloc((2,), F32), rs2=Xa.alloc((2,), F32)))
        assert Xa.off <= xlimit, (Xa.off, xlimit)
        P.act(act_fn(sk8, g0[:, G0_SINK:G0_SINK + 8], AF.Exp, bias=zero_c), ["g0"], ["sk8"])
        P.dve(copy_fn(v3(se, 128), sk8.unsqueeze(2).broadcast_to([128, 8, 128])), ["sk8"], ["se"])
        s_ = next_slot()
        wsv = v3(ws[s_][:, 0:6144], 768)
        cpe, cfq, cfk, cbq, cbk = [], [], [], [], []
        for tt in range(18):
            lat = tt >= 2
            ft = ft_of_tt(tt)
            tok = tt * 128
            pb = tt % 2
            par = pb
            ts_ = tsets[par]
            qn, ra, rb, sqb, kqn, ksq, skb, rak, rbk = (ts_["qn"], ts_["ra"], ts_["rb"], ts_["sqb"], ts_["kqn"], ts_["ksq"], ts_["skb"],
                                                         ts_["rak"], ts_["rbk"])
            ss8, rs8, ss2, rs2 = ts_["ss8"], ts_["rs8"], ts_["ss2"], ts_["rs2"]
            hreads = [H(k, ft) for k in range(8)]
            if lat:
                j = tt - 2
            P.capture_begin()
            P.pe(mm_group([(ps[pb][:, :], hT_v[:, k, tok:tok + 128], wsv[:, k, 0:512], k == 0, k == 7) for k in range(8)]),
                 [f"ws{s_}"] + hreads, [f"ps{pb}"])
            P.pe(mm_group([(ps[2 + pb][:, 0:256], hT_v[:, k, tok:tok + 128], wsv[:, k, 512:768], k == 0, k == 7) for k in range(8)]),
                 [f"ws{s_}"] + hreads, [f"ps{2 + pb}"])
            P.act(act_fn(v3(sV_v[:, tt, :], 128)[:, :, 0:64], v3(ps[2 + pb][:, 128:256], 64), AF.Identity, bias=zero_c), [f"ps{2 + pb}"], [f"V.{tt}"])
            cpe.append(P.capture_end())
            P.capture_begin()
            q3 = v3(qn, 64)
            P.act(act_fn(qn, ps[pb][:, :], AF.Square, bias=zero_c), [f"ps{pb}"], [f"qn_{par}"])
            P.dve(red_fn(ss8, q3), [f"qn_{par}"], [f"ss8_{par}"])
            P.act(act_fn(rs8, ss8, AF.Ln, bias=eps_c, scale=1.0 / 64), [f"ss8_{par}"], [f"rs8_{par}"])
            P.act(act_fn(rs8, rs8, AF.Exp, bias=zero_c, scale=-0.5), [f"rs8_{par}"], [f"rs8_{par}"])
            P.dve(tt_fn(q3, v3(ps[pb][:, :], 64), rs8.unsqueeze(2).broadcast_to([128, 8, 64]), ALU.mult), [f"ps{pb}", f"rs8_{par}"], [f"qn_{par}"])
            gq = g0[:, G0_SQ8:G0_SQ8 + 512]
            qb4 = sqb.rearrange("p (pp hf d) -> p hf pp d", pp=4, hf=2)
            if lat:
                cosb = ropeB_v[:, j, 0:32].unsqueeze(1).broadcast_to([128, 8, 32])
                sinb = ropeB_v[:, j, 32:64].unsqueeze(1).broadcast_to([128, 8, 32])
                P.dve(tt_fn(qn, qn, gq, ALU.mult), [f"qn_{par}", "g0"], [f"qn_{par}"])
                ra3 = v3(ra, 32)
                rb3 = v3(rb, 32)
                ra4 = ra.rearrange("p (hf pp d) -> p hf pp d", hf=2, pp=4)
                rb4 = rb.rearrange("p (hf pp d) -> p hf pp d", hf=2, pp=4)
                P.dve(tt_fn(ra3, q3[:, :, 0:32], cosb, ALU.mult), [f"qn_{par}", "ropeB"], [f"ra_{par}"])
                P.dve(tt_fn(rb3, q3[:, :, 32:64], sinb, ALU.mult), [f"qn_{par}", "ropeB"], [f"rb_{par}"])
                P.dve(tt_fn(qb4[:, :, :, 0:32], ra4, rb4, ALU.subtract), [f"ra_{par}", f"rb_{par}"], [f"sqb0_{par}"])
                P.dve(tt_fn(ra3, q3[:, :, 0:32], sinb, ALU.mult), [f"qn_{par}", "ropeB"], [f"ra_{par}"])
                P.dve(tt_fn(rb3, q3[:, :, 32:64], cosb, ALU.mult), [f"qn_{par}", "ropeB"], [f"rb_{par}"])
                P.dve(tt_fn(qb4[:, :, :, 32:64], ra4, rb4, ALU.add), [f"ra_{par}", f"rb_{par}"], [f"sqb1_{par}"])
            else:
                qn4 = qn.rearrange("p (hf pp d) -> p hf pp d", hf=2, pp=4)
                gq4 = gq.rearrange("p (hf pp d) -> p hf pp d", hf=2, pp=4)
                P.dve(tt_fn(qb4, qn4, gq4, ALU.mult), [f"qn_{par}", "g0"], [f"sqb0_{par}", f"sqb1_{par}"])
            cfq.append(P.capture_end())
            P.capture_begin()
            P.pe(tr_group([(psb[4 + par][:, pp * 128:(pp + 1) * 128], sqb[:, pp * 128:(pp + 1) * 128], ident) for pp in range(4)]),
                 [f"sqb0_{par}", f"sqb1_{par}", "ident"], [f"ps{4 + par}"])
            P.act(act_fn(sqT_v[0][0:64, tt, :], psb[4 + par][0:64, 0:512], AF.Identity, bias=zero_c[0:64, :]), [f"ps{4 + par}", "sqz0"], [f"QT.{tt}.0"])
            P.act(act_fn(sqT_v[1][64:128, tt, :], psb[4 + par][64:128, 0:512], AF.Identity, bias=zero_c[64:128, :]), [f"ps{4 + par}", "sqz1"], [f"QT.{tt}.1"])
            cbq.append(P.capture_end())
            P.capture_begin()
            P.act(act_fn(ksq, ps[2 + pb][:, 0:128], AF.Square, bias=zero_c), [f"ps{2 + pb}"], [f"ksq_{par}"])
            P.dve(red_fn(ss2, v3(ksq, 64)), [f"ksq_{par}"], [f"ss2_{par}"])
            P.act(act_fn(rs2, ss2, AF.Ln, bias=eps_c, scale=1.0 / 64), [f"ss2_{par}"], [f"rs2_{par}"])
            P.act(act_fn(rs2, rs2, AF.Exp, bias=zero_c, scale=-0.5), [f"rs2_{par}"], [f"rs2_{par}"])
            k3 = v3(kqn, 64)
            P.dve(tt_fn(k3, v3(ps[2 + pb][:, 0:128], 64), rs2.unsqueeze(2).broadcast_to([128, 2, 64]), ALU.mult), [f"ps{2 + pb}", f"rs2_{par}"], [f"kqn_{par}"])
            gk = g0[:, G0_SK2:G0_SK2 + 128]
            kb3 = v3(skb, 64)
            if lat:
                cos2 = ropeB_v[:, j, 0:32].unsqueeze(1).broadcast_to([128, 2, 32])
                sin2 = ropeB_v[:, j, 32:64].unsqueeze(1).broadcast_to([128, 2, 32])
                P.dve(tt_fn(kqn, kqn, gk, ALU.mult), [f"kqn_{par}", "g0"], [f"kqn_{par}"])
                ra2 = v3(rak, 32)
                rb2 = v3(rbk, 32)
                P.dve(tt_fn(ra2, k3[:, :, 0:32], cos2, ALU.mult), [f"kqn_{par}", "ropeB"], [f"rak_{par}"])
                P.dve(tt_fn(rb2, k3[:, :, 32:64], sin2, ALU.mult), [f"kqn_{par}", "ropeB"], [f"rbk_{par}"])
                P.dve(tt_fn(kb3[:, :, 0:32], ra2, rb2, ALU.subtract), [f"rak_{par}", f"rbk_{par}"], [f"skb0_{par}"])
                P.dve(tt_fn(ra2, k3[:, :, 0:32], sin2, ALU.mult), [f"kqn_{par}", "ropeB"], [f"rak_{par}"])
                P.dve(tt_fn(rb2, k3[:, :, 32:64], cos2, ALU.mult), [f"kqn_{par}", "ropeB"], [f"rbk_{par}"])
                P.dve(tt_fn(kb3[:, :, 32:64], ra2, rb2, ALU.add), [f"rak_{par}", f"rbk_{par}"], [f"skb1_{par}"])
            else:
                P.dve(tt_fn(skb, kqn, gk, ALU.mult), [f"kqn_{par}", "g0"], [f"skb0_{par}", f"skb1_{par}"])
            cfk.append(P.capture_end())
            P.capture_begin()
            P.pe(tr_group([(psb[4 + par][:, 512:640], skb, ident)]), [f"skb0_{par}", f"skb1_{par}", "ident"], [f"ps{4 + par}"])
            P.act(act_fn(skT[:, tok:tok + 128], psb[4 + par][:, 512:640], AF.Identity, bias=zero_c), [f"ps{4 + par}"], [f"KT.{tt}"])
            cbk.append(P.capture_end())
        P.replay_zip(cpe[0:2])
        for i_ in range(0, 18, 2):
            P.replay_zip(cfq[i_:i_ + 2] + cfk[i_:i_ + 2])
            if i_ + 2 < 18:
                P.replay_zip(cpe[i_ + 2:i_ + 4])
            P.replay_zip(cbq[i_:i_ + 2] + cbk[i_:i_ + 2])
        P.barrier()
        Xa.reset(tmark)
        E = [Xa.alloc((512,), BF16) for _ in range(4)]
        r_ = [Xa.alloc((512,), F32) for _ in range(2)]
        assert Xa.off <= xlimit, (Xa.off, xlimit)
        se_v = v3(se, 512)
        cnts = {"s": 0, "e": 0}
        items = []
        uidx = 0
        for g in range(2):
            for tt in range(18):
                if tt < 2:
                    kcs = [(0, None), (1, None)]
                else:
                    n_ = tt - 2
                    kcs = [(0, None), (1, None)]
                    if n_ > 0:
                        kcs.append((tt - 1, maskp))
                    kcs.append((tt, None))
                    if n_ < 15:
                        kcs.append((tt + 1, maskn))
                for ki, (kc, msk) in enumerate(kcs):
                    items.append((g, tt, kc, msk, ki == 0, ki == len(kcs) - 1, uidx))
                uidx += 1

        def s1(it):
            g, tt, kc, msk, first, last, u = it
            gp = slice(g * 64, (g + 1) * 64)
            sb = cnts["s"] % 3
            cnts["s"] += 1
            eb = cnts["e"] % 4
            cnts["e"] += 1
            P.pe(mm_group([(ps[sb][:, :], skT[:, kc * 128:(kc + 1) * 128], sqT_v[g][:, tt, :], True, True)]),
                 [f"KT.{kc}", f"QT.{tt}.{g}", f"sqz{g}"], [f"ps{sb}"])
            P.act(act_fn(E[eb], ps[sb][:, :], AF.Exp, bias=zero_c, scale=0.125), [f"ps{sb}"], [f"E{eb}"])
            if msk is not None:
                P.pool(tt_fn(E[eb], E[eb], msk, ALU.mult), [f"E{eb}", "maskp", "maskn"], [f"E{eb}"])
            return eb

        def s2(it, eb):
            g, tt, kc, msk, first, last, u = it
            gp = slice(g * 64, (g + 1) * 64)
            tok = tt * 128
            ft = ft_of_tt(tt)
            ob = 3 + (u % 4)
            rb_ = u % 2
            P.pe(mm_group([(ps[ob][:, :], sV_v[:, kc, g * 128:(g + 1) * 128], E[eb], first, last)]),
                 [f"V.{kc}", "Vones", f"E{eb}"], [f"ps{ob}"])
            if last:
                P.dve(tt_fn(r_[rb_][0:64, :], ps[ob][64:128, :], se_v[64:128, g, :], ALU.add), [f"ps{ob}", "se"], [f"r{rb_}"])
                P.dve(recip_fn(r_[rb_][0:64, :], r_[rb_][0:64, :]), [f"r{rb_}"], [f"r{rb_}"])
                P.dve(tt_fn(aT_v[gp, 4:8, tok:tok + 128], v3(ps[ob][0:64, :], 128), v3(r_[rb_][0:64, :], 128), ALU.mult),
                      [f"ps{ob}", f"r{rb_}"], [f"aT.{4 + pp}.{ft}.{g}.{tt}" for pp in range(4)])
        run_pipe(items, s1, s2, lag=2)
        P.barrier()

        def wo_load():
            return next_slot()

        def a_names(ft):
            t0, n = FT[ft]
            nm = [f"aT.{c}.{ft}.{hp}" for c in range(4) for hp in range(2)]
            nm += [f"aT.{4 + pp}.{ft}.{g}.{tt}" for pp in range(4) for g in range(2) for tt in range(t0 // 128, (t0 + n) // 128)]
            return nm
        wout_phase(l, s, [0, 1, 2, 3, 4], aT_v, a_names, 8, wo_load)
        P.barrier()

    stages = ["ffn00", "mix0", "ffn01", "ffn10", "mix1", "ffn11"]
    n_stage = len(stages) if stop_after is None else stages.index(stop_after) + 1
    for s in range(nseq):
        for si in range(n_stage):
            st = stages[si]
            if st.startswith("ffn"):
                WL.extend(wl_ffn(int(st[3]), int(st[4]), g) for g in range(6))
            elif st == "mix0":
                WL.extend([wl_mla(0), wl_mla(1), wl_swa(), wl_wout0()])
            else:
                WL.extend([wl_diff(0), wl_diff(1), wl_wout1()])
    for s in range(nseq):
        if s > 0:
            load_x(s)
        for si in range(n_stage):
            st = stages[si]
            if st == "ffn00":
                ffn(0, 0, s, [0, 1, 2, 3, 4])
            elif st == "mix0":
                mixer0(s)
            elif st == "ffn01":
                ffn(0, 1, s, [0, 1, 2, 3, 4])
            elif st == "ffn10":
                ffn(1, 0, s, [0, 1, 2, 3, 4])
            elif st == "mix1":
                mixer1(s)
            elif st == "ffn11":
                ffn(1, 1, s, [1, 2, 3, 4])
            if si < n_stage - 1:
                P.barrier()
        outr = outT[s].rearrange("(k p) t -> p k t", p=128)
        for ft in range(1, 5):
            t0, n = FT[ft]
            P.dma("sp", dma_fn([(outr[:, :, t0 - LC:t0 - LC + n], xT_v[:, :, t0:t0 + n])]),
                  [X(k, ft) for k in range(8)], [f"out{s}.{ft}"], key=f"out{s}.{ft}")
    P._add("sp", None, [f"out{s}.{ft}" for s in range(nseq) for ft in range(1, 5)], [])

    blk = ctx.enter_context(nc.Block())
    semstack = P.finalize_and_emit(blk)
    ctx.enter_context(semstack)
    ctx.close()
    nc._prog_stats = {e: len(P.per_eng[e]) for e in ENGINES}
    nc._n_sems = P.n_sems
    return nc


def _rope_tables():
    pos = np.arange(S)
    row = (pos // 64).astype(np.float32)
    col = (pos % 64).astype(np.float32)

    def tab(rot_dim):
        n_f = rot_dim // 4
        inv = (np.float32(10000.0) ** (-np.arange(n_f, dtype=np.float32) / np.float32(n_f))).astype(np.float32)
        ang = np.concatenate([row[:, None] * inv[None, :], col[:, None] * inv[None, :]], axis=-1).astype(np.float32)
        return np.cos(ang).astype(np.float32), np.sin(ang).astype(np.float32)
    ca, sa = tab(32)
    cb, sb = tab(64)
    A_ = np.concatenate([ca, sa], axis=-1).reshape(16, 128, 32).transpose(1, 0, 2).reshape(128, 512)
    B_ = np.concatenate([cb, sb], axis=-1).reshape(16, 128, 64).transpose(1, 0, 2).reshape(128, 1024)
    return np.ascontiguousarray(np.concatenate([A_, B_], axis=1), dtype=np.float32)


def _const_b():
    ident = np.eye(128, dtype=np.float32)
    ones = np.ones((128, 128), dtype=np.float32)
    a = np.arange(128)[:, None]
    b = np.arange(128)[None, :]
    mp = (b <= a).astype(np.float32)
    mn = (a <= b).astype(np.float32)
    return np.ascontiguousarray(np.concatenate([ident, ones, np.tile(mp, (1, 4)), np.tile(mn, (1, 4))], axis=1), dtype=np.float32)


def _rep(v, times=1):
    v = np.asarray(v, dtype=np.float32).reshape(-1)
    return np.tile(np.tile(v, times)[None, :], (128, 1))


_CACHE = {}


def make_in_maps(inp):
    f = lambda a: np.ascontiguousarray(np.asarray(a, dtype=np.float32))
    shared = {}
    for l in range(2):
        p = f"l{l}_"
        shared[f"ada_w{l}"] = f(inp[p + "ada_w"])
        shared[f"ada_bT{l}"] = f(np.asarray(inp[p + "ada_b"]).reshape(72, 128).T)
        shared[f"norm_gT{l}"] = f(np.asarray(inp[p + "norm_g"]).reshape(3, 8, 128).transpose(2, 0, 1).reshape(128, 24))
        shared[f"wg{l}"] = f(inp[p + "ffn_wg"])
        shared[f"wu{l}"] = f(inp[p + "ffn_wu"])
        shared[f"wd{l}"] = f(inp[p + "ffn_wd"])
    shared["w_in0"] = f(inp["l0_w_in"])
    shared["wqb"] = f(inp["l0_mla_wqb"])
    shared["wkvb"] = f(inp["l0_mla_wkvb"])
    shared["w_out0"] = f(inp["l0_w_out"])
    shared["w_out1"] = f(inp["l1_w_out"])
    shared["w_in1"] = f(inp["l1_w_in"])
    kg = np.asarray(inp["l0_mla_k_g"], dtype=np.float32)
    shared["gains0"] = f(np.concatenate([
        _rep(inp["l0_mla_qa_g"]), _rep(inp["l0_mla_kva_g"]), _rep(inp["l0_mla_q_g"], 4), _rep(kg[:64], 4), _rep(kg[64:]),
        _rep(inp["l0_swa_q_g"], 8), _rep(inp["l0_swa_k_g"], 2), _rep(inp["l0_swa_sink"])], axis=1))
    assert shared["gains0"].shape == (128, G0N)
    shared["gains1"] = f(np.concatenate([_rep(inp["l1_q_g"]), _rep(inp["l1_k_g"]),
                                         np.asarray(inp["l1_subln_g"], dtype=np.float32).reshape(128, 1)], axis=1))
    shared["lam"] = f(np.stack([np.asarray(inp[k], dtype=np.float32) for k in
                                ("l1_lambda_q1", "l1_lambda_k1", "l1_lambda_q2", "l1_lambda_k2")], axis=1))
    shared["consts_f"] = _rope_tables()
    shared["consts_b"] = _const_b()
    x = np.asarray(inp["x"], dtype=np.float32)
    c = np.asarray(inp["c"], dtype=np.float32)
    cx = np.asarray(inp["ctx"], dtype=np.float32)
    cc = np.asarray(inp["c_ctx"], dtype=np.float32)
    maps = []
    for core in range(NCORES):
        b0 = 2 * core
        m = dict(shared)
        m["xT"] = np.ascontiguousarray(x[b0:b0 + 2].transpose(0, 2, 1))
        m["ctxT"] = np.ascontiguousarray(cx[b0:b0 + 2].transpose(0, 2, 1))
        m["c3"] = np.ascontiguousarray(np.stack([c[b0], c[b0 + 1], cc], axis=1))
        maps.append(m)
    return maps


def kernel(**inputs):
    if "nc" not in _CACHE:
        _CACHE["nc"] = build_program()
    nc = _CACHE["nc"]
    in_maps = make_in_maps(inputs)
    res = run_bass_kernel_spmd(nc, in_maps, core_ids=list(range(NCORES)))
    out = np.empty((2 * NCORES, S, D), dtype=np.float32)
    for core in range(NCORES):
        o = np.asarray(res.results[core]["outT"])
        out[2 * core:2 * core + 2] = o.transpose(0, 2, 1)
    return out
```
